# Optimizing a Trainium2 kernel written in Bass

```python
import jax, jax.numpy as jnp
from jax import lax
import numpy as np

D_MODEL = 2048
BATCH = 4
SEQ = 2048
DEPTH = 2

CHUNK = 64
D_RWKV = D_MODEL // 2
D_HGRN = D_MODEL - D_RWKV
RWKV_HEAD = 64
RWKV_HEADS = D_RWKV // RWKV_HEAD
DECAY_RANK = max(32, int(round(1.8 * D_RWKV ** 0.5 / 32)) * 32)
A_RANK = max(32, int(round(1.8 * D_RWKV ** 0.5 / 32)) * 32)
VRES_RANK = max(32, int(round(1.3 * D_RWKV ** 0.5 / 32)) * 32)
HGRN_EXPAND = 128
HGRN_HEADS = D_HGRN // HGRN_EXPAND
HGRN_HEAD_V = D_HGRN // HGRN_HEADS
RWKV_COLS = 4 * D_RWKV + DECAY_RANK + A_RANK
HGRN_COLS = 4 * D_HGRN
IN_COLS = RWKV_COLS + HGRN_COLS
ALPHA = (2 * DEPTH) ** 0.25
BETA = (8 * DEPTH) ** -0.25
LN_EPS = 1e-5
GN_EPS = 64e-5
RMS_EPS = 1e-5
LB_FLOOR = 1e-30

kernel_name = "rwkv7_hgrn2_parallel_deepnorm"


def _heads(t, n):
    return t.reshape(t.shape[:-1] + (-1, n))


def _layer_norm(x, w, b):
    x = x.astype(jnp.float32)
    mu = jnp.mean(x, -1, keepdims=True)
    var = jnp.mean(jnp.square(x - mu), -1, keepdims=True)
    return (x - mu) * lax.rsqrt(var + LN_EPS) * w + b


def _token_shift(y, mu):
    y_prev = jnp.pad(y, ((0, 0), (1, 0), (0, 0)))[:, :-1]
    return y + mu * (y_prev - y)


def _rwkv7_scan(r, w, k, v, a_vec, b_vec):
    Bsz, T, H, N = r.shape

    def step(S, inp):
        r_t, w_t, k_t, v_t, a_t, b_t = inp
        sa = jnp.einsum('bhij,bhj->bhi', S, a_t)
        S = S * w_t[:, :, None, :] + sa[..., None] * b_t[:, :, None, :] + v_t[..., None] * k_t[:, :, None, :]
        return S, jnp.einsum('bhij,bhj->bhi', S, r_t)

    S0 = jnp.zeros((Bsz, H, N, N), jnp.float32)
    xs = tuple(jnp.moveaxis(t, 1, 0) for t in (r, w, k, v, a_vec, b_vec))
    _, o = lax.scan(step, S0, xs)
    return jnp.moveaxis(o, 0, 1)


def _rwkv7_branch(rw, v_first, w0, w_up, a0, a_up, k_k, k_a, r_k, gn_w, gn_b, v_mix):
    r, k, v, z, wd, ad = jnp.split(
        rw, [D_RWKV, 2 * D_RWKV, 3 * D_RWKV, 4 * D_RWKV, 4 * D_RWKV + DECAY_RANK], axis=-1)
    w_raw = w0 + jnp.tanh(wd) @ w_up
    decay = jnp.exp(-jnp.exp(-jax.nn.softplus(-w_raw) - 0.5))
    a = jax.nn.sigmoid(a0 + ad @ a_up)
    if v_mix is None:
        v_first = v
    else:
        v0, v_down, v_up = v_mix
        v = v + (v_first - v) * jax.nn.sigmoid(v0 + (v @ v_down) @ v_up)
    kk = _heads(k * k_k, RWKV_HEAD)
    kk = kk / jnp.maximum(jnp.sqrt(jnp.sum(kk * kk, -1, keepdims=True)), 1e-12)
    k = k * (1.0 + (a - 1.0) * k_a)
    rh, kh, vh = _heads(r, RWKV_HEAD), _heads(k, RWKV_HEAD), _heads(v, RWKV_HEAD)
    ah = _heads(a, RWKV_HEAD)
    o = _rwkv7_scan(rh, _heads(decay, RWKV_HEAD), kh, vh, -kk, kk * ah)
    mu = jnp.mean(o, -1, keepdims=True)
    var = jnp.mean(jnp.square(o - mu), -1, keepdims=True)
    o = (o - mu) * lax.rsqrt(var + GN_EPS) * _heads(gn_w, RWKV_HEAD) + _heads(gn_b, RWKV_HEAD)
    o = o + jnp.sum(rh * kh * _heads(r_k, RWKV_HEAD), -1, keepdims=True) * vh
    return o.reshape(rw.shape[:-1] + (D_RWKV,)) * jax.nn.silu(z), v_first


def _hgrn2_chunkwise(q, log_f, k, i):
    Bsz, T, H, DK = q.shape
    DV = i.shape[-1]
    NC = T // CHUNK

    def to_chunks(t):
        return jnp.moveaxis(t.reshape(Bsz, NC, CHUNK, H, t.shape[-1]), 1, 0)

    causal = jnp.tril(jnp.ones((CHUNK, CHUNK), bool))[None, :, :, None, None]

    def step(S, inp):
        q_c, lf_c, k_c, i_c = inp
        b = jnp.cumsum(lf_c, axis=1)
        diff = b[:, :, None] - b[:, None, :]
        decay = jnp.where(causal, jnp.exp(jnp.where(causal, diff, 0.0)), 0.0)
        att = jnp.einsum('btshd,bshd->btsh', q_c[:, :, None] * decay, k_c)
        o_intra = jnp.einsum('btsh,bshv->bthv', att, i_c)
        o_inter = jnp.einsum('bthd,bhdv->bthv', q_c * jnp.exp(b), S)
        b_last = b[:, -1]
        k_dec = k_c * jnp.exp(b_last[:, None] - b)
        S = S * jnp.exp(b_last)[..., None] + jnp.einsum('bshd,bshv->bhdv', k_dec, i_c)
        return S, o_intra + o_inter

    S0 = jnp.zeros((Bsz, H, DK, DV), jnp.float32)
    _, o = lax.scan(step, S0, (to_chunks(q), to_chunks(log_f), to_chunks(k), to_chunks(i)))
    return jnp.moveaxis(o, 0, 1).reshape(Bsz, T, H, DV)


def _hgrn2_branch(hg, lb, g_norm_w):
    q, f_raw, i_in, z = jnp.split(hg, 4, axis=-1)
    q = jax.nn.silu(q)
    log_lb = jnp.log(jnp.maximum(lb, LB_FLOOR))
    log_f = jnp.logaddexp(log_lb, jnp.log1p(-lb) + jax.nn.log_sigmoid(f_raw))
    k = (1.0 - lb) * jax.nn.sigmoid(-f_raw)
    o = _hgrn2_chunkwise(_heads(q, HGRN_EXPAND), _heads(log_f, HGRN_EXPAND),
                         _heads(k, HGRN_EXPAND), _heads(i_in, HGRN_HEAD_V))
    o = o * lax.rsqrt(jnp.mean(o * o, -1, keepdims=True) + RMS_EPS)
    return o.reshape(hg.shape[:-1] + (D_HGRN,)) * g_norm_w * jax.nn.silu(z)


def setup_inputs(seed: int = 0) -> dict:
    key = jax.random.key(seed)
    ks = jax.random.split(key, 24)
    L, L1 = DEPTH, DEPTH - 1

    def nrm(k, shape, s):
        return s * jax.random.normal(k, shape, jnp.float32)

    x = nrm(ks[0], (BATCH, SEQ, D_MODEL), 1.0)
    col_scale = jnp.concatenate([
        jnp.ones((2 * D_RWKV,), jnp.float32), jnp.full((D_RWKV,), BETA, jnp.float32),
        jnp.ones((D_RWKV + DECAY_RANK + A_RANK,), jnp.float32),
        jnp.ones((2 * D_HGRN,), jnp.float32), jnp.full((D_HGRN,), BETA, jnp.float32),
        jnp.ones((D_HGRN,), jnp.float32)])
    w_in = nrm(ks[1], (L, D_MODEL, IN_COLS), D_MODEL ** -0.5) * col_scale
    shift_mu = jax.random.uniform(ks[2], (L, RWKV_COLS), jnp.float32)
    ramp = (jnp.arange(D_RWKV, dtype=jnp.float32) / (D_RWKV - 1)) ** 0.85
    w_decay0 = -6.0 + 5.0 * ramp + nrm(ks[3], (L, D_RWKV), 0.1)
    w_decay_up = nrm(ks[4], (L, DECAY_RANK, D_RWKV), 0.3 * DECAY_RANK ** -0.5)
    a0 = nrm(ks[5], (L, D_RWKV), 0.1)
    a_up = nrm(ks[6], (L, A_RANK, D_RWKV), 0.3 * A_RANK ** -0.5)
    k_k = 0.85 + nrm(ks[7], (L, D_RWKV), 0.05)
    k_a = 1.0 + nrm(ks[8], (L, D_RWKV), 0.05)
    r_k = nrm(ks[9], (L, D_RWKV), 0.1)
    ln_x_w = 1.0 + nrm(ks[10], (L, D_RWKV), 0.05)
    ln_x_b = nrm(ks[11], (L, D_RWKV), 0.02)
    v_mix0 = 1.0 + nrm(ks[12], (L1, D_RWKV), 0.1)
    v_mix_down = nrm(ks[13], (L1, D_RWKV, VRES_RANK), D_RWKV ** -0.5)
    v_mix_up = nrm(ks[14], (L1, VRES_RANK, D_RWKV), 0.3 * VRES_RANK ** -0.5)
    lb_logits = nrm(ks[15], (L, D_HGRN), 0.5)
    g_norm_w = 1.0 + nrm(ks[16], (L, D_HGRN), 0.05)
    w_out = nrm(ks[17], (L, D_MODEL, D_MODEL), BETA * D_MODEL ** -0.5)
    ln_w = 1.0 + nrm(ks[18], (L, D_MODEL), 0.05)
    ln_b = nrm(ks[19], (L, D_MODEL), 0.02)
    return {"x": x, "w_in": w_in, "shift_mu": shift_mu, "w_decay0": w_decay0,
            "w_decay_up": w_decay_up, "a0": a0, "a_up": a_up, "k_k": k_k, "k_a": k_a,
            "r_k": r_k, "ln_x_w": ln_x_w, "ln_x_b": ln_x_b, "v_mix0": v_mix0,
            "v_mix_down": v_mix_down, "v_mix_up": v_mix_up, "lb_logits": lb_logits,
            "g_norm_w": g_norm_w, "w_out": w_out, "ln_w": ln_w, "ln_b": ln_b}


def reference(x, w_in, shift_mu, w_decay0, w_decay_up, a0, a_up, k_k, k_a, r_k, ln_x_w, ln_x_b,
              v_mix0, v_mix_down, v_mix_up, lb_logits, g_norm_w, w_out, ln_w, ln_b):
    out_dtype = x.dtype
    lb_sm = jax.nn.softmax(lb_logits.astype(jnp.float32), axis=0)
    lower_bounds = jnp.cumsum(lb_sm, axis=0) - lb_sm[0]
    h = x.astype(jnp.float32)
    v_first = None
    for l in range(DEPTH):
        proj = jnp.einsum('btd,dc->btc', h, w_in[l].astype(jnp.float32))
        rw = _token_shift(proj[..., :RWKV_COLS], shift_mu[l])
        hg = proj[..., RWKV_COLS:]
        v_mix = None if l == 0 else (v_mix0[l - 1], v_mix_down[l - 1], v_mix_up[l - 1])
        o_rwkv, v_first = _rwkv7_branch(rw, v_first, w_decay0[l], w_decay_up[l], a0[l], a_up[l],
                                        k_k[l], k_a[l], r_k[l], ln_x_w[l], ln_x_b[l], v_mix)
        o_hgrn = _hgrn2_branch(hg, lower_bounds[l], g_norm_w[l])
        y = jnp.einsum('btc,cd->btd', jnp.concatenate([o_rwkv, o_hgrn], axis=-1), w_out[l])
        h = _layer_norm(ALPHA * h + y, ln_w[l], ln_b[l])
    return h.astype(out_dtype)
```

```python
import contextlib
import numpy as np
import ml_dtypes
import concourse.bass as bass
import concourse.mybir as mybir
from concourse.bass_utils import run_bass_kernel_spmd

F32 = mybir.dt.float32
BF16 = mybir.dt.bfloat16
ALU = mybir.AluOpType
AF = mybir.ActivationFunctionType

D = 2048
KC = D // 128
SEQ = 2048
BATCH = 4
DEPTH = 2
TB = 256
CH = 64
NCH = TB // CH
ALPHA = (2 * DEPTH) ** 0.25
LN_EPS = 1e-5
GN_EPS = 64e-5
RMS_EPS = 1e-5
LOGW_SCALE = -float(np.exp(-0.5))


class Buf:
    __slots__ = ("name", "writers", "readers")

    def __init__(self, name=""):
        self.name = name
        self.writers = {}
        self.readers = {}


class Sched:
    ENGS = ("pe", "act", "dve", "pool", "sp")

    def __init__(self, nc, n_dma_sems=32):
        self.nc = nc
        self.cnt = {e: 0 for e in self.ENGS}
        self.seen = {e: {} for e in self.ENGS}
        self.prog = {e: [] for e in self.ENGS}
        self.n_dma_sems = n_dma_sems
        self.dma_val = [0] * n_dma_sems
        self.dma_rr = 0
        self.out_dma = []
        self.pending = {e: {} for e in self.ENGS}

    def _need(self, eng, deps, key, val):
        if val <= self.seen[eng].get(key, 0):
            return
        if deps.get(key, 0) < val:
            deps[key] = val

    def barrier(self):
        snap = {e: self.cnt[e] for e in self.ENGS if self.cnt[e] > 0}
        for s in range(self.n_dma_sems):
            if self.dma_val[s] > 0:
                snap[("dma", s)] = self.dma_val[s]
        for e in self.ENGS:
            for k, v in snap.items():
                if k == e:
                    continue
                if self.pending[e].get(k, 0) < v:
                    self.pending[e][k] = v

    def _take_pending(self, eng, deps):
        if self.pending[eng]:
            for k, v in self.pending[eng].items():
                self._need(eng, deps, k, v)
            self.pending[eng] = {}

    def op(self, eng, fn, reads=(), writes=()):
        deps = {}
        self._take_pending(eng, deps)
        for b in reads:
            for k, v in b.writers.items():
                if k == eng and eng == "pe":
                    continue
                self._need(eng, deps, k, v)
        for b in writes:
            for k, v in b.writers.items():
                if k == eng:
                    continue
                self._need(eng, deps, k, v)
            for k, v in b.readers.items():
                if k == eng:
                    continue
                self._need(eng, deps, k, v)
        for k, v in deps.items():
            self.seen[eng][k] = v
        self.cnt[eng] += 1
        idx = self.cnt[eng]
        self.prog[eng].append((list(deps.items()), fn, ("eng", eng)))
        for b in reads:
            b.readers[eng] = idx
        for b in writes:
            b.writers = {eng: idx}
            b.readers = {}
        return idx

    def dma(self, q, fn, reads=(), writes=(), is_output=False):
        deps = {}
        self._take_pending(q, deps)
        for b in reads:
            for k, v in b.writers.items():
                self._need(q, deps, k, v)
        for b in writes:
            for k, v in b.writers.items():
                self._need(q, deps, k, v)
            for k, v in b.readers.items():
                self._need(q, deps, k, v)
        s = self.dma_rr
        self.dma_rr = (self.dma_rr + 1) % self.n_dma_sems
        key = ("dma", s)
        if self.dma_val[s] > 0:
            self._need(q, deps, key, self.dma_val[s])
        for k, v in deps.items():
            self.seen[q][k] = v
        self.dma_val[s] += 16
        val = self.dma_val[s]
        self.prog[q].append((list(deps.items()), fn, ("dma", s)))
        for b in reads:
            b.readers[key] = val
        for b in writes:
            b.writers = {key: val}
            b.readers = {}
        if is_output:
            self.out_dma.append((key, val))
        return key, val

    def emit(self, st, final_eng="sp"):
        nc = self.nc
        esem = {e: st.enter_context(nc.semaphore("s_" + e)) for e in self.ENGS}
        dsem = [st.enter_context(nc.semaphore("d%d" % i)) for i in range(self.n_dma_sems)]

        def semof(key):
            if isinstance(key, tuple):
                return dsem[key[1]]
            return esem[key]

        fin = {}
        for k, v in self.out_dma:
            fin[k] = max(fin.get(k, 0), v)
        block = st.enter_context(nc.Block())
        hmap = {"pe": nc.tensor, "act": nc.scalar, "dve": nc.vector, "pool": nc.gpsimd, "sp": nc.sync}

        def mk(e):
            def body(h):
                for waits, fn, inc in self.prog[e]:
                    for k, v in waits:
                        h.wait_ge(semof(k), v)
                    ins = fn(h)
                    if inc[0] == "eng":
                        ins.then_inc(esem[inc[1]], 1)
                    else:
                        ins.then_inc(dsem[inc[1]], 16)
                if e == final_eng:
                    for k, v in fin.items():
                        h.wait_ge(semof(k), v)
            return body
        block.tensor(mk("pe"))
        block.scalar(mk("act"))
        block.vector(mk("dve"))
        block.gpsimd(mk("pool"))
        block.sync(mk("sp"))


class Arena:
    def __init__(self, tensor, n):
        self.t = tensor
        self.n = n
        self.off = 0

    def reset(self):
        self.off = 0

    def alloc(self, shape):
        free = int(np.prod(shape[1:]))
        free_al = (free + 1) // 2 * 2
        assert self.off + free_al <= self.n, ("arena overflow", self.off, free_al, self.n)
        ap = self.t[0:shape[0], self.off:self.off + free]
        self.off += free_al
        if len(shape) == 2:
            return ap
        names = " ".join("d%d" % i for i in range(len(shape) - 1))
        kw = {"d%d" % i: shape[i + 1] for i in range(len(shape) - 1)}
        return ap.rearrange("p (%s) -> p %s" % (names, names), **kw)


class Ctx:
    pass


def bc(ap_col, n):
    return ap_col.unsqueeze(len(ap_col.shape)).to_broadcast(list(ap_col.shape) + [n])


def mk_ctx(nc, st):
    C = Ctx()
    C.nc = nc
    C.S = Sched(nc)
    NF = 50 * 1024
    C.arena_t = st.enter_context(nc.sbuf_tensor("arena", [128, NF], F32))
    C.arena = Arena(C.arena_t, NF)
    C.ps = [st.enter_context(nc.psum_tensor("ps%d" % i, [128, 512], F32)) for i in range(8)]
    C.bps = [Buf("ps%d" % i) for i in range(8)]
    C.uid = [0]
    return C


def _ops(C):
    S = C.S

    def mm(out, lhsT, rhs, start, stop, reads, writes):
        S.op("pe", lambda h: h.matmul(out, lhsT=lhsT, rhs=rhs, start=start, stop=stop), reads, writes)

    def tr(out, in_, ident, reads, writes):
        S.op("pe", lambda h: h.transpose(out, in_, ident), reads, writes)

    def act(out, in_, func, reads, writes, bias=None, scale=1.0):
        if bias is None:
            S.op("act", lambda h: h.activation(out=out, in_=in_, func=func, scale=scale), reads, writes)
        else:
            S.op("act", lambda h: h.activation(out=out, in_=in_, func=func, bias=bias, scale=scale), reads, writes)

    def tt(eng, out, a, b, op, reads, writes):
        S.op(eng, lambda h: h.tensor_tensor(out=out, in0=a, in1=b, op=op), reads, writes)

    def ts(eng, out, a, s1, op0, reads, writes, s2=None, op1=None):
        if op1 is None:
            S.op(eng, lambda h: h.tensor_scalar(out, a, s1, None, op0), reads, writes)
        else:
            S.op(eng, lambda h: h.tensor_scalar(out, a, s1, s2, op0, op1), reads, writes)

    def stt(eng, out, in0, scalar, in1, op0, op1, reads, writes):
        S.op(eng, lambda h: h.scalar_tensor_tensor(out=out, in0=in0, scalar=scalar, in1=in1, op0=op0, op1=op1), reads, writes)

    def cp(eng, out, in_, reads, writes):
        if eng == "act":
            S.op("act", lambda h: h.copy(out, in_), reads, writes)
        else:
            S.op(eng, lambda h: h.tensor_copy(out, in_), reads, writes)

    def ms(eng, out, val, writes):
        S.op(eng, lambda h: h.memset(out, val), (), writes)

    return mm, tr, act, tt, ts, stt, cp, ms


def emit_consts(C):
    mm, tr, act, tt, ts, stt, cp, ms = _ops(C)
    S = C.S
    A = C.arena
    K = Ctx()
    C.K = K
    K.b = Buf("consts")
    K.ones = A.alloc([128, 128])
    K.ident = A.alloc([128, 128])
    K.obd1 = A.alloc([128, 128])
    K.obd64 = A.alloc([128, 128])
    K.o128 = A.alloc([128, 128])
    K.mask2 = A.alloc([64, 2, 64])
    K.rst = A.alloc([128, TB])
    K.eps_kk = A.alloc([128, 2])
    K.eps_gn = A.alloc([128, 2])
    K.eps_rms = A.alloc([128, 2])
    K.eps_ln = A.alloc([128, 2])
    b = [K.b]
    ms("pool", K.ones, 1.0, b)
    S.op("pool", lambda h: h.affine_select(out=K.ident, in_=K.ones, pattern=[[-1, 128]], compare_op=ALU.is_equal,
                                           fill=0.0, base=0, channel_multiplier=1), b, b)
    ms("pool", K.obd1, 0.0, b)
    ms("pool", K.obd1[0:64, 0:64], 1.0, b)
    ms("pool", K.obd1[64:128, 64:128], 1.0, b)
    ts("pool", K.obd64, K.obd1, 1.0 / 64, ALU.mult, b, b)
    ms("pool", K.o128, 1.0 / 128, b)
    S.op("pool", lambda h: h.affine_select(out=K.mask2[:, 0, :], in_=K.ones[0:64, 0:64], pattern=[[1, 64]],
                                           compare_op=ALU.is_gt, fill=0.0, base=0, channel_multiplier=-1), b, b)
    S.op("pool", lambda h: h.affine_select(out=K.mask2[:, 1, :], in_=K.ones[0:64, 0:64], pattern=[[1, 64]],
                                           compare_op=ALU.is_ge, fill=0.0, base=0, channel_multiplier=-1), b, b)
    ms("pool", K.rst, 1.0, b)
    ms("pool", K.eps_kk, 1e-24, b)
    ms("pool", K.eps_gn, GN_EPS, b)
    ms("pool", K.eps_rms, RMS_EPS, b)
    ms("pool", K.eps_ln, LN_EPS, b)
    ms("pool", K.rst.rearrange("p (c k) -> p c k", k=CH)[:, :, 0:1], 0.0, b)
    C.persist_off = A.off


def emit_A(C, hin, W, oT_out, vf, has_vmix, li, T, dbg=None):
    mm, tr, act, tt, ts, stt, cp, ms = _ops(C)
    S = C.S
    K = C.K
    A = C.arena
    A.off = C.persist_off
    S.barrier()
    PS, BPS = C.ps, C.bps
    NS = 21 if has_vmix else 17
    NFM = NS + 12
    R0, K0, V0, Z0, WA = 0, 4, 8, 12, 16
    VO0 = 17
    Q0, F0, ZH0 = NS, NS + 4, NS + 8
    cMU = 0
    cW0, cA0, cKK, cKA, cRK, cGNW, cGNB, cVM0, cLB0, cLB1, cGW = [NS + 4 * i for i in range(11)]
    NPP = NS + 44
    kb = K.b

    stg = [A.alloc([128, 2048]) for _ in range(2)]
    bstg = [Buf("stg%d" % i) for i in range(2)]
    HTB = A.alloc([128, KC * TB // 2]).bitcast(BF16).rearrange("p (k t) -> p k t", k=KC)
    bHTB = Buf("HTB")
    NWB = 2
    WB = [A.alloc([128, KC * 128 // 2]).bitcast(BF16).rearrange("p (k f) -> p k f", k=KC) for _ in range(NWB)]
    bWB = [Buf("wb%d" % i) for i in range(NWB)]
    WIB = A.alloc([128, KC * 512 // 2]).bitcast(BF16).rearrange("p (k f) -> p k f", k=KC)
    bWIB = Buf("WIB")
    PR = A.alloc([128, NFM, TB + 2])
    bPRs, bPRh = Buf("PRs"), Buf("PRh")
    LAST = A.alloc([128, NS])
    bLAST = Buf("LAST")
    ITOK = A.alloc([64, NCH, 512])
    bITOK = Buf("ITOK")
    PP = A.alloc([128, NPP])
    PD = A.alloc([128, 32])
    bPP = Buf("PP")
    LRW = A.alloc([128, 512])
    LRA = A.alloc([128, 512])
    if has_vmix:
        VDN = A.alloc([128, 8, 32])
        VUP = A.alloc([32, 512])
        VD = A.alloc([32, TB])
        bVD = Buf("VD")
    Ssl = [A.alloc([128, 4, TB]) for _ in range(7)]
    bS = [Buf("S%d" % i) for i in range(7)]
    S1, S2, S3, S4, S5, S6, S7 = Ssl
    b1, b2, b3, b4, b5, b6, b7 = bS
    GC = A.alloc([128, 4, NCH])
    bGC = Buf("GC")
    Hsl = [A.alloc([128, 4, TB]) for _ in range(3)]
    bH = [Buf("H%d" % i) for i in range(3)]
    H1, H2, H3 = Hsl
    bh1, bh2, bh3 = bH
    BM = A.alloc([128, 4, NCH]); BL = A.alloc([128, 4, NCH])
    EM = A.alloc([128, 4, NCH]); EL = A.alloc([128, 4, NCH]); ELM = A.alloc([128, 4, NCH])
    bBM = Buf("BM")
    SBm = [A.alloc([64, 4, 2, 64]) for _ in range(2)]
    SKm = [A.alloc([64, 4, 2, 64]) for _ in range(2)]
    bSBm = [Buf("SBm%d" % i) for i in range(2)]
    bSKm = [Buf("SKm%d" % i) for i in range(2)]
    NA = [A.alloc([64, 8, 64]) for _ in range(2)]
    NAT = [A.alloc([64, 8, 64]) for _ in range(2)]
    bNA = [Buf("NA%d" % i) for i in range(2)]
    bNAT = [Buf("NAT%d" % i) for i in range(2)]
    PT = A.alloc([64, 4, 2, 64]); bPT = Buf("PT")
    Xs = A.alloc([64, 8, 64]); bXs = Buf("Xs")
    Us = A.alloc([64, 4, 2, 64]); bUs = Buf("Us")
    VT = A.alloc([64, 4, 128]); bVT = Buf("VT")
    BST = A.alloc([64, 4, 128]); bBST = Buf("BST")
    KST = A.alloc([64, 4, 128]); bKST = Buf("KST")
    HB = [A.alloc([128, 4, 128]) for _ in range(2)]
    bHB = [Buf("HB%d" % i) for i in range(2)]
    HD = A.alloc([128, 4, 64]); bHD = Buf("HD")
    OTR = A.alloc([128, 4, TB]); bOTR = Buf("OTR")
    OTH = A.alloc([128, 4, TB]); bOTH = Buf("OTH")
    ATT = A.alloc([64, 4, 64]); bATT = Buf("ATT")
    KTK = A.alloc([64, 4, 128]); bKTK = Buf("KTK")
    SH = A.alloc([128, 4, 128]); bSH = Buf("SH")
    SM = A.alloc([128, 4, 128]); bSM = Buf("SM")
    SD = A.alloc([128, 4, 128]); bSD = Buf("SD")
    OTB = A.alloc([128, 8 * TB // 2]).bitcast(BF16).rearrange("p (c t) -> p c t", c=8)
    bOTB = Buf("OTB")
    bVF = Buf("vf_dram")
    bWS = [Buf("wscr%d" % i) for i in range(NFM)]

    S.dma("sp", lambda h: h.dma_start(out=PP, in_=W["pp"]), (), [bPP])
    S.dma("sp", lambda h: h.dma_start(out=LRW, in_=W["lrw"]), (), [bPP])
    S.dma("sp", lambda h: h.dma_start(out=LRA, in_=W["lra"]), (), [bPP])
    if has_vmix:
        S.dma("sp", lambda h: h.dma_start(out=VDN, in_=W["vdn"].rearrange("p (c r) -> p c r", c=8)), (), [bPP])
        S.dma("sp", lambda h: h.dma_start(out=VUP, in_=W["vup"]), (), [bPP])
    pq = [bPP]
    ts("pool", PD[:, 0:4], PP[:, cW0:cW0 + 4], 0.5, ALU.mult, pq, pq)
    ts("pool", PD[:, 4:8], PP[:, cA0:cA0 + 4], 0.5, ALU.mult, pq, pq)
    ts("pool", PD[:, 8:12], PP[:, cKA:cKA + 4], -1.0, ALU.mult, pq, pq, 1.0, ALU.add)
    ts("pool", PD[:, 12:16], PP[:, cVM0:cVM0 + 4], 0.5, ALU.mult, pq, pq)
    tt("dve", PD[:, 28:32], PP[:, cLB0:cLB0 + 4], PP[:, cLB1:cLB1 + 4], ALU.max, pq, pq)
    tt("pool", PD[:, 16:20], PP[:, cLB0:cLB0 + 4], PD[:, 28:32], ALU.subtract, pq, pq)
    tt("pool", PD[:, 20:24], PP[:, cLB1:cLB1 + 4], PD[:, 28:32], ALU.subtract, pq, pq)
    act(PD[:, 16:20], PD[:, 16:20], AF.Exp, pq, pq)
    act(PD[:, 20:24], PD[:, 20:24], AF.Exp, pq, pq)
    tt("dve", PD[:, 28:32], PD[:, 16:20], PD[:, 20:24], ALU.add, pq, pq)
    S.op("dve", lambda h: h.reciprocal(PD[:, 28:32], PD[:, 28:32]), pq, pq)
    tt("dve", PD[:, 16:20], PD[:, 16:20], PD[:, 28:32], ALU.mult, pq, pq)
    tt("dve", PD[:, 20:24], PD[:, 20:24], PD[:, 28:32], ALU.mult, pq, pq)
    if li == 0:
        tt("dve", PD[:, 20:24], PD[:, 16:20], PD[:, 16:20], ALU.subtract, pq, pq)
        cp("dve", PD[:, 16:20], PD[:, 20:24], pq, pq)
    else:
        tt("dve", PD[:, 20:24], PD[:, 16:20], PD[:, 20:24], ALU.add, pq, pq)
        tt("dve", PD[:, 16:20], PD[:, 20:24], PD[:, 16:20], ALU.subtract, pq, pq)
    ts("dve", PD[:, 20:24], PD[:, 16:20], -1.0, ALU.mult, pq, pq, 1.0, ALU.add)
    ts("dve", PD[:, 24:28], PD[:, 16:20], 1e-30, ALU.max, pq, pq)
    for q in range(4):
        sl = q % 2
        S.dma("sp", lambda h, sl=sl, q=q: h.dma_start(out=stg[sl], in_=W["wi"][:, q * 2048:(q + 1) * 2048]), (), [bstg[sl]])
        cp("pool", WIB[:, 4 * q:4 * q + 4, :], stg[sl].rearrange("p (k f) -> p k f", k=4), [bstg[sl]], [bWIB])
    ms("pool", LAST, 0.0, [bLAST])
    ms("pool", HB[0], 0.0, [bHB[0]])
    ms("pool", HB[1], 0.0, [bHB[1]])
    ms("pool", SH, 0.0, [bSH])
    hcur = 0
    psrr = [0]

    def nxt_bank(lo, hi):
        b = lo + psrr[0] % (hi - lo)
        psrr[0] += 1
        return b

    mub = lambda a, n: bc(PP[:, cMU + a:cMU + a + n], TB)
    pcol = lambda c0: bc(PP[:, c0:c0 + 4], TB)
    dcol = lambda c0: bc(PD[:, c0:c0 + 4], TB)

    for tb in range(T // TB):
        t0 = tb * TB
        for j in range(TB // 128):
            sl = j % 2
            S.dma("sp", lambda h, sl=sl, j=j, t0=t0: h.dma_start(out=stg[sl], in_=hin[t0 + j * 128:t0 + (j + 1) * 128, :]), (), [bstg[sl]])
            for q in range(4):
                bk = nxt_bank(0, 4)
                for i in range(4):
                    kc = 4 * q + i
                    tr(PS[bk][:, i * 128:(i + 1) * 128], stg[sl][:, kc * 128:(kc + 1) * 128], K.ident,
                       [bstg[sl], kb], [BPS[bk]])
                cp("act" if q % 2 == 0 else "dve", HTB[:, 4 * q:4 * q + 4, j * 128:(j + 1) * 128],
                   PS[bk].rearrange("p (k t) -> p k t", k=4), [BPS[bk]], [bHTB])
        for fc in range(NFM):
            wsl = fc % NWB
            if tb == 0:
                sl = fc % 2
                S.dma("sp", lambda h, sl=sl, fc=fc: h.dma_start(out=stg[sl], in_=W["wfm"][fc]), (), [bstg[sl]])
                cp("pool", WB[wsl], stg[sl].rearrange("p (k f) -> p k f", k=KC), [bstg[sl]], [bWB[wsl]])
                S.dma("pool", lambda h, wsl=wsl, fc=fc: h.dma_start(out=W["wscr"][fc], in_=WB[wsl].rearrange("p k f -> p (k f)")),
                      [bWB[wsl]], [bWS[fc]])
            else:
                S.dma("sp", lambda h, wsl=wsl, fc=fc: h.dma_start(out=WB[wsl].rearrange("p k f -> p (k f)"), in_=W["wscr"][fc]),
                      [bWS[fc]], [bWB[wsl]])
            bk = nxt_bank(0, 4)
            for kc in range(KC):
                mm(PS[bk][:, 0:TB], WB[wsl][:, kc, :], HTB[:, kc, :], kc == 0, kc == KC - 1, [bWB[wsl], bHTB], [BPS[bk]])
            cp("act", PR[:, fc, 1:1 + TB], PS[bk][:, 0:TB], [BPS[bk]], [bPRs if fc < NS else bPRh])
        if dbg is not None and tb == 0:
            S.dma('sp', lambda h: h.dma_start(out=dbg['pr'], in_=PR), [bPRs, bPRh], [], is_output=True)
            S.dma('sp', lambda h: h.dma_start(out=dbg['htb'], in_=HTB), [bHTB], [], is_output=True)
        for c in range(NCH):
            bk = nxt_bank(0, 4)
            for kc in range(KC):
                mm(PS[bk][0:64, :], HTB[:, kc, c * CH:(c + 1) * CH], WIB[:, kc, :], kc == 0, kc == KC - 1, [bHTB, bWIB], [BPS[bk]])
            cp("act" if c % 2 == 0 else "dve", ITOK[:, c, :], PS[bk][0:64, :], [BPS[bk]], [bITOK])

        Rr, Kk, Vv, Zz = (PR[:, a:a + 4, 1:1 + TB] for a in (R0, K0, V0, Z0))
        cp("pool", PR[:, 0:NS, 0], LAST, [bLAST], [bPRs])
        cp("pool", LAST, PR[:, 0:NS, TB], [bPRs], [bLAST])
        groups = [(0, 4), (4, 4), (8, 4), (12, 4), (16, 1)] + ([(17, 4)] if has_vmix else [])
        for (a, n) in groups:
            cur = PR[:, a:a + n, 1:1 + TB]
            prv = PR[:, a:a + n, 0:TB]
            tt("dve", S4[:, 0:n, :], prv, cur, ALU.subtract, [bPRs], [b4])
            tt("pool", S4[:, 0:n, :], S4[:, 0:n, :], mub(a, n), ALU.mult, [b4, bPP], [b4])
            tt("dve", cur, cur, S4[:, 0:n, :], ALU.add, [bPRs, b4], [bPRs])
        act(PR[0:64, WA, 1:1 + TB], PR[0:64, WA, 1:1 + TB], AF.Tanh, [bPRs], [bPRs])
        for p in range(4):
            bk = nxt_bank(4, 8)
            mm(PS[bk][:, 0:TB], LRW[:, p * 128:(p + 1) * 128], PR[:, WA, 1:1 + TB], True, True, [bPP, bPRs], [BPS[bk]])
            act(S1[:, p, :], PS[bk][:, 0:TB], AF.Tanh, [BPS[bk], bPP], [b1], bias=PD[:, p:p + 1], scale=0.5)
            bk = nxt_bank(4, 8)
            mm(PS[bk][:, 0:TB], LRA[:, p * 128:(p + 1) * 128], PR[:, WA, 1:1 + TB], True, True, [bPP, bPRs], [BPS[bk]])
            act(S2[:, p, :], PS[bk][:, 0:TB], AF.Tanh, [BPS[bk], bPP], [b2], bias=PD[:, 4 + p:5 + p], scale=0.5)
        ts("pool", S1, S1, 0.5 * LOGW_SCALE, ALU.mult, [b1], [b1], 0.5 * LOGW_SCALE, ALU.add)
        ts("pool", S2, S2, 0.5, ALU.mult, [b2], [b2], 0.5, ALU.add)
        if has_vmix:
            bk = nxt_bank(4, 8)
            for c8 in range(8):
                fcv = V0 + c8 if c8 < 4 else VO0 + c8 - 4
                mm(PS[bk][0:32, 0:TB], VDN[:, c8, :], PR[:, fcv, 1:1 + TB], c8 == 0, c8 == 7, [bPP, bPRs], [BPS[bk]])
            cp("act", VD, PS[bk][0:32, 0:TB], [BPS[bk]], [bVD])
            for p in range(4):
                bk = nxt_bank(4, 8)
                mm(PS[bk][:, 0:TB], VUP[:, p * 128:(p + 1) * 128], VD, True, True, [bPP, bVD], [BPS[bk]])
                act(S3[:, p, :], PS[bk][:, 0:TB], AF.Tanh, [BPS[bk], bPP], [b3], bias=PD[:, 12 + p:13 + p], scale=0.5)
            ts("pool", S3, S3, 0.5, ALU.mult, [b3], [b3], 0.5, ALU.add)
            S.dma("sp", lambda h, t0=t0: h.dma_start(out=S4, in_=vf[:, :, t0:t0 + TB].rearrange("c p t -> p c t")), [bVF], [b4])
            tt("dve", S4, S4, Vv, ALU.subtract, [b4, bPRs], [b4])
            tt("pool", S4, S4, S3, ALU.mult, [b4, b3], [b4])
            tt("dve", Vv, Vv, S4, ALU.add, [bPRs, b4], [bPRs])
        else:
            S.dma("sp", lambda h, t0=t0: h.dma_start(out=vf[:, :, t0:t0 + TB].rearrange("c p t -> p c t"), in_=Vv), [bPRs], [bVF],
                  is_output=True)
        tt("pool", S3, Kk, pcol(cKK), ALU.mult, [bPRs, bPP], [b3])
        act(S4, S3, AF.Square, [b3], [b4])
        for p in range(4):
            bk = nxt_bank(4, 8)
            mm(PS[bk][:, 0:TB], K.obd1, S4[:, p, :], True, True, [kb, b4], [BPS[bk]])
            act(S6[:, p, :], PS[bk][:, 0:TB], AF.Ln, [BPS[bk], kb], [b6], bias=K.eps_kk[:, 0:1])
        act(S6, S6, AF.Exp, [b6], [b6], scale=-0.5)
        tt("dve", S3, S3, S6, ALU.mult, [b3, b6], [b3])
        tt("pool", S4, S2, pcol(cKA), ALU.mult, [b2, bPP], [b4])
        tt("pool", S4, S4, dcol(8), ALU.add, [b4, bPP], [b4])
        tt("dve", Kk, Kk, S4, ALU.mult, [bPRs, b4], [bPRs])
        tt("pool", S4, Rr, Kk, ALU.mult, [bPRs], [b4])
        tt("pool", S4, S4, pcol(cRK), ALU.mult, [b4, bPP], [b4])
        for p in range(4):
            bk = nxt_bank(4, 8)
            mm(PS[bk][:, 0:TB], K.obd1, S4[:, p, :], True, True, [kb, b4], [BPS[bk]])
            tt("dve", S7[:, p, :], PS[bk][:, 0:TB], Vv[:, p, :], ALU.mult, [BPS[bk], bPRs], [b7])
        for p in range(4):
            S.op("dve", lambda h, p=p: h.tensor_tensor_scan(out=S5[:, p, :], data0=K.rst, data1=S1[:, p, :], initial=0.0,
                                                            op0=ALU.mult, op1=ALU.add), [kb, b1], [b5])
        tt("pool", S1, S5, S1, ALU.subtract, [b5, b1], [b1])
        act(S1, S1, AF.Exp, [b1], [b1])
        stt("dve", S1, S3, -1.0, S1, ALU.mult, ALU.mult, [b3, b1], [b1])
        act(S6, S5, AF.Exp, [b5], [b6])
        tt("dve", Rr, Rr, S6, ALU.mult, [bPRs, b6], [bPRs])
        cp("pool", GC, S6.rearrange("p a (c k) -> p a c k", k=CH)[:, :, :, CH - 1], [b6], [bGC])
        act(S6, S5, AF.Exp, [b5], [b6], scale=-1.0)
        tt("pool", S2, S3, S2, ALU.mult, [b3, b2], [b2])
        tt("dve", S3, S2, S6, ALU.mult, [b2, b6], [b3])
        tt("dve", Kk, Kk, S6, ALU.mult, [bPRs, b6], [bPRs])
        gcb = GC.unsqueeze(3).to_broadcast([128, 4, NCH, CH])
        v4 = lambda x: x.rearrange("p a (c k) -> p a c k", k=CH)
        tt("pool", v4(S2), v4(S3), gcb, ALU.mult, [b3, bGC], [b2])
        tt("pool", v4(S5), v4(Kk), gcb, ALU.mult, [bPRs, bGC], [b5])

        for c in range(NCH):
            cs = slice(c * CH, (c + 1) * CH)
            for h8 in range(8):
                p, e = h8 // 2, h8 % 2
                rows = slice(64 * e, 64 * e + 64)
                vB = PS[e].rearrange("p (a x t) -> p a x t", a=4, x=2)
                vK = PS[2 + e].rearrange("p (a x t) -> p a x t", a=4, x=2)
                mm(vB[0:64, p, 0, :], S3[rows, p, cs], S1[rows, p, cs], True, True, [b3, b1], [BPS[e]])
                mm(vB[0:64, p, 1, :], S3[rows, p, cs], Rr[rows, p, cs], True, True, [b3, bPRs], [BPS[e]])
                mm(vK[0:64, p, 0, :], Kk[rows, p, cs], S1[rows, p, cs], True, True, [bPRs, b1], [BPS[2 + e]])
                mm(vK[0:64, p, 1, :], Kk[rows, p, cs], Rr[rows, p, cs], True, True, [bPRs], [BPS[2 + e]])
            m2b = K.mask2.unsqueeze(1).to_broadcast([64, 4, 2, 64])
            for e in range(2):
                tt("dve", SBm[e], PS[e][0:64, :].rearrange("p (a x t) -> p a x t", a=4, x=2), m2b, ALU.mult,
                   [BPS[e], kb], [bSBm[e]])
                tt("dve", SKm[e], PS[2 + e][0:64, :].rearrange("p (a x t) -> p a x t", a=4, x=2), m2b, ALU.mult,
                   [BPS[2 + e], kb], [bSKm[e]])
            vA = PS[4].rearrange("p (h t) -> p h t", h=8)
            for h8 in range(8):
                p, e = h8 // 2, h8 % 2
                tr(vA[0:64, h8, :], SBm[e][:, p, 0, :], K.ident[0:64, 0:64], [bSBm[e], kb], [BPS[4]])
            cp("act", NA[0], vA[0:64], [BPS[4]], [bNA[0]])
            NATv = NAT[0].rearrange("p (a x) t -> p a x t", x=2)
            idb = K.ident[0:64, 0:64].unsqueeze(1).to_broadcast([64, 4, 64])
            for e in range(2):
                cp("pool", NATv[:, :, e, :], SBm[e][:, :, 0, :], [bSBm[e]], [bNAT[0]])
                tt("pool", PT[:, :, e, :], SBm[e][:, :, 0, :], idb, ALU.add, [bSBm[e], kb], [bPT])
            cur = 0
            PTf = PT.rearrange("p a x t -> p (a x) t")
            for lvl in range(5):
                nx = 1 - cur
                vN = PS[5].rearrange("p (h t) -> p h t", h=8)
                for h8 in range(8):
                    mm(vN[0:64, h8, :], NAT[cur][:, h8, :], NA[cur][:, h8, :], True, True, [bNAT[cur], bNA[cur]], [BPS[5]])
                cp("act", NA[nx], vN[0:64], [BPS[5]], [bNA[nx]])
                if lvl < 4:
                    vNT = PS[6].rearrange("p (h t) -> p h t", h=8)
                    for h8 in range(8):
                        mm(vNT[0:64, h8, :], NA[cur][:, h8, :], NAT[cur][:, h8, :], True, True, [bNAT[cur], bNA[cur]], [BPS[6]])
                    cp("dve", NAT[nx], vNT[0:64], [BPS[6]], [bNAT[nx]])
                vD = PS[7].rearrange("p (h t) -> p h t", h=8)
                for h8 in range(8):
                    mm(vD[0:64, h8, :], NA[nx][:, h8, :], PTf[:, h8, :], True, True, [bNA[nx], bPT], [BPS[7]])
                tt("dve", PTf, PTf, vD[0:64], ALU.add, [bPT, BPS[7]], [bPT])
                cur = nx
            for (src, bsrc, dst, bdst, bk, eng) in ((Vv, bPRs, VT, bVT, 4, "act"), (S2, b2, BST, bBST, 5, "dve"), (S5, b5, KST, bKST, 6, "act")):
                vT = PS[bk].rearrange("p (a f) -> p a f", a=4)
                for p in range(4):
                    tr(vT[0:64, p, :], src[:, p, cs], K.ident, [bsrc, kb], [BPS[bk]])
                cp(eng, dst, vT[0:64], [BPS[bk]], [bdst])
            vX = PS[7].rearrange("p (h t) -> p h t", h=8)
            for h8 in range(8):
                p, e = h8 // 2, h8 % 2
                mm(vX[0:64, h8, :], S1[:, p, cs], HB[hcur][:, p, 64 * e:64 * e + 64], True, False, [b1, bHB[hcur]], [BPS[7]])
                mm(vX[0:64, h8, :], SKm[e][:, p, 0, :], VT[:, p, 64 * e:64 * e + 64], False, True, [bSKm[e], bVT], [BPS[7]])
            cp("act", Xs, vX[0:64], [BPS[7]], [bXs])
            vU = PS[4].rearrange("p (h t) -> p h t", h=8)
            for h8 in range(8):
                mm(vU[0:64, h8, :], PTf[:, h8, :], Xs[:, h8, :], True, True, [bPT, bXs], [BPS[4]])
            cp("dve", Us.rearrange("p a x t -> p (a x) t"), vU[0:64], [BPS[4]], [bUs])
            vO = PS[5].rearrange("p (a t) -> p a t", a=8)
            for h8 in range(8):
                p, e = h8 // 2, h8 % 2
                o_ap = vO[64 * e:64 * e + 64, p, :]
                mm(o_ap, HB[hcur][:, p, 64 * e:64 * e + 64], Rr[:, p, cs], True, False, [bHB[hcur], bPRs], [BPS[5]])
                mm(o_ap, Us[:, p, e, :], SBm[e][:, p, 1, :], False, False, [bUs, bSBm[e]], [BPS[5]])
                mm(o_ap, VT[:, p, 64 * e:64 * e + 64], SKm[e][:, p, 1, :], False, True, [bVT, bSKm[e]], [BPS[5]])
            cp("act", OTR[:, :, cs], vO[:, 0:4, :], [BPS[5]], [bOTR])
            vH = PS[6].rearrange("p (a f) -> p a f", a=4)
            for p in range(4):
                mm(vH[:, p, :], BST[:, p, :], Us[:, p, :, :].rearrange("p x t -> p (x t)"), True, False, [bBST, bUs], [BPS[6]])
                mm(vH[:, p, :], KST[:, p, :], VT[:, p, :], False, True, [bKST, bVT], [BPS[6]])
            hn = 1 - hcur
            for e in range(2):
                rows = slice(64 * e, 64 * e + 64)
                cols = slice(64 * e, 64 * e + 64)
                tt("pool", HD[rows], HB[hcur][rows, :, cols], GC[rows, :, c:c + 1].to_broadcast([64, 4, 64]), ALU.mult,
                   [bHB[hcur], bGC], [bHD])
                tt("dve", HB[hn][rows, :, cols], HD[rows], vH[rows, :, cols], ALU.add, [bHD, BPS[6]], [bHB[hn]])
            hcur = hn

        for p in range(4):
            bk = nxt_bank(0, 4)
            mm(PS[bk][:, 0:TB], K.obd64, OTR[:, p, :], True, True, [kb, bOTR], [BPS[bk]])
            tt("dve", S4[:, p, :], OTR[:, p, :], PS[bk][:, 0:TB], ALU.subtract, [bOTR, BPS[bk]], [b4])
        act(S6, S4, AF.Square, [b4], [b6])
        for p in range(4):
            bk = nxt_bank(0, 4)
            mm(PS[bk][:, 0:TB], K.obd64, S6[:, p, :], True, True, [kb, b6], [BPS[bk]])
            act(S6[:, p, :], PS[bk][:, 0:TB], AF.Ln, [BPS[bk], kb], [b6], bias=K.eps_gn[:, 0:1])
        act(S6, S6, AF.Exp, [b6], [b6], scale=-0.5)
        tt("dve", S4, S4, S6, ALU.mult, [b4, b6], [b4])
        tt("pool", S4, S4, pcol(cGNW), ALU.mult, [b4, bPP], [b4])
        tt("pool", S4, S4, pcol(cGNB), ALU.add, [b4, bPP], [b4])
        tt("dve", S4, S4, S7, ALU.add, [b4, b7], [b4])
        act(S6, Zz, AF.Tanh, [bPRs], [b6], scale=0.5)
        ts("pool", S6, S6, 0.5, ALU.mult, [b6], [b6], 0.5, ALU.add)
        tt("pool", S6, S6, Zz, ALU.mult, [b6, bPRs], [b6])
        tt("dve", OTB[:, 0:4, :], S4, S6, ALU.mult, [b4, b6], [bOTB])

        Qq, Ff, Zh = (PR[:, a:a + 4, 1:1 + TB] for a in (Q0, F0, ZH0))
        act(H1, Qq, AF.Tanh, [bPRh], [bh1], scale=0.5)
        ts("pool", H1, H1, 0.5, ALU.mult, [bh1], [bh1], 0.5, ALU.add)
        tt("dve", Qq, Qq, H1, ALU.mult, [bPRh, bh1], [bPRh])
        act(H1, Ff, AF.Tanh, [bPRh], [bh1], scale=0.5)
        ts("pool", H1, H1, 0.5, ALU.mult, [bh1], [bh1], 0.5, ALU.add)
        tt("pool", H2, H1, dcol(20), ALU.mult, [bh1, bPP], [bh2])
        tt("pool", H2, H2, dcol(24), ALU.add, [bh2, bPP], [bh2])
        act(H2, H2, AF.Ln, [bh2], [bh2])
        ts("pool", Ff, H1, -1.0, ALU.mult, [bh1], [bPRh], 1.0, ALU.add)
        tt("pool", Ff, Ff, dcol(20), ALU.mult, [bPRh, bPP], [bPRh])
        for p in range(4):
            S.op("dve", lambda h, p=p: h.tensor_tensor_scan(out=H3[:, p, :], data0=K.rst, data1=H2[:, p, :], initial=0.0,
                                                            op0=ALU.mult, op1=ALU.add), [kb, bh2], [bh3])
        H3v = H3.rearrange("p a (c k) -> p a c k", k=CH)
        cp("pool", BM, H3v[:, :, :, CH // 2 - 1], [bh3], [bBM])
        cp("pool", BL, H3v[:, :, :, CH - 1], [bh3], [bBM])
        bmb = BM.unsqueeze(3).to_broadcast([128, 4, NCH, CH])
        tt("pool", v4(H2), H3v, bmb, ALU.subtract, [bh3, bBM], [bh2])
        act(H1, H2, AF.Exp, [bh2], [bh1])
        tt("dve", Qq, Qq, H1, ALU.mult, [bPRh, bh1], [bPRh])
        act(H1, H2, AF.Exp, [bh2], [bh1], scale=-1.0)
        tt("dve", Ff, Ff, H1, ALU.mult, [bPRh, bh1], [bPRh])
        act(EM, BM, AF.Exp, [bBM], [bBM])
        act(EL, BL, AF.Exp, [bBM], [bBM])
        tt("pool", ELM, BL, BM, ALU.subtract, [bBM], [bBM])
        act(ELM, ELM, AF.Exp, [bBM], [bBM])
        for c in range(NCH):
            cs = slice(c * CH, (c + 1) * CH)
            bkA = nxt_bank(0, 4)
            vAt = PS[bkA].rearrange("p (a t) -> p a t", a=8)
            for h4 in range(4):
                mm(vAt[0:64, h4, :], Ff[:, h4, cs], Qq[:, h4, cs], True, True, [bPRh], [BPS[bkA]])
            tt("dve", ATT, vAt[0:64, 0:4, :], K.mask2[:, 1, :].unsqueeze(1).to_broadcast([64, 4, 64]), ALU.mult,
               [BPS[bkA], kb], [bATT])
            bkT = nxt_bank(0, 4)
            vKt = PS[bkT].rearrange("p (a f) -> p a f", a=4)
            for h4 in range(4):
                tr(vKt[0:64, h4, :], Ff[:, h4, cs], K.ident, [bPRh, kb], [BPS[bkT]])
            cp("act", KTK, vKt[0:64], [BPS[bkT]], [bKTK])
            tt("pool", SM, SH, EM[:, :, c:c + 1].to_broadcast([128, 4, 128]), ALU.mult, [bSH, bBM], [bSM])
            bkO = nxt_bank(0, 4)
            vOh = PS[bkO].rearrange("p (a t) -> p a t", a=8)
            for h4 in range(4):
                mm(vOh[:, h4, :], SM[:, h4, :], Qq[:, h4, cs], True, False, [bSM, bPRh], [BPS[bkO]])
                mm(vOh[:, h4, :], ITOK[:, c, h4 * 128:(h4 + 1) * 128], ATT[:, h4, :], False, True, [bITOK, bATT], [BPS[bkO]])
            cp("act", OTH[:, :, cs], vOh[:, 0:4, :], [BPS[bkO]], [bOTH])
            bkS = nxt_bank(0, 4)
            vS = PS[bkS].rearrange("p (a f) -> p a f", a=4)
            for h4 in range(4):
                mm(vS[:, h4, :], KTK[:, h4, :], ITOK[:, c, h4 * 128:(h4 + 1) * 128], True, True, [bKTK, bITOK], [BPS[bkS]])
            tt("dve", SD, vS, ELM[:, :, c:c + 1].to_broadcast([128, 4, 128]), ALU.mult, [BPS[bkS], bBM], [bSD])
            tt("pool", SH, SH, EL[:, :, c:c + 1].to_broadcast([128, 4, 128]), ALU.mult, [bSH, bBM], [bSH])
            tt("pool", SH, SH, SD, ALU.add, [bSH, bSD], [bSH])
        act(H1, OTH, AF.Square, [bOTH], [bh1])
        for p in range(4):
            bk = nxt_bank(0, 4)
            mm(PS[bk][:, 0:TB], K.o128, H1[:, p, :], True, True, [kb, bh1], [BPS[bk]])
            act(H1[:, p, :], PS[bk][:, 0:TB], AF.Ln, [BPS[bk], kb], [bh1], bias=K.eps_rms[:, 0:1])
        act(H1, H1, AF.Exp, [bh1], [bh1], scale=-0.5)
        tt("dve", OTH, OTH, H1, ALU.mult, [bOTH, bh1], [bOTH])
        tt("pool", OTH, OTH, pcol(cGW), ALU.mult, [bOTH, bPP], [bOTH])
        act(H2, Zh, AF.Tanh, [bPRh], [bh2], scale=0.5)
        ts("pool", H2, H2, 0.5, ALU.mult, [bh2], [bh2], 0.5, ALU.add)
        tt("pool", H2, H2, Zh, ALU.mult, [bh2, bPRh], [bh2])
        tt("dve", OTB[:, 4:8, :], OTH, H2, ALU.mult, [bOTH, bh2], [bOTB])
        S.dma("sp", lambda h, t0=t0: h.dma_start(out=oT_out[:, :, t0:t0 + TB].rearrange("c p t -> p c t"), in_=OTB), [bOTB], [],
              is_output=True)


def _cols(g, has_vmix):
    fr = g * 512 + np.arange(512)
    fo = (1 - g) * 512 + np.arange(512)
    sh = [fr, 1024 + fr, 2048 + fr, 3072 + fr, 4096 + np.arange(128)]
    if has_vmix:
        sh.append(2048 + fo)
    shift_cols = np.concatenate(sh)
    base = 4224
    hg = np.concatenate([base + fr, base + 1024 + fr, base + 3072 + fr])
    icols = base + 2048 + fr
    return fr, fo, shift_cols, np.concatenate([shift_cols, hg]), icols


def pack_A(inp, l, g):
    has_vmix = l > 0
    fr, fo, shift_cols, fm_cols, icols = _cols(g, has_vmix)
    NS = len(shift_cols) // 128
    NFM = len(fm_cols) // 128
    w = np.asarray(inp["w_in"][l], np.float32)
    wfm = np.ascontiguousarray(w[:, fm_cols].reshape(KC, 128, NFM, 128).transpose(2, 1, 0, 3)).reshape(NFM, 128, KC * 128)
    wi = np.ascontiguousarray(w[:, icols].reshape(KC, 128, 512).transpose(1, 0, 2)).reshape(128, KC * 512)

    def c4(v):
        return np.asarray(v, np.float32).reshape(4, 128).T

    mu = np.asarray(inp["shift_mu"][l], np.float32)[shift_cols].reshape(NS, 128).T
    vm0 = inp["v_mix0"][0][fr] if has_vmix else np.zeros(512, np.float32)
    pp = np.concatenate([mu, c4(inp["w_decay0"][l][fr]), c4(inp["a0"][l][fr]), c4(inp["k_k"][l][fr]), c4(inp["k_a"][l][fr]),
                         c4(inp["r_k"][l][fr]), c4(inp["ln_x_w"][l][fr]), c4(inp["ln_x_b"][l][fr]), c4(vm0),
                         c4(inp["lb_logits"][0][fr]), c4(inp["lb_logits"][1][fr]), c4(inp["g_norm_w"][l][fr])], axis=1)
    lrw = np.zeros((128, 512), np.float32)
    lrw[0:64] = inp["w_decay_up"][l][:, fr]
    lra = np.zeros((128, 512), np.float32)
    lra[64:128] = inp["a_up"][l][:, fr]
    d = {"wfm": wfm, "wi": wi, "pp": np.ascontiguousarray(pp, dtype=np.float32), "lrw": lrw, "lra": lra}
    if has_vmix:
        vd = np.asarray(inp["v_mix_down"][0], np.float32)
        rows = np.concatenate([fr, fo])
        d["vdn"] = np.ascontiguousarray(vd[rows].reshape(8, 128, 32).transpose(1, 0, 2)).reshape(128, 256)
        d["vup"] = np.ascontiguousarray(np.asarray(inp["v_mix_up"][0], np.float32)[:, fr])
    return d


def decl_A(nc, tag, has_vmix, kind="ExternalInput"):
    NS = 21 if has_vmix else 17
    NFM = NS + 12
    W = {
        "wfm": nc.dram_tensor("wfm" + tag, [NFM, 128, KC * 128], F32, kind=kind).ap(),
        "wi": nc.dram_tensor("wi" + tag, [128, KC * 512], F32, kind=kind).ap(),
        "pp": nc.dram_tensor("pp" + tag, [128, NS + 44], F32, kind=kind).ap(),
        "lrw": nc.dram_tensor("lrw" + tag, [128, 512], F32, kind=kind).ap(),
        "lra": nc.dram_tensor("lra" + tag, [128, 512], F32, kind=kind).ap(),
        "wscr": nc.dram_tensor("wscr" + tag, [NFM, 128, KC * 128], BF16, kind="Internal").ap(),
    }
    if has_vmix:
        W["vdn"] = nc.dram_tensor("vdn" + tag, [128, 256], F32, kind=kind).ap()
        W["vup"] = nc.dram_tensor("vup" + tag, [32, 512], F32, kind=kind).ap()
    return W


def emit_B(C, oT_full, hin, tok0, ntok, W, hout, is_output):
    mm, tr, act, tt, ts, stt, cp, ms = _ops(C)
    S = C.S
    K = C.K
    A = C.arena
    A.off = C.persist_off
    S.barrier()
    PS, BPS = C.ps, C.bps
    kb = K.b
    stg = [A.alloc([128, 2048]) for _ in range(2)]
    bstg = [Buf("bstg%d" % i) for i in range(2)]
    WOB = A.alloc([128, KC * 2048 // 2]).bitcast(BF16).rearrange("p (k d) -> p k d", k=KC)
    bWOB = Buf("WOB")
    LNW = A.alloc([128, 2048]); LNB = A.alloc([128, 2048]); bLN = Buf("LN")
    OTt = [A.alloc([128, KC * 128 // 2]).bitcast(BF16).rearrange("p (c t) -> p c t", c=KC) for _ in range(2)]
    bOTt = [Buf("OTt%d" % i) for i in range(2)]
    Ht = [A.alloc([128, 2048]) for _ in range(2)]
    bHt = [Buf("Ht%d" % i) for i in range(2)]
    Z = A.alloc([128, 2048]); bZ = Buf("Z")
    JK = A.alloc([128, 2048]); bJK = Buf("JK")
    OU = A.alloc([128, 2048]); bOU = Buf("OU")
    ST = A.alloc([128, 8]); bST = Buf("ST")
    S.dma("sp", lambda h: h.dma_start(out=LNW, in_=W["lnw"]), (), [bLN])
    S.dma("sp", lambda h: h.dma_start(out=LNB, in_=W["lnb"]), (), [bLN])
    for kc in range(KC):
        sl = kc % 2
        S.dma("sp", lambda h, sl=sl, kc=kc: h.dma_start(out=stg[sl], in_=W["wout"][:, kc * 2048:(kc + 1) * 2048]), (), [bstg[sl]])
        cp("pool", WOB[:, kc, :], stg[sl], [bstg[sl]], [bWOB])
    for i in range(ntok // 128):
        tk = tok0 + i * 128
        sl = i % 2
        S.dma("sp", lambda h, sl=sl, tk=tk: h.dma_start(out=OTt[sl], in_=oT_full[:, :, tk:tk + 128].rearrange("c p t -> p c t")),
              W.get("oT_deps", ()), [bOTt[sl]])
        S.dma("sp", lambda h, sl=sl, tk=tk: h.dma_start(out=Ht[sl], in_=hin[tk:tk + 128, :]), W.get("h_deps", ()), [bHt[sl]])
        for n in range(4):
            bk = (4 * (i % 2)) + n
            for kc in range(KC):
                mm(PS[bk][:, :], OTt[sl][:, kc, :], WOB[:, kc, n * 512:(n + 1) * 512], kc == 0, kc == KC - 1, [bOTt[sl], bWOB], [BPS[bk]])
            stt("dve", Z[:, n * 512:(n + 1) * 512], Ht[sl][:, n * 512:(n + 1) * 512], ALPHA, PS[bk][:, :], ALU.mult, ALU.add,
                [bHt[sl], BPS[bk]], [bZ])
        S.op("act", lambda h: h.activation(out=JK, in_=Z, func=AF.Identity, accum_out=ST[:, 0:1]), [bZ], [bJK, bST])
        S.op("act", lambda h: h.activation(out=JK, in_=Z, func=AF.Square, accum_out=ST[:, 1:2]), [bZ], [bJK, bST])
        q = [bST]
        ts("dve", ST[:, 2:3], ST[:, 0:1], 1.0 / D, ALU.mult, q, q)
        tt("dve", ST[:, 3:4], ST[:, 2:3], ST[:, 2:3], ALU.mult, q, q)
        stt("dve", ST[:, 4:5], ST[:, 1:2], 1.0 / D, ST[:, 3:4], ALU.mult, ALU.subtract, q, q)
        act(ST[:, 5:6], ST[:, 4:5], AF.Ln, q + [kb], q, bias=K.eps_ln[:, 0:1])
        act(ST[:, 5:6], ST[:, 5:6], AF.Exp, q, q, scale=-0.5)
        S.op("dve", lambda h: h.tensor_scalar(OU, Z, ST[:, 2:3], ST[:, 5:6], ALU.subtract, ALU.mult), [bZ, bST], [bOU])
        tt("pool", OU, OU, LNW, ALU.mult, [bOU, bLN], [bOU])
        tt("dve", OU, OU, LNB, ALU.add, [bOU, bLN], [bOU])
        S.dma("sp", lambda h, i=i: h.dma_start(out=hout[i * 128:(i + 1) * 128, :], in_=OU), [bOU], W.get("out_bufs", []),
              is_output=is_output)


def pack_B(inp, l):
    rows = np.concatenate([np.arange(0, 512), 1024 + np.arange(0, 512), 512 + np.arange(0, 512), 1536 + np.arange(0, 512)])
    w = np.asarray(inp["w_out"][l], np.float32)[rows]
    wout = np.ascontiguousarray(w.reshape(KC, 128, D).transpose(1, 0, 2)).reshape(128, KC * D)
    lnw = np.ascontiguousarray(np.broadcast_to(np.asarray(inp["ln_w"][l], np.float32)[None, :], (128, D)))
    lnb = np.ascontiguousarray(np.broadcast_to(np.asarray(inp["ln_b"][l], np.float32)[None, :], (128, D)))
    return {"wout": wout, "lnw": lnw, "lnb": lnb}


def decl_B(nc, tag, kind="ExternalInput"):
    return {"wout": nc.dram_tensor("wout" + tag, [128, KC * D], F32, kind=kind).ap(),
            "lnw": nc.dram_tensor("lnw" + tag, [128, D], F32, kind=kind).ap(),
            "lnb": nc.dram_tensor("lnb" + tag, [128, D], F32, kind=kind).ap()}


def _build_A(has_vmix, li, T):
    nc = bass.Bass("TRN2", target_bir_lowering=False)
    with contextlib.ExitStack() as st:
        C = mk_ctx(nc, st)
        hin = nc.dram_tensor("hin", [T, D], F32, kind="ExternalInput").ap()
        W = decl_A(nc, "", has_vmix)
        oT = nc.dram_tensor("oT", [8, 128, T], BF16, kind="ExternalOutput").ap()
        vf = nc.dram_tensor("vf", [4, 128, T], F32, kind="ExternalInput" if has_vmix else "ExternalOutput").ap()
        emit_consts(C)
        emit_A(C, hin, W, oT, vf, has_vmix, li, T)
        C.S.emit(st)
    return nc


def _build_B(ntok):
    nc = bass.Bass("TRN2", target_bir_lowering=False)
    with contextlib.ExitStack() as st:
        C = mk_ctx(nc, st)
        hin = nc.dram_tensor("hin", [ntok, D], F32, kind="ExternalInput").ap()
        oT = nc.dram_tensor("oT", [16, 128, ntok], BF16, kind="ExternalInput").ap()
        W = decl_B(nc, "")
        hout = nc.dram_tensor("hout", [ntok, D], F32, kind="ExternalOutput").ap()
        emit_consts(C)
        emit_B(C, oT, hin, 0, ntok, W, hout, True)
        C.S.emit(st)
    return nc


def kernel_unfused(inp):
    x = np.asarray(inp["x"], np.float32)
    Bn, T, _ = x.shape
    cores = list(range(8))
    half = T // 2
    h = x
    vfs = None
    for l in range(DEPTH):
        ncA = _build_A(l > 0, l, T)
        packs = [pack_A(inp, l, g) for g in range(2)]
        maps = []
        for cid in cores:
            b, g = cid // 2, cid % 2
            m = dict(packs[g])
            m["hin"] = np.ascontiguousarray(h[b])
            if l > 0:
                m["vf"] = vfs[cid]
            maps.append(m)
        res = run_bass_kernel_spmd(ncA, maps, core_ids=cores).results
        if l == 0:
            vfs = [np.asarray(r["vf"]) for r in res]
        oTs = [np.asarray(r["oT"]) for r in res]
        ncB = _build_B(half)
        pb = pack_B(inp, l)
        maps = []
        for cid in cores:
            b, s = cid // 2, cid % 2
            m = dict(pb)
            full = np.concatenate([oTs[2 * b], oTs[2 * b + 1]], axis=0)
            m["oT"] = np.ascontiguousarray(full[:, :, s * half:(s + 1) * half])
            m["hin"] = np.ascontiguousarray(h[b, s * half:(s + 1) * half])
            maps.append(m)
        res = run_bass_kernel_spmd(ncB, maps, core_ids=cores).results
        hn = np.empty_like(x)
        for cid in cores:
            b, s = cid // 2, cid % 2
            hn[b, s * half:(s + 1) * half] = np.asarray(res[cid]["hout"])
        h = hn
    return h.astype(np.float32)


def kernel(**inputs):
    return kernel_unfused(inputs)
```

```python
import contextlib
import os
import numpy as np
import ml_dtypes
import concourse.bass as bass
import concourse.mybir as mybir
from concourse.bass_utils import run_bass_kernel_spmd

F32 = mybir.dt.float32
BF16 = mybir.dt.bfloat16
ALU = mybir.AluOpType
AF = mybir.ActivationFunctionType

D = 2048
KC = D // 128
SEQ = 2048
BATCH = 4
DEPTH = 2
TB = 256
CH = 64
NCH = TB // CH
ALPHA = (2 * DEPTH) ** 0.25
LN_EPS = 1e-5
GN_EPS = 64e-5
RMS_EPS = 1e-5
LOGW_SCALE = -float(np.exp(-0.5))


class Buf:
    __slots__ = ("name", "writers", "readers")

    def __init__(self, name=""):
        self.name = name
        self.writers = {}
        self.readers = {}


class Sched:
    ENGS = ("pe", "act", "dve", "pool", "sp")

    def __init__(self, nc, n_dma_sems=32):
        self.nc = nc
        self.cnt = {e: 0 for e in self.ENGS}
        self.seen = {e: {} for e in self.ENGS}
        self.prog = {e: [] for e in self.ENGS}
        self.n_dma_sems = n_dma_sems
        self.dma_val = [0] * n_dma_sems
        self.dma_rr = 0
        self.out_dma = []
        self.pending = {e: {} for e in self.ENGS}

    def _need(self, eng, deps, key, val):
        if val <= self.seen[eng].get(key, 0):
            return
        if deps.get(key, 0) < val:
            deps[key] = val

    def barrier(self):
        snap = {e: self.cnt[e] for e in self.ENGS if self.cnt[e] > 0}
        for s in range(self.n_dma_sems):
            if self.dma_val[s] > 0:
                snap[("dma", s)] = self.dma_val[s]
        for e in self.ENGS:
            for k, v in snap.items():
                if k == e:
                    continue
                if self.pending[e].get(k, 0) < v:
                    self.pending[e][k] = v

    def _take_pending(self, eng, deps):
        if self.pending[eng]:
            for k, v in self.pending[eng].items():
                self._need(eng, deps, k, v)
            self.pending[eng] = {}

    def op(self, eng, fn, reads=(), writes=()):
        deps = {}
        self._take_pending(eng, deps)
        for b in reads:
            for k, v in b.writers.items():
                if k == eng and eng == "pe":
                    continue
                self._need(eng, deps, k, v)
        strict = False
        for b in writes:
            for k, v in b.writers.items():
                if k == eng and not strict:
                    continue
                self._need(eng, deps, k, v)
            for k, v in b.readers.items():
                if k == eng and not strict:
                    continue
                self._need(eng, deps, k, v)
        for k, v in deps.items():
            self.seen[eng][k] = v
        self.cnt[eng] += 1
        idx = self.cnt[eng]
        self.prog[eng].append((list(deps.items()), fn, ("eng", eng)))
        for b in reads:
            b.readers[eng] = idx
        for b in writes:
            b.writers = {eng: idx}
            b.readers = {}
        return idx

    def dma(self, q, fn, reads=(), writes=(), is_output=False):
        deps = {}
        self._take_pending(q, deps)
        for b in reads:
            for k, v in b.writers.items():
                self._need(q, deps, k, v)
        for b in writes:
            for k, v in b.writers.items():
                self._need(q, deps, k, v)
            for k, v in b.readers.items():
                self._need(q, deps, k, v)
        s = self.dma_rr
        self.dma_rr = (self.dma_rr + 1) % self.n_dma_sems
        key = ("dma", s)
        if self.dma_val[s] > 0:
            self._need(q, deps, key, self.dma_val[s])
        for k, v in deps.items():
            self.seen[q][k] = v
        self.dma_val[s] += 16
        val = self.dma_val[s]
        self.prog[q].append((list(deps.items()), fn, ("dma", s)))
        for b in reads:
            b.readers[key] = val
        for b in writes:
            b.writers = {key: val}
            b.readers = {}
        if is_output:
            self.out_dma.append((key, val))
        return key, val

    def emit(self, st, final_eng="sp"):
        nc = self.nc
        esem = {e: st.enter_context(nc.semaphore("s_" + e)) for e in self.ENGS}
        dsem = [st.enter_context(nc.semaphore("d%d" % i)) for i in range(self.n_dma_sems)]

        def semof(key):
            if isinstance(key, tuple):
                return dsem[key[1]]
            return esem[key]

        fin = {}
        for k, v in self.out_dma:
            fin[k] = max(fin.get(k, 0), v)
        block = st.enter_context(nc.Block())
        hmap = {"pe": nc.tensor, "act": nc.scalar, "dve": nc.vector, "pool": nc.gpsimd, "sp": nc.sync}

        def mk(e):
            def body(h):
                for waits, fn, inc in self.prog[e]:
                    for k, v in waits:
                        h.wait_ge(semof(k), v)
                    ins = fn(h)
                    if inc[0] == "eng":
                        ins.then_inc(esem[inc[1]], 1)
                    else:
                        ins.then_inc(dsem[inc[1]], 16)
                if e == final_eng:
                    for k, v in fin.items():
                        h.wait_ge(semof(k), v)
            return body
        block.tensor(mk("pe"))
        block.scalar(mk("act"))
        block.vector(mk("dve"))
        block.gpsimd(mk("pool"))
        block.sync(mk("sp"))


class Arena:
    def __init__(self, tensor, n):
        self.t = tensor
        self.n = n
        self.off = 0

    def reset(self):
        self.off = 0

    def alloc(self, shape):
        free = int(np.prod(shape[1:]))
        free_al = (free + 1) // 2 * 2
        assert self.off + free_al <= self.n, ("arena overflow", self.off, free_al, self.n)
        ap = self.t[0:shape[0], self.off:self.off + free]
        self.off += free_al
        if len(shape) == 2:
            return ap
        names = " ".join("d%d" % i for i in range(len(shape) - 1))
        kw = {"d%d" % i: shape[i + 1] for i in range(len(shape) - 1)}
        return ap.rearrange("p (%s) -> p %s" % (names, names), **kw)


class Ctx:
    pass


def alloc_bf(A, shape):
    free = int(np.prod(shape[1:]))
    ap = A.alloc([shape[0], (free + 1) // 2]).bitcast(BF16)[:, 0:free]
    if len(shape) == 2:
        return ap
    names = " ".join("d%d" % i for i in range(len(shape) - 1))
    kw = {"d%d" % i: shape[i + 1] for i in range(len(shape) - 1)}
    return ap.rearrange("p (%s) -> p %s" % (names, names), **kw)


def bc(ap_col, n):
    return ap_col.unsqueeze(len(ap_col.shape)).to_broadcast(list(ap_col.shape) + [n])


def mk_ctx(nc, st):
    C = Ctx()
    C.nc = nc
    C.S = Sched(nc)
    NF = 53200
    C.arena_t = st.enter_context(nc.sbuf_tensor("arena", [128, NF], F32))
    C.arena = Arena(C.arena_t, NF)
    C.ps = [st.enter_context(nc.psum_tensor("ps%d" % i, [128, 512], F32)) for i in range(8)]
    C.bps = [Buf("ps%d" % i) for i in range(8)]
    C.uid = [0]
    return C


def _ops(C):
    S = C.S

    def mm(out, lhsT, rhs, start, stop, reads, writes):
        S.op("pe", lambda h: h.matmul(out, lhsT=lhsT, rhs=rhs, start=start, stop=stop), reads, writes)

    def tr(out, in_, ident, reads, writes):
        S.op("pe", lambda h: h.transpose(out, in_, ident), reads, writes)

    def act(out, in_, func, reads, writes, bias=None, scale=1.0):
        if bias is None:
            S.op("act", lambda h: h.activation(out=out, in_=in_, func=func, scale=scale), reads, writes)
        else:
            S.op("act", lambda h: h.activation(out=out, in_=in_, func=func, bias=bias, scale=scale), reads, writes)

    def tt(eng, out, a, b, op, reads, writes):
        S.op(eng, lambda h: h.tensor_tensor(out=out, in0=a, in1=b, op=op), reads, writes)

    def ts(eng, out, a, s1, op0, reads, writes, s2=None, op1=None):
        if op1 is None:
            S.op(eng, lambda h: h.tensor_scalar(out, a, s1, None, op0), reads, writes)
        else:
            S.op(eng, lambda h: h.tensor_scalar(out, a, s1, s2, op0, op1), reads, writes)

    def stt(eng, out, in0, scalar, in1, op0, op1, reads, writes):
        S.op(eng, lambda h: h.scalar_tensor_tensor(out=out, in0=in0, scalar=scalar, in1=in1, op0=op0, op1=op1), reads, writes)

    def cp(eng, out, in_, reads, writes):
        if eng == "act":
            S.op("act", lambda h: h.copy(out, in_), reads, writes)
        else:
            S.op(eng, lambda h: h.tensor_copy(out, in_), reads, writes)

    def ms(eng, out, val, writes):
        S.op(eng, lambda h: h.memset(out, val), (), writes)

    return mm, tr, act, tt, ts, stt, cp, ms


def emit_consts(C):
    mm, tr, act, tt, ts, stt, cp, ms = _ops(C)
    S = C.S
    A = C.arena
    K = Ctx()
    C.K = K
    K.b = Buf("consts")
    K.ones = A.alloc([128, 128])
    K.ident = A.alloc([128, 128])
    K.obd1 = A.alloc([128, 128])
    K.obd64 = A.alloc([128, 128])
    K.o128 = A.alloc([128, 128])
    K.mask2 = A.alloc([64, 2, 64])
    K.mask3 = A.alloc([64, 64])
    K.rst = A.alloc([128, TB])
    K.eps_kk = A.alloc([128, 2])
    K.eps_gn = A.alloc([128, 2])
    K.eps_rms = A.alloc([128, 2])
    K.eps_ln = A.alloc([128, 2])
    b = [K.b]
    ms("pool", K.ones, 1.0, b)
    S.op("pool", lambda h: h.affine_select(out=K.ident, in_=K.ones, pattern=[[-1, 128]], compare_op=ALU.is_equal,
                                           fill=0.0, base=0, channel_multiplier=1), b, b)
    ms("pool", K.obd1, 0.0, b)
    ms("pool", K.obd1[0:64, 0:64], 1.0, b)
    ms("pool", K.obd1[64:128, 64:128], 1.0, b)
    ts("pool", K.obd64, K.obd1, 1.0 / 64, ALU.mult, b, b)
    ms("pool", K.o128, 1.0 / 128, b)
    S.op("pool", lambda h: h.affine_select(out=K.mask2[:, 0, :], in_=K.ones[0:64, 0:64], pattern=[[1, 64]],
                                           compare_op=ALU.is_gt, fill=0.0, base=0, channel_multiplier=-1), b, b)
    S.op("pool", lambda h: h.affine_select(out=K.mask2[:, 1, :], in_=K.ones[0:64, 0:64], pattern=[[1, 64]],
                                           compare_op=ALU.is_ge, fill=0.0, base=0, channel_multiplier=-1), b, b)
    S.op("pool", lambda h: h.affine_select(out=K.mask3, in_=K.ones[0:64, 0:64], pattern=[[-1, 64]],
                                           compare_op=ALU.is_gt, fill=0.0, base=0, channel_multiplier=1), b, b)
    ms("pool", K.rst, 1.0, b)
    ms("pool", K.eps_kk, 1e-24, b)
    ms("pool", K.eps_gn, GN_EPS, b)
    ms("pool", K.eps_rms, RMS_EPS, b)
    ms("pool", K.eps_ln, LN_EPS, b)
    ms("pool", K.rst.rearrange("p (c k) -> p c k", k=CH)[:, :, 0:1], 0.0, b)
    C.persist_off = A.off


def _interleave(gens, burst=None):
    if burst is None:
        burst = [1] * len(gens)
    pairs = [(g, b) for g, b in zip(gens, burst) if g is not None]
    while pairs:
        alive = []
        for g, b in pairs:
            ok = True
            for _ in range(b):
                try:
                    next(g)
                except StopIteration:
                    ok = False
                    break
            if ok:
                alive.append((g, b))
        pairs = alive


def _chain(*gens):
    for g in gens:
        if g is not None:
            yield from g


def _drain(g):
    for _ in g:
        pass


def emit_A(C, hin, W, oT_out, vf, has_vmix, li, T, dbg=None):
    mm, tr, act, tt, ts, stt, cp, ms = _ops(C)
    S = C.S
    K = C.K
    A = C.arena
    A.off = C.persist_off
    S.barrier()
    PS = C.ps
    NS = 21 if has_vmix else 17
    NFM = NS + 12
    R0, K0, V0, Z0, WA = 0, 4, 8, 12, 16
    VO0 = 17
    Q0, F0, ZH0 = NS, NS + 4, NS + 8
    cMU = 0
    cW0, cA0, cKK, cKA, cRK, cGNW, cGNB, cVM0, cLB0, cLB1, cGW = [NS + 4 * i for i in range(11)]
    NPP = NS + 44
    kb = K.b
    NB = T // TB
    BK = [Buf("psb%d" % i) for i in range(8)]

    def pv(bank, i):
        return PS[bank][:, i * 256:(i + 1) * 256]

    NG = 2
    PG = 4 // NG

    def tok2(name):
        return [Buf(name + str(i)) for i in range(NG)]

    stg = [A.alloc([128, 2048]) for _ in range(2)]
    bstg = [Buf("stg%d" % i) for i in range(2)]
    HTB = alloc_bf(A, [128, KC, TB])
    bHTB = Buf("HTB")
    NWB = 4
    WB = [alloc_bf(A, [128, KC, 128]) for _ in range(NWB)]
    bWB = [Buf("wb%d" % i) for i in range(NWB)]
    WIB = alloc_bf(A, [128, KC, 512])
    bWIB = Buf("WIB")
    PR = A.alloc([128, NFM, TB + 2])
    bPR = [Buf("PR%d" % i) for i in range(NFM)]
    LAST = A.alloc([128, NS])
    bLAST = Buf("LAST")
    ITOK = alloc_bf(A, [64, NCH, 512])
    bITOK = Buf("ITOK")
    PP = A.alloc([128, NPP])
    PD = A.alloc([128, 32])
    bPP = Buf("PP")
    LRW = A.alloc([128, 512])
    LRA = A.alloc([128, 512])
    if has_vmix:
        VDN = A.alloc([128, 8, 32])
        VUP = A.alloc([32, 512])
        VD = A.alloc([32, TB])
        bVD = Buf("VD")
    Ssl = [A.alloc([128, 4, TB]) for _ in range(7)]
    S1, S2, S3, S4, S5, S6, S7 = Ssl
    b1, b2, b3, b4, b5, b6, b7 = [tok2("S%d" % i) for i in range(7)]
    GC = A.alloc([128, 4, NCH]); bGC = tok2("GC")
    H1, H2, H3 = [A.alloc([128, 4, TB]) for _ in range(3)]
    bh1, bh2, bh3 = [tok2("H%d" % i) for i in range(3)]
    BM = A.alloc([128, 4, NCH]); BL = A.alloc([128, 4, NCH])
    EM = A.alloc([128, 4, NCH]); EL = A.alloc([128, 4, NCH]); ELM = A.alloc([128, 4, NCH])
    bBM = tok2("BM")
    SBmD = [[alloc_bf(A, [64, 4, 2, 64]) for _ in range(2)] for _ in range(2)]
    SKmD = [[alloc_bf(A, [64, 4, 2, 64]) for _ in range(2)] for _ in range(2)]
    tSBD = [[[Buf("SBm%d%d" % (q, i))] for i in range(2)] for q in range(2)]
    tSKD = [[[Buf("SKm%d%d" % (q, i))] for i in range(2)] for q in range(2)]
    PTD = [alloc_bf(A, [64, 8, 64]) for _ in range(2)]
    bPTD = [Buf("PTD0"), Buf("PTD1")]
    NA = [alloc_bf(A, [64, 8, 64]) for _ in range(2)]
    NTP = [alloc_bf(A, [64, 8, 2, 64]) for _ in range(2)]
    bNA = [tok2("NA%d" % i) for i in range(2)]
    bNAT = [tok2("NAT%d" % i) for i in range(2)]
    bPT = tok2("PT")
    Xs = alloc_bf(A, [64, 8, 64]); bXs = tok2("Xs")
    Us = alloc_bf(A, [64, 4, 2, 64]); bUs = tok2("Us")
    VT = alloc_bf(A, [64, 4, 128]); bVT = tok2("VT")
    BST = alloc_bf(A, [64, 4, 128]); bBST = tok2("BST")
    KST = alloc_bf(A, [64, 4, 128]); bKST = tok2("KST")
    HB = [A.alloc([128, 4, 128]) for _ in range(2)]
    bHB = [tok2("HB%d" % i) for i in range(2)]
    HBb = [alloc_bf(A, [128, 4, 128]) for _ in range(2)]
    bHBb = [tok2("HBb%d" % i) for i in range(2)]
    ARb = alloc_bf(A, [128, 4, 2, TB])
    ATb = ARb[:, :, 0, :]; bATb = tok2("ATb")
    RTb = ARb[:, :, 1, :]; bRTb = tok2("RTb")
    BTb = alloc_bf(A, [128, 4, TB]); bBTb = tok2("BTb")
    KTb = alloc_bf(A, [128, 4, TB]); bKTb = tok2("KTb")
    Qb = alloc_bf(A, [128, 4, TB]); bQb = tok2("Qb")
    Fb = alloc_bf(A, [128, 4, TB]); bFb = tok2("Fb")
    HD = A.alloc([128, 4, 64]); bHD = tok2("HD")
    OTR = A.alloc([128, 4, TB]); bOTR = tok2("OTR")
    OTH = A.alloc([128, 4, TB]); bOTH = tok2("OTH")
    ATT = alloc_bf(A, [64, 4, 64]); bATT = tok2("ATT")
    KTK = alloc_bf(A, [64, 4, 128]); bKTK = tok2("KTK")
    SH = A.alloc([128, 4, 128]); bSH = tok2("SH")
    SM = alloc_bf(A, [128, 4, 128]); bSM = tok2("SM")
    SD = A.alloc([128, 4, 128]); bSD = tok2("SD")
    OTB = alloc_bf(A, [128, 8, TB])
    bOTB = [Buf("OTB%d" % i) for i in range(2 * NG)]
    bVF = Buf("vf_dram")
    bWS = [Buf("wscr%d" % i) for i in range(NFM)]

    S.dma("sp", lambda h: h.dma_start(out=PP, in_=W["pp"]), (), [bPP])
    S.dma("sp", lambda h: h.dma_start(out=LRW, in_=W["lrw"]), (), [bPP])
    S.dma("sp", lambda h: h.dma_start(out=LRA, in_=W["lra"]), (), [bPP])
    if has_vmix:
        S.dma("sp", lambda h: h.dma_start(out=VDN, in_=W["vdn"].rearrange("p (c r) -> p c r", c=8)), (), [bPP])
        S.dma("sp", lambda h: h.dma_start(out=VUP, in_=W["vup"]), (), [bPP])
    pq = [bPP]
    ts("pool", PD[:, 0:4], PP[:, cW0:cW0 + 4], 0.5, ALU.mult, pq, pq)
    ts("pool", PD[:, 4:8], PP[:, cA0:cA0 + 4], 0.5, ALU.mult, pq, pq)
    ts("pool", PD[:, 8:12], PP[:, cKA:cKA + 4], -1.0, ALU.mult, pq, pq, 1.0, ALU.add)
    ts("pool", PD[:, 12:16], PP[:, cVM0:cVM0 + 4], 0.5, ALU.mult, pq, pq)
    tt("dve", PD[:, 28:32], PP[:, cLB0:cLB0 + 4], PP[:, cLB1:cLB1 + 4], ALU.max, pq, pq)
    tt("pool", PD[:, 16:20], PP[:, cLB0:cLB0 + 4], PD[:, 28:32], ALU.subtract, pq, pq)
    tt("pool", PD[:, 20:24], PP[:, cLB1:cLB1 + 4], PD[:, 28:32], ALU.subtract, pq, pq)
    act(PD[:, 16:20], PD[:, 16:20], AF.Exp, pq, pq)
    act(PD[:, 20:24], PD[:, 20:24], AF.Exp, pq, pq)
    tt("dve", PD[:, 28:32], PD[:, 16:20], PD[:, 20:24], ALU.add, pq, pq)
    S.op("dve", lambda h: h.reciprocal(PD[:, 28:32], PD[:, 28:32]), pq, pq)
    tt("dve", PD[:, 16:20], PD[:, 16:20], PD[:, 28:32], ALU.mult, pq, pq)
    tt("dve", PD[:, 20:24], PD[:, 20:24], PD[:, 28:32], ALU.mult, pq, pq)
    if li == 0:
        tt("dve", PD[:, 20:24], PD[:, 16:20], PD[:, 16:20], ALU.subtract, pq, pq)
        cp("dve", PD[:, 16:20], PD[:, 20:24], pq, pq)
    else:
        tt("dve", PD[:, 20:24], PD[:, 16:20], PD[:, 20:24], ALU.add, pq, pq)
        tt("dve", PD[:, 16:20], PD[:, 20:24], PD[:, 16:20], ALU.subtract, pq, pq)
    ts("dve", PD[:, 20:24], PD[:, 16:20], -1.0, ALU.mult, pq, pq, 1.0, ALU.add)
    ts("dve", PD[:, 24:28], PD[:, 16:20], 1e-30, ALU.max, pq, pq)
    def G_wib():
        for q in range(4):
            sl = q % 2
            S.dma("sp", lambda h, sl=sl, q=q: h.dma_start(out=stg[sl], in_=W["wi"][:, q * 2048:(q + 1) * 2048]), (), [bstg[sl]])
            cp("dve" if q % 2 == 0 else "act", WIB[:, 4 * q:4 * q + 4, :], stg[sl].rearrange("p (k f) -> p k f", k=4), [bstg[sl]], [bWIB])
            yield

    ms("pool", LAST, 0.0, [bLAST])
    for i in range(2):
        ms("pool", HB[i], 0.0, bHB[i])
        ms("pool", HBb[i], 0.0, bHBb[i])
    ms("pool", SH, 0.0, bSH)
    hcur = [0, 0]
    inrr = [0]

    def in_slot():
        i = inrr[0] % 2
        inrr[0] += 1
        return 6 + i, 0

    def pcolh(c0, h):
        return bc(PP[:, c0 + PG * h:c0 + PG * h + PG], TB)

    def dcolh(c0, h):
        return bc(PD[:, c0 + PG * h:c0 + PG * h + PG], TB)

    def v4(x):
        return x.rearrange("p a (c k) -> p a c k", k=CH)

    def PRc(a, n):
        return PR[:, a:a + n, 1:1 + TB]

    def load_w(tb, fc):
        wsl = fc % NWB
        if tb == 0:
            sl = fc % 2
            S.dma("sp", lambda h, sl=sl, fc=fc: h.dma_start(out=stg[sl], in_=W["wfm"][fc]), (), [bstg[sl]])
            cp("dve" if fc % 2 == 0 else "act", WB[wsl], stg[sl].rearrange("p (k f) -> p k f", k=KC), [bstg[sl]], [bWB[wsl]])
            S.dma("pool", lambda h, wsl=wsl, fc=fc: h.dma_start(out=W["wscr"][fc], in_=WB[wsl].rearrange("p k f -> p (k f)")),
                  [bWB[wsl]], [bWS[fc]])
        else:
            S.dma("sp", lambda h, wsl=wsl, fc=fc: h.dma_start(out=WB[wsl].rearrange("p k f -> p (k f)"), in_=W["wscr"][fc]),
                  [bWS[fc]], [bWB[wsl]])
        return wsl

    def G_proj(tb, fcs):
        for fc in fcs:
            wsl = load_w(tb, fc)
            bk, hh = in_slot()
            o = pv(bk, hh)
            for kc in range(KC):
                mm(o, WB[wsl][:, kc, :], HTB[:, kc, :], kc == 0, kc == KC - 1, [bWB[wsl], bHTB], [BK[bk]])
            cp("act", PR[:, fc, 1:1 + TB], o, [BK[bk]], [bPR[fc]])
            yield

    def G_ht(tb):
        t0 = tb * TB
        for j in range(TB // 128):
            sl = j % 2
            S.dma("sp", lambda h, sl=sl, j=j, t0=t0: h.dma_start(out=stg[sl], in_=hin[t0 + j * 128:t0 + (j + 1) * 128, :]), (), [bstg[sl]])
            for q in range(4):
                bk = 6 + (q % 2)
                for i in range(4):
                    kc = 4 * q + i
                    tr(PS[bk][:, i * 128:(i + 1) * 128], stg[sl][:, kc * 128:(kc + 1) * 128], K.ident, [bstg[sl], kb], [BK[bk]])
                cp("act" if q % 2 == 0 else "dve", HTB[:, 4 * q:4 * q + 4, j * 128:(j + 1) * 128],
                   PS[bk].rearrange("p (k t) -> p k t", k=4), [BK[bk]], [bHTB])
                yield

    def G_itok(tb):
        for c in range(NCH):
            bk = 6 + (c % 2)
            for kc in range(KC):
                mm(PS[bk][0:64, :], HTB[:, kc, c * CH:(c + 1) * CH], WIB[:, kc, :], kc == 0, kc == KC - 1, [bHTB, bWIB], [BK[bk]])
            cp("act" if c % 2 == 0 else "dve", ITOK[:, c, :], PS[bk][0:64, :], [BK[bk]], [bITOK])
            yield

    bLASTe, bLASTl = Buf("LASTe"), Buf("LASTl")

    def _shift_group(a, n, tmp, btmp):
        cur = PR[:, a:a + n, 1:1 + TB]
        prv = PR[:, a:a + n, 0:TB]
        tt("dve", tmp[:, 0:n, :], prv, cur, ALU.subtract, bPR[a:a + n], btmp)
        tt("pool", tmp[:, 0:n, :], tmp[:, 0:n, :], bc(PP[:, cMU + a:cMU + a + n], TB), ALU.mult, btmp + [bPP], btmp)
        tt("dve", cur, cur, tmp[:, 0:n, :], ALU.add, bPR[a:a + n] + btmp, bPR[a:a + n])

    def G_common_early(tb):
        for (lo, hi) in ((0, 8), (16, NS)):
            cp("pool", PR[:, lo:hi, 0], LAST[:, lo:hi], [bLASTe], bPR[lo:hi])
            cp("pool", LAST[:, lo:hi], PR[:, lo:hi, TB], bPR[lo:hi], [bLASTe])
        yield
        groups = [(0, 4), (4, 4), (16, 1)] + ([(17, 4)] if has_vmix else [])
        for gi, (a, n) in enumerate(groups):
            _shift_group(a, n, S4 if gi % 2 == 0 else S6, list(b4) if gi % 2 == 0 else list(b6))
            yield
        act(PR[0:64, WA, 1:1 + TB], PR[0:64, WA, 1:1 + TB], AF.Tanh, [bPR[WA]], [bPR[WA]])
        yield

    def G_common_late(tb):
        t0 = tb * TB
        cp("pool", PR[:, 8:16, 0], LAST[:, 8:16], [bLASTl], bPR[8:16])
        cp("pool", LAST[:, 8:16], PR[:, 8:16, TB], bPR[8:16], [bLASTl])
        yield
        for (a, n) in ((8, 4), (12, 4)):
            _shift_group(a, n, OTR, list(bOTR))
            yield
        if has_vmix:
            bk, hh = in_slot()
            o = pv(bk, hh)
            for c8 in range(8):
                fcv = V0 + c8 if c8 < 4 else VO0 + c8 - 4
                mm(o[0:32, :], VDN[:, c8, :], PR[:, fcv, 1:1 + TB], c8 == 0, c8 == 7, [bPP, bPR[fcv]], [BK[bk]])
            cp("act", VD, o[0:32, :], [BK[bk]], [bVD])
            yield
        else:
            S.dma("pool", lambda h, t0=t0: h.dma_start(out=vf[:, :, t0:t0 + TB].rearrange("c p t -> p c t"), in_=PRc(V0, 4)),
                  bPR[V0:V0 + 4], [bVF], is_output=True)

    def G_prep_rw(tb, h):
        t0 = tb * TB
        p0 = PG * h
        pr2 = range(p0, p0 + PG)
        sl2 = slice(p0, p0 + PG)
        Rr, Kk, Vv, Zz = (PR[:, a + p0:a + p0 + PG, 1:1 + TB] for a in (R0, K0, V0, Z0))
        bR, bK, bV = bPR[R0 + p0:R0 + p0 + PG], bPR[K0 + p0:K0 + p0 + PG], bPR[V0 + p0:V0 + p0 + PG]
        s1, s2, s3, s4, s5, s6, s7 = (x[:, sl2, :] for x in Ssl)
        q1, q2, q3, q4, q5, q6, q7 = ([x[h]] for x in (b1, b2, b3, b4, b5, b6, b7))
        for p in pr2:
            bk, hh = in_slot()
            mm(pv(bk, hh), LRW[:, p * 128:(p + 1) * 128], PR[:, WA, 1:1 + TB], True, True, [bPP, bPR[WA]], [BK[bk]])
            act(S1[:, p, :], pv(bk, hh), AF.Tanh, [BK[bk], bPP], q1, bias=PD[:, p:p + 1], scale=0.5)
            bk, hh = in_slot()
            mm(pv(bk, hh), LRA[:, p * 128:(p + 1) * 128], PR[:, WA, 1:1 + TB], True, True, [bPP, bPR[WA]], [BK[bk]])
            act(S2[:, p, :], pv(bk, hh), AF.Tanh, [BK[bk], bPP], q2, bias=PD[:, 4 + p:5 + p], scale=0.5)
            yield
        ts("pool", s1, s1, 0.5 * LOGW_SCALE, ALU.mult, q1, q1, 0.5 * LOGW_SCALE, ALU.add)
        ts("pool", s2, s2, 0.5, ALU.mult, q2, q2, 0.5, ALU.add)
        yield
        for p in pr2:
            S.op("dve", lambda hd, p=p: hd.tensor_tensor_scan(out=S5[:, p, :], data0=K.rst, data1=S1[:, p, :], initial=0.0,
                                                              op0=ALU.mult, op1=ALU.add), [kb] + q1, q5)
        tt("pool", s1, s5, s1, ALU.subtract, q5 + q1, q1)
        act(s1, s1, AF.Exp, q1, q1)
        yield
        tt("pool", s3, Kk, pcolh(cKK, h), ALU.mult, bK + [bPP], q3)
        act(s4, s3, AF.Square, q3, q4)
        yield
        for p in pr2:
            bk, hh = in_slot()
            mm(pv(bk, hh), K.obd1, S4[:, p, :], True, True, [kb] + q4, [BK[bk]])
            act(S6[:, p, :], pv(bk, hh), AF.Ln, [BK[bk], kb], q6, bias=K.eps_kk[:, 0:1])
        act(s6, s6, AF.Exp, q6, q6, scale=-0.5)
        tt("dve", s3, s3, s6, ALU.mult, q3 + q6, q3)
        yield
        stt("dve", ATb[:, sl2, :], s3, -1.0, s1, ALU.mult, ALU.mult, q3 + q1, [bATb[h]])
        act(s6, s5, AF.Exp, q5, q6)
        tt("dve", RTb[:, sl2, :], Rr, s6, ALU.mult, bR + q6, [bRTb[h]])
        cp("act", GC[:, sl2, :], v4(s6)[:, :, :, CH - 1], q6, [bGC[h]])
        yield
        tt("pool", s4, s2, pcolh(cKA, h), ALU.mult, q2 + [bPP], q4)
        tt("pool", s4, s4, dcolh(8, h), ALU.add, q4 + [bPP], q4)
        tt("dve", Kk, Kk, s4, ALU.mult, bK + q4, bK)
        yield
        tt("pool", s4, Rr, Kk, ALU.mult, bR + bK, q4)
        tt("pool", s4, s4, pcolh(cRK, h), ALU.mult, q4 + [bPP], q4)
        yield
        for p in pr2:
            bk, hh = in_slot()
            mm(pv(bk, hh), K.obd1, S4[:, p, :], True, True, [kb] + q4, [BK[bk]])
            cp("dve", S7[:, p, :], pv(bk, hh), [BK[bk]], q7)
        yield
        act(s6, s5, AF.Exp, q5, q6, scale=-1.0)
        tt("pool", s2, s3, s2, ALU.mult, q3 + q2, q2)
        tt("dve", s3, s2, s6, ALU.mult, q2 + q6, q3)
        tt("dve", Kk, Kk, s6, ALU.mult, bK + q6, bK)
        yield
        cp("act", BTb[:, sl2, :], s3, q3, [bBTb[h]])
        cp("act", KTb[:, sl2, :], Kk, bK, [bKTb[h]])
        gcb = GC[:, sl2, :].unsqueeze(3).to_broadcast([128, PG, NCH, CH])
        tt("pool", v4(s2), v4(s3), gcb, ALU.mult, q3 + [bGC[h]], q2)
        tt("dve", v4(s5), v4(Kk), gcb, ALU.mult, bK + [bGC[h]], q5)
        yield

    def both(x):
        return list(x)

    def G_prep_rw2(tb, h):
        t0 = tb * TB
        p0 = PG * h
        pr2 = range(p0, p0 + PG)
        sl2 = slice(p0, p0 + PG)
        Vv = PR[:, V0 + p0:V0 + p0 + PG, 1:1 + TB]
        bV = bPR[V0 + p0:V0 + p0 + PG]
        s3, s4, s7 = S3[:, sl2, :], S4[:, sl2, :], S7[:, sl2, :]
        q3, q4, q7 = [b3[h]], [b4[h]], [b7[h]]
        if has_vmix:
            for p in pr2:
                bk, hh = in_slot()
                mm(pv(bk, hh), VUP[:, p * 128:(p + 1) * 128], VD, True, True, [bPP, bVD], [BK[bk]])
                act(S3[:, p, :], pv(bk, hh), AF.Tanh, [BK[bk], bPP], q3, bias=PD[:, 12 + p:13 + p], scale=0.5)
            ts("pool", s3, s3, 0.5, ALU.mult, q3, q3, 0.5, ALU.add)
            S.dma("pool", lambda hd, t0=t0: hd.dma_start(out=s4, in_=vf[p0:p0 + PG, :, t0:t0 + TB].rearrange("c p t -> p c t")), [bVF], q4)
            yield
            tt("dve", s4, s4, Vv, ALU.subtract, q4 + bV, q4)
            tt("pool", s4, s4, s3, ALU.mult, q4 + q3, q4)
            tt("dve", Vv, Vv, s4, ALU.add, bV + q4, bV)
            yield
        tt("dve", s7, s7, Vv, ALU.mult, q7 + bV, q7)
        yield

    def G_chunks_rw(tb):
        bV = bPR[V0:V0 + 4]
        m2b = K.mask2.unsqueeze(1).to_broadcast([64, 4, 2, 64])
        m3b = K.mask3.unsqueeze(1).to_broadcast([64, 4, 64])
        idb = K.ident[0:64, 0:64].unsqueeze(1).to_broadcast([64, 8, 64])
        tATb, tRTb, tBTb, tKTb = both(bATb), both(bRTb), both(bBTb), both(bKTb)
        tNA = [both(bNA[0]), both(bNA[1])]
        tNAT = [both(bNAT[0]), both(bNAT[1])]
        tXs, tUs, tVT, tBST, tKST, tHD, tOTR, tGC = (both(x) for x in (bXs, bUs, bVT, bBST, bKST, bHD, bOTR, bGC))
        tHB = [both(bHB[0]), both(bHB[1])]
        tHBb = [both(bHBb[0]), both(bHBb[1])]
        nch = NCH

        def pre(c):
            cs = slice(c * CH, (c + 1) * CH)
            par = c % 2
            SB_, SK_ = SBmD[par], SKmD[par]
            tSB_, tSK_ = tSBD[par], tSKD[par]
            NAv = NA[0].rearrange("p (a x) t -> p a x t", x=2)
            NTPv = NTP[0].rearrange("p (a x) y t -> p a x y t", x=2)
            for e in range(2):
                rows = slice(64 * e, 64 * e + 64)
                vB = PS[0].rearrange("p (a x t) -> p a x t", a=4, x=2)
                vK = PS[1].rearrange("p (a x t) -> p a x t", a=4, x=2)
                vL = PS[2].rearrange("p (a t) -> p a t", a=8)
                for p in range(4):
                    mm(vB[0:64, p, :, :], BTb[rows, p, cs], ARb[rows, p, :, cs], True, True, tBTb + tATb + tRTb, [BK[0]])
                    mm(vK[0:64, p, :, :], KTb[rows, p, cs], ARb[rows, p, :, cs], True, True, tKTb + tATb + tRTb, [BK[1]])
                    mm(vL[0:64, p, :], ATb[rows, p, cs], BTb[rows, p, cs], True, True, tBTb + tATb, [BK[2]])
                tt("dve", SB_[e], vB[0:64], m2b, ALU.mult, [BK[0], kb], tSB_[e])
                tt("dve", SK_[e], vK[0:64], m2b, ALU.mult, [BK[1], kb], tSK_[e])
                tt("dve", NAv[:, :, e, :], vL[0:64, 0:4, :], m3b, ALU.mult, [BK[2], kb], tNA[0])
                cp("pool", NTPv[:, :, e, 0, :], SB_[e][:, :, 0, :], tSB_[e], tNAT[0])
                yield
            tt("pool", NTP[0][:, :, 1, :], NTP[0][:, :, 0, :], idb, ALU.add, tNAT[0] + [kb], tNAT[0])
            cur = 0
            for step in range(1, 7):
                nx = 1 - cur
                if step <= 5:
                    vN = PS[0].rearrange("p (h t) -> p h t", h=8)
                    for h8 in range(8):
                        mm(vN[0:64, h8, :], NTP[cur][:, h8, 0, :], NA[cur][:, h8, :], True, True, tNAT[cur] + tNA[cur], [BK[0]])
                    cp("act", NA[nx], vN[0:64], [BK[0]], tNA[nx])
                if step == 1:
                    vT1 = PS[1].rearrange("p (h t) -> p h t", h=8)
                    for h8 in range(8):
                        mm(vT1[0:64, h8, :], NA[cur][:, h8, :], NTP[cur][:, h8, 0, :], True, True, tNAT[cur] + tNA[cur], [BK[1]])
                    cp("dve", NTP[nx][:, :, 0, :], vT1[0:64], [BK[1]], tNAT[nx])
                    cp("pool", NTP[nx][:, :, 1, :], NTP[cur][:, :, 1, :], tNAT[cur], tNAT[nx])
                elif step <= 4:
                    for half in range(2):
                        bkk = 1 + half
                        vBC = PS[bkk].rearrange("p (h y t) -> p h y t", h=4, y=2)
                        hs4 = slice(4 * half, 4 * half + 4)
                        for j in range(4):
                            h8 = 4 * half + j
                            mm(vBC[0:64, j, :, :], NA[cur][:, h8, :], NTP[cur][:, h8, :, :], True, True, tNAT[cur] + tNA[cur], [BK[bkk]])
                        cp("act" if half else "dve", NTP[nx][:, hs4, 0, :], vBC[0:64, :, 0, :], [BK[bkk]], tNAT[nx])
                        tt("dve", NTP[nx][:, hs4, 1, :], NTP[cur][:, hs4, 1, :], vBC[0:64, :, 1, :], ALU.add, tNAT[cur] + [BK[bkk]], tNAT[nx])
                elif step == 5:
                    vC = PS[1].rearrange("p (h t) -> p h t", h=8)
                    for h8 in range(8):
                        mm(vC[0:64, h8, :], NA[cur][:, h8, :], NTP[cur][:, h8, 1, :], True, True, tNAT[cur] + tNA[cur], [BK[1]])
                    tt("dve", NTP[nx][:, :, 1, :], NTP[cur][:, :, 1, :], vC[0:64], ALU.add, tNAT[cur] + [BK[1]], tNAT[nx])
                else:
                    vC = PS[1].rearrange("p (h t) -> p h t", h=8)
                    for h8 in range(8):
                        mm(vC[0:64, h8, :], NA[cur][:, h8, :], NTP[cur][:, h8, 1, :], True, True, tNAT[cur] + tNA[cur], [BK[1]])
                    tt("dve", PTD[par], NTP[cur][:, :, 1, :], vC[0:64], ALU.add, tNAT[cur] + [BK[1]], [bPTD[par]])
                cur = nx
                yield

        def chain(c):
            cs = slice(c * CH, (c + 1) * CH)
            par = c % 2
            SB_, SK_ = SBmD[par], SKmD[par]
            tSB_, tSK_ = tSBD[par], tSKD[par]
            PTf = PTD[par]
            tPT = [bPTD[par]]
            for (src, bsrc, dst, bdst, bk, eng) in ((PRc(V0, 4), bV, VT, tVT, 3, "act"), (S2, both(b2), BST, tBST, 4, "dve"),
                                                     (S5, both(b5), KST, tKST, 5, "act")):
                vT = PS[bk].rearrange("p (a f) -> p a f", a=4)
                for p in range(4):
                    tr(vT[0:64, p, :], src[:, p, cs], K.ident, bsrc + [kb], [BK[bk]])
                cp(eng, dst, vT[0:64], [BK[bk]], bdst)
                yield
            hc = hcur[0]
            vX = PS[3].rearrange("p (h t) -> p h t", h=8)
            for h8 in range(8):
                p, e = h8 // 2, h8 % 2
                mm(vX[0:64, h8, :], ATb[:, p, cs], HBb[hc][:, p, 64 * e:64 * e + 64], True, False, tATb + tHBb[hc], [BK[3]])
                mm(vX[0:64, h8, :], SK_[e][:, p, 0, :], VT[:, p, 64 * e:64 * e + 64], False, True, tSK_[e] + tVT, [BK[3]])
            cp("act", Xs, vX[0:64], [BK[3]], tXs)
            yield
            vU = PS[4].rearrange("p (h t) -> p h t", h=8)
            for h8 in range(8):
                mm(vU[0:64, h8, :], PTf[:, h8, :], Xs[:, h8, :], True, True, tPT + tXs, [BK[4]])
            cp("dve", Us.rearrange("p a x t -> p (a x) t"), vU[0:64], [BK[4]], tUs)
            yield
            vH = PS[5].rearrange("p (a f) -> p a f", a=4)
            for p in range(4):
                mm(vH[:, p, :], BST[:, p, :], Us[:, p, :, :].rearrange("p x t -> p (x t)"), True, False, tBST + tUs, [BK[5]])
                mm(vH[:, p, :], KST[:, p, :], VT[:, p, :], False, True, tKST + tVT, [BK[5]])
            hn = 1 - hc
            for e in range(2):
                rows = slice(64 * e, 64 * e + 64)
                cols = slice(64 * e, 64 * e + 64)
                tt("pool", HD[rows], HB[hc][rows, :, cols], GC[rows, :, c:c + 1].to_broadcast([64, 4, 64]), ALU.mult,
                   tHB[hc] + tGC, tHD)
                tt("dve", HBb[hn][rows, :, cols], HD[rows], vH[rows, :, cols], ALU.add, tHD + [BK[5]], tHBb[hn])
                tt("dve", HB[hn][rows, :, cols], HD[rows], vH[rows, :, cols], ALU.add, tHD + [BK[5]], tHB[hn])
            yield
            vO = PS[3].rearrange("p (a t) -> p a t", a=8)
            for h8 in range(8):
                p, e = h8 // 2, h8 % 2
                o_ap = vO[64 * e:64 * e + 64, p, :]
                mm(o_ap, HBb[hc][:, p, 64 * e:64 * e + 64], RTb[:, p, cs], True, False, tHBb[hc] + tRTb, [BK[3]])
                mm(o_ap, Us[:, p, e, :], SB_[e][:, p, 1, :], False, False, tUs + tSB_[e], [BK[3]])
                mm(o_ap, VT[:, p, 64 * e:64 * e + 64], SK_[e][:, p, 1, :], False, True, tVT + tSK_[e], [BK[3]])
            cp("act", OTR[:, :, cs], vO[:, 0:4, :], [BK[3]], tOTR)
            hcur[0] = hn
            yield

        if nch:
            yield from pre(0)
            for c in range(nch):
                nxt = pre(c + 1) if c + 1 < nch else None
                yield from _interleave_gen([chain(c), nxt])

    def G_post_rw(tb, h):
        p0 = PG * h
        pr2 = range(p0, p0 + PG)
        sl2 = slice(p0, p0 + PG)
        Zz = PR[:, Z0 + p0:Z0 + p0 + PG, 1:1 + TB]
        bZ = bPR[Z0 + p0:Z0 + p0 + PG]
        s4, s6, s7 = S4[:, sl2, :], S6[:, sl2, :], S7[:, sl2, :]
        q4, q6, q7 = [b4[h]], [b6[h]], [b7[h]]
        bk = 4 + (h % 2)
        sl_ = (h // 2) % 2
        for p in pr2:
            mm(pv(bk, (sl_ + p - p0) % 2), K.obd64, OTR[:, p, :], True, True, [kb, bOTR[h]], [BK[bk]])
            tt("dve", S4[:, p, :], OTR[:, p, :], pv(bk, (sl_ + p - p0) % 2), ALU.subtract, [bOTR[h], BK[bk]], q4)
        act(s6, s4, AF.Square, q4, q6)
        yield
        for p in pr2:
            mm(pv(bk, (sl_ + p - p0) % 2), K.obd64, S6[:, p, :], True, True, [kb] + q6, [BK[bk]])
            act(S6[:, p, :], pv(bk, (sl_ + p - p0) % 2), AF.Ln, [BK[bk], kb], q6, bias=K.eps_gn[:, 0:1])
        act(s6, s6, AF.Exp, q6, q6, scale=-0.5)
        yield
        tt("dve", s4, s4, s6, ALU.mult, q4 + q6, q4)
        tt("pool", s4, s4, pcolh(cGNW, h), ALU.mult, q4 + [bPP], q4)
        tt("pool", s4, s4, pcolh(cGNB, h), ALU.add, q4 + [bPP], q4)
        tt("dve", s4, s4, s7, ALU.add, q4 + q7, q4)
        yield
        act(s6, Zz, AF.Tanh, bZ, q6, scale=0.5)
        ts("pool", s6, s6, 0.5, ALU.mult, q6, q6, 0.5, ALU.add)
        tt("pool", s6, s6, Zz, ALU.mult, q6 + bZ, q6)
        tt("dve", OTB[:, p0:p0 + PG, :], s4, s6, ALU.mult, q4 + q6, [bOTB[h]])
        yield

    def G_prep_hg(tb, h):
        p0 = PG * h
        pr2 = range(p0, p0 + PG)
        sl2 = slice(p0, p0 + PG)
        Qq, Ff = (PR[:, a + p0:a + p0 + PG, 1:1 + TB] for a in (Q0, F0))
        bQ, bF = bPR[Q0 + p0:Q0 + p0 + PG], bPR[F0 + p0:F0 + p0 + PG]
        g1, g2, g3 = H1[:, sl2, :], H2[:, sl2, :], H3[:, sl2, :]
        q1, q2, q3 = [bh1[h]], [bh2[h]], [bh3[h]]
        bm = [bBM[h]]
        act(g1, Qq, AF.Tanh, bQ, q1, scale=0.5)
        ts("pool", g1, g1, 0.5, ALU.mult, q1, q1, 0.5, ALU.add)
        tt("dve", Qq, Qq, g1, ALU.mult, bQ + q1, bQ)
        yield
        act(g1, Ff, AF.Tanh, bF, q1, scale=0.5)
        ts("pool", g1, g1, 0.5, ALU.mult, q1, q1, 0.5, ALU.add)
        tt("pool", g2, g1, dcolh(20, h), ALU.mult, q1 + [bPP], q2)
        tt("pool", g2, g2, dcolh(24, h), ALU.add, q2 + [bPP], q2)
        act(g2, g2, AF.Ln, q2, q2)
        yield
        ts("pool", Ff, g1, -1.0, ALU.mult, q1, bF, 1.0, ALU.add)
        tt("pool", Ff, Ff, dcolh(20, h), ALU.mult, bF + [bPP], bF)
        for p in pr2:
            S.op("dve", lambda hd, p=p: hd.tensor_tensor_scan(out=H3[:, p, :], data0=K.rst, data1=H2[:, p, :], initial=0.0,
                                                              op0=ALU.mult, op1=ALU.add), [kb] + q2, q3)
        yield
        H3v = v4(g3)
        cp("pool", BM[:, sl2, :], H3v[:, :, :, CH // 2 - 1], q3, bm)
        cp("pool", BL[:, sl2, :], H3v[:, :, :, CH - 1], q3, bm)
        bmb = BM[:, sl2, :].unsqueeze(3).to_broadcast([128, PG, NCH, CH])
        tt("pool", v4(g2), H3v, bmb, ALU.subtract, q3 + bm, q2)
        act(g1, g2, AF.Exp, q2, q1)
        tt("dve", Qb[:, sl2, :], Qq, g1, ALU.mult, bQ + q1, [bQb[h]])
        yield
        act(g1, g2, AF.Exp, q2, q1, scale=-1.0)
        tt("dve", Ff, Ff, g1, ALU.mult, bF + q1, bF)
        cp("act", Fb[:, sl2, :], Ff, bF, [bFb[h]])
        act(EM[:, sl2, :], BM[:, sl2, :], AF.Exp, bm, bm)
        act(EL[:, sl2, :], BL[:, sl2, :], AF.Exp, bm, bm)
        tt("pool", ELM[:, sl2, :], BL[:, sl2, :], BM[:, sl2, :], ALU.subtract, bm, bm)
        act(ELM[:, sl2, :], ELM[:, sl2, :], AF.Exp, bm, bm)
        yield

    def G_chunks_hg(tb):
        bF = bPR[F0:F0 + 4]
        tFb, tQb, tATT, tKTK, tSH, tSM, tSD, tOTH, tBM = (both(x) for x in (bFb, bQb, bATT, bKTK, bSH, bSM, bSD, bOTH, bBM))
        for c in range(NCH):
            cs = slice(c * CH, (c + 1) * CH)
            vAt = PS[0].rearrange("p (a t) -> p a t", a=8)
            for p in range(4):
                mm(vAt[0:64, p, :], Fb[:, p, cs], Qb[:, p, cs], True, True, tFb + tQb, [BK[0]])
            tt("dve", ATT, vAt[0:64, 0:4, :], K.mask2[:, 1, :].unsqueeze(1).to_broadcast([64, 4, 64]), ALU.mult, [BK[0], kb], tATT)
            vKt = PS[1].rearrange("p (a f) -> p a f", a=4)
            for p in range(4):
                tr(vKt[0:64, p, :], PR[:, F0 + p, 1 + c * CH:1 + (c + 1) * CH], K.ident, bF + [kb], [BK[1]])
            cp("act", KTK, vKt[0:64], [BK[1]], tKTK)
            tt("pool", SM, SH, EM[:, :, c:c + 1].to_broadcast([128, 4, 128]), ALU.mult, tSH + tBM, tSM)
            yield
            vOh = PS[2].rearrange("p (a t) -> p a t", a=8)
            for p in range(4):
                mm(vOh[:, p, :], SM[:, p, :], Qb[:, p, cs], True, False, tSM + tQb, [BK[2]])
                mm(vOh[:, p, :], ITOK[:, c, p * 128:(p + 1) * 128], ATT[:, p, :], False, True, [bITOK] + tATT, [BK[2]])
            cp("act", OTH[:, :, cs], vOh[:, 0:4, :], [BK[2]], tOTH)
            vS = PS[3].rearrange("p (a f) -> p a f", a=4)
            for p in range(4):
                mm(vS[:, p, :], KTK[:, p, :], ITOK[:, c, p * 128:(p + 1) * 128], True, True, tKTK + [bITOK], [BK[3]])
            tt("dve", SD, vS, ELM[:, :, c:c + 1].to_broadcast([128, 4, 128]), ALU.mult, [BK[3]] + tBM, tSD)
            tt("pool", SH, SH, EL[:, :, c:c + 1].to_broadcast([128, 4, 128]), ALU.mult, tSH + tBM, tSH)
            tt("pool", SH, SH, SD, ALU.add, tSH + tSD, tSH)
            yield

    def G_post_hg(tb, h):
        p0 = PG * h
        pr2 = range(p0, p0 + PG)
        sl2 = slice(p0, p0 + PG)
        Zh = PR[:, ZH0 + p0:ZH0 + p0 + PG, 1:1 + TB]
        bZ = bPR[ZH0 + p0:ZH0 + p0 + PG]
        g1, g2 = H1[:, sl2, :], H2[:, sl2, :]
        q1, q2 = [bh1[h]], [bh2[h]]
        oth = OTH[:, sl2, :]
        act(g1, oth, AF.Square, [bOTH[h]], q1)
        bk = 4 + (h % 2)
        sl_ = (h // 2) % 2
        for p in pr2:
            mm(pv(bk, (sl_ + p - p0) % 2), K.o128, H1[:, p, :], True, True, [kb] + q1, [BK[bk]])
            act(H1[:, p, :], pv(bk, (sl_ + p - p0) % 2), AF.Ln, [BK[bk], kb], q1, bias=K.eps_rms[:, 0:1])
        act(g1, g1, AF.Exp, q1, q1, scale=-0.5)
        yield
        tt("dve", oth, oth, g1, ALU.mult, [bOTH[h]] + q1, [bOTH[h]])
        tt("pool", oth, oth, pcolh(cGW, h), ALU.mult, [bOTH[h], bPP], [bOTH[h]])
        act(g2, Zh, AF.Tanh, bZ, q2, scale=0.5)
        ts("pool", g2, g2, 0.5, ALU.mult, q2, q2, 0.5, ALU.add)
        tt("pool", g2, g2, Zh, ALU.mult, q2 + bZ, q2)
        tt("dve", OTB[:, 4 + p0:4 + p0 + PG, :], oth, g2, ALU.mult, [bOTH[h]] + q2, [bOTB[NG + h]])
        yield

    def G_store(tb):
        t0 = tb * TB
        S.dma("pool", lambda hd, t0=t0: hd.dma_start(out=oT_out[:, :, t0:t0 + TB].rearrange("c p t -> p c t"), in_=OTB), bOTB, [],
              is_output=True)
        yield

    early = list(range(R0, R0 + 8)) + [WA] + (list(range(VO0, VO0 + 4)) if has_vmix else [])
    late = list(range(V0, V0 + 8))

    def rw_prep_stage(tb, other=None):
        _interleave([
            other,
            _chain(G_common_early(tb), _interleave_gen([G_prep_rw(tb, g) for g in range(NG)])),
            _chain(G_proj(tb, late), G_common_late(tb)),
        ], burst=[1, 1, 2] if li == 0 else [1, 1, 1])
        _drain(_interleave_gen([G_prep_rw2(tb, g) for g in range(NG)]))

    _drain(_chain(G_ht(0), G_proj(0, early)))
    rw_prep_stage(0)
    for tb in range(NB):
        more = tb + 1 < NB
        _interleave([
            _chain(G_chunks_rw(tb), _interleave_gen([G_post_rw(tb, g) for g in range(NG)])),
            _chain(G_proj(tb, range(NS, NFM)), G_wib() if tb == 0 else None, G_itok(tb),
                   _interleave_gen([G_prep_hg(tb, g) for g in range(NG)]),
                   G_ht(tb + 1) if more else None, G_proj(tb + 1, early) if more else None),
        ])
        hg = _chain(G_chunks_hg(tb), _interleave_gen([G_post_hg(tb, g) for g in range(NG)]))
        if more:
            rw_prep_stage(tb + 1, hg)
        else:
            _drain(hg)
        _drain(G_store(tb))


def _interleave_gen(gens):
    gens = [g for g in gens if g is not None]
    while gens:
        alive = []
        for g in gens:
            try:
                next(g)
                alive.append(g)
            except StopIteration:
                pass
        gens = alive
        yield


def _cols(g, has_vmix):
    fr = g * 512 + np.arange(512)
    fo = (1 - g) * 512 + np.arange(512)
    sh = [fr, 1024 + fr, 2048 + fr, 3072 + fr, 4096 + np.arange(128)]
    if has_vmix:
        sh.append(2048 + fo)
    shift_cols = np.concatenate(sh)
    base = 4224
    hg = np.concatenate([base + fr, base + 1024 + fr, base + 3072 + fr])
    icols = base + 2048 + fr
    return fr, fo, shift_cols, np.concatenate([shift_cols, hg]), icols


def pack_A(inp, l, g):
    has_vmix = l > 0
    fr, fo, shift_cols, fm_cols, icols = _cols(g, has_vmix)
    NS = len(shift_cols) // 128
    NFM = len(fm_cols) // 128
    w = np.asarray(inp["w_in"][l], np.float32)
    wfm = np.ascontiguousarray(w[:, fm_cols].reshape(KC, 128, NFM, 128).transpose(2, 1, 0, 3)).reshape(NFM, 128, KC * 128)
    wi = np.ascontiguousarray(w[:, icols].reshape(KC, 128, 512).transpose(1, 0, 2)).reshape(128, KC * 512)

    def c4(v):
        return np.asarray(v, np.float32).reshape(4, 128).T

    mu = np.asarray(inp["shift_mu"][l], np.float32)[shift_cols].reshape(NS, 128).T
    vm0 = inp["v_mix0"][0][fr] if has_vmix else np.zeros(512, np.float32)
    pp = np.concatenate([mu, c4(inp["w_decay0"][l][fr]), c4(inp["a0"][l][fr]), c4(inp["k_k"][l][fr]), c4(inp["k_a"][l][fr]),
                         c4(inp["r_k"][l][fr]), c4(inp["ln_x_w"][l][fr]), c4(inp["ln_x_b"][l][fr]), c4(vm0),
                         c4(inp["lb_logits"][0][fr]), c4(inp["lb_logits"][1][fr]), c4(inp["g_norm_w"][l][fr])], axis=1)
    lrw = np.zeros((128, 512), np.float32)
    lrw[0:64] = inp["w_decay_up"][l][:, fr]
    lra = np.zeros((128, 512), np.float32)
    lra[64:128] = inp["a_up"][l][:, fr]
    d = {"wfm": wfm, "wi": wi, "pp": np.ascontiguousarray(pp, dtype=np.float32), "lrw": lrw, "lra": lra}
    if has_vmix:
        vd = np.asarray(inp["v_mix_down"][0], np.float32)
        rows = np.concatenate([fr, fo])
        d["vdn"] = np.ascontiguousarray(vd[rows].reshape(8, 128, 32).transpose(1, 0, 2)).reshape(128, 256)
        d["vup"] = np.ascontiguousarray(np.asarray(inp["v_mix_up"][0], np.float32)[:, fr])
    return d


def decl_A(nc, tag, has_vmix, kind="ExternalInput"):
    NS = 21 if has_vmix else 17
    NFM = NS + 12
    W = {
        "wfm": nc.dram_tensor("wfm" + tag, [NFM, 128, KC * 128], F32, kind=kind).ap(),
        "wi": nc.dram_tensor("wi" + tag, [128, KC * 512], F32, kind=kind).ap(),
        "pp": nc.dram_tensor("pp" + tag, [128, NS + 44], F32, kind=kind).ap(),
        "lrw": nc.dram_tensor("lrw" + tag, [128, 512], F32, kind=kind).ap(),
        "lra": nc.dram_tensor("lra" + tag, [128, 512], F32, kind=kind).ap(),
        "wscr": nc.dram_tensor("wscr" + tag, [NFM, 128, KC * 128], BF16, kind="Internal").ap(),
    }
    if has_vmix:
        W["vdn"] = nc.dram_tensor("vdn" + tag, [128, 256], F32, kind=kind).ap()
        W["vup"] = nc.dram_tensor("vup" + tag, [32, 512], F32, kind=kind).ap()
    return W


def emit_B(C, oT_full, hin, tok0, ntok, W, hout, is_output):
    mm, tr, act, tt, ts, stt, cp, ms = _ops(C)
    S = C.S
    K = C.K
    A = C.arena
    A.off = C.persist_off
    S.barrier()
    PS, BPS = C.ps, C.bps
    kb = K.b
    stg = [A.alloc([128, 2048]) for _ in range(2)]
    bstg = [Buf("bstg%d" % i) for i in range(2)]
    WOB = A.alloc([128, KC * 2048 // 2]).bitcast(BF16).rearrange("p (k d) -> p k d", k=KC)
    bWOB = Buf("WOB")
    LNW = A.alloc([128, 2048]); LNB = A.alloc([128, 2048]); bLN = Buf("LN")
    OTt = [A.alloc([128, KC * 128 // 2]).bitcast(BF16).rearrange("p (c t) -> p c t", c=KC) for _ in range(2)]
    bOTt = [Buf("OTt%d" % i) for i in range(2)]
    Ht = [A.alloc([128, 2048]) for _ in range(2)]
    bHt = [Buf("Ht%d" % i) for i in range(2)]
    Zs = [A.alloc([128, 2048]) for _ in range(2)]; bZs = [Buf("Z0"), Buf("Z1")]
    JK = A.alloc([128, 2048]); bJK = Buf("JK")
    OUs = [A.alloc([128, 2048]) for _ in range(2)]; bOUs = [Buf("OU0"), Buf("OU1")]
    STs = [A.alloc([128, 8]) for _ in range(2)]; bSTs = [Buf("ST0"), Buf("ST1")]
    S.dma("sp", lambda h: h.dma_start(out=LNW, in_=W["lnw"]), (), [bLN])
    S.dma("sp", lambda h: h.dma_start(out=LNB, in_=W["lnb"]), (), [bLN])
    for kc in range(KC):
        sl = kc % 2
        S.dma("sp", lambda h, sl=sl, kc=kc: h.dma_start(out=stg[sl], in_=W["wout"][:, kc * 2048:(kc + 1) * 2048]), (), [bstg[sl]])
        cp("dve" if kc % 2 == 0 else "act", WOB[:, kc, :], stg[sl], [bstg[sl]], [bWOB])
    for i in range(ntok // 128):
        tk = tok0 + i * 128
        sl = i % 2
        Z, bZ, OU, bOU, ST, bST = Zs[sl], bZs[sl], OUs[sl], bOUs[sl], STs[sl], bSTs[sl]
        S.dma("sp", lambda h, sl=sl, tk=tk: h.dma_start(out=OTt[sl], in_=oT_full[:, :, tk:tk + 128].rearrange("c p t -> p c t")),
              W.get("oT_deps", ()), [bOTt[sl]])
        S.dma("sp", lambda h, sl=sl, tk=tk: h.dma_start(out=Ht[sl], in_=hin[tk:tk + 128, :]), W.get("h_deps", ()), [bHt[sl]])
        for n in range(4):
            bk = (4 * (i % 2)) + n
            for kc in range(KC):
                mm(PS[bk][:, :], OTt[sl][:, kc, :], WOB[:, kc, n * 512:(n + 1) * 512], kc == 0, kc == KC - 1, [bOTt[sl], bWOB], [BPS[bk]])
            stt("dve", Z[:, n * 512:(n + 1) * 512], Ht[sl][:, n * 512:(n + 1) * 512], ALPHA, PS[bk][:, :], ALU.mult, ALU.add,
                [bHt[sl], BPS[bk]], [bZ])
        S.op("act", lambda h, Z=Z, ST=ST: h.activation(out=JK, in_=Z, func=AF.Identity, accum_out=ST[:, 0:1]), [bZ], [bJK, bST])
        S.op("act", lambda h, Z=Z, ST=ST: h.activation(out=JK, in_=Z, func=AF.Square, accum_out=ST[:, 1:2]), [bZ], [bJK, bST])
        q = [bST]
        ts("dve", ST[:, 2:3], ST[:, 0:1], 1.0 / D, ALU.mult, q, q)
        tt("dve", ST[:, 3:4], ST[:, 2:3], ST[:, 2:3], ALU.mult, q, q)
        stt("dve", ST[:, 4:5], ST[:, 1:2], 1.0 / D, ST[:, 3:4], ALU.mult, ALU.subtract, q, q)
        act(ST[:, 5:6], ST[:, 4:5], AF.Ln, q + [kb], q, bias=K.eps_ln[:, 0:1])
        act(ST[:, 5:6], ST[:, 5:6], AF.Exp, q, q, scale=-0.5)
        S.op("dve", lambda h, Z=Z, ST=ST, OU=OU: h.tensor_scalar(OU, Z, ST[:, 2:3], ST[:, 5:6], ALU.subtract, ALU.mult), [bZ, bST], [bOU])
        tt("pool", OU, OU, LNW, ALU.mult, [bOU, bLN], [bOU])
        tt("dve", OU, OU, LNB, ALU.add, [bOU, bLN], [bOU])
        S.dma("pool", lambda h, i=i, OU=OU: h.dma_start(out=hout[i * 128:(i + 1) * 128, :], in_=OU), [bOU], W.get("out_bufs", []),
              is_output=is_output)


def pack_B(inp, l):
    rows = np.concatenate([np.arange(0, 512), 1024 + np.arange(0, 512), 512 + np.arange(0, 512), 1536 + np.arange(0, 512)])
    w = np.asarray(inp["w_out"][l], np.float32)[rows]
    wout = np.ascontiguousarray(w.reshape(KC, 128, D).transpose(1, 0, 2)).reshape(128, KC * D)
    lnw = np.ascontiguousarray(np.broadcast_to(np.asarray(inp["ln_w"][l], np.float32)[None, :], (128, D)))
    lnb = np.ascontiguousarray(np.broadcast_to(np.asarray(inp["ln_b"][l], np.float32)[None, :], (128, D)))
    return {"wout": wout, "lnw": lnw, "lnb": lnb}


def decl_B(nc, tag, kind="ExternalInput"):
    return {"wout": nc.dram_tensor("wout" + tag, [128, KC * D], F32, kind=kind).ap(),
            "lnw": nc.dram_tensor("lnw" + tag, [128, D], F32, kind=kind).ap(),
            "lnb": nc.dram_tensor("lnb" + tag, [128, D], F32, kind=kind).ap()}


def _build_A(has_vmix, li, T):
    nc = bass.Bass("TRN2", target_bir_lowering=False)
    with contextlib.ExitStack() as st:
        C = mk_ctx(nc, st)
        hin = nc.dram_tensor("hin", [T, D], F32, kind="ExternalInput").ap()
        W = decl_A(nc, "", has_vmix)
        oT = nc.dram_tensor("oT", [8, 128, T], BF16, kind="ExternalOutput").ap()
        vf = nc.dram_tensor("vf", [4, 128, T], F32, kind="ExternalInput" if has_vmix else "ExternalOutput").ap()
        emit_consts(C)
        emit_A(C, hin, W, oT, vf, has_vmix, li, T)
        C.S.emit(st)
    return nc


def _build_B(ntok):
    nc = bass.Bass("TRN2", target_bir_lowering=False)
    with contextlib.ExitStack() as st:
        C = mk_ctx(nc, st)
        hin = nc.dram_tensor("hin", [ntok, D], F32, kind="ExternalInput").ap()
        oT = nc.dram_tensor("oT", [16, 128, ntok], BF16, kind="ExternalInput").ap()
        W = decl_B(nc, "")
        hout = nc.dram_tensor("hout", [ntok, D], F32, kind="ExternalOutput").ap()
        emit_consts(C)
        emit_B(C, oT, hin, 0, ntok, W, hout, True)
        C.S.emit(st)
    return nc


def kernel_unfused(inp):
    x = np.asarray(inp["x"], np.float32)
    Bn, T, _ = x.shape
    cores = list(range(8))
    half = T // 2
    h = x
    vfs = None
    for l in range(DEPTH):
        ncA = _build_A(l > 0, l, T)
        packs = [pack_A(inp, l, g) for g in range(2)]
        maps = []
        for cid in cores:
            b, g = cid // 2, cid % 2
            m = dict(packs[g])
            m["hin"] = np.ascontiguousarray(h[b])
            if l > 0:
                m["vf"] = vfs[cid]
            maps.append(m)
        res = run_bass_kernel_spmd(ncA, maps, core_ids=cores).results
        if l == 0:
            vfs = [np.asarray(r["vf"]) for r in res]
        oTs = [np.asarray(r["oT"]) for r in res]
        ncB = _build_B(half)
        pb = pack_B(inp, l)
        maps = []
        for cid in cores:
            b, s = cid // 2, cid % 2
            m = dict(pb)
            full = np.concatenate([oTs[2 * b], oTs[2 * b + 1]], axis=0)
            m["oT"] = np.ascontiguousarray(full[:, :, s * half:(s + 1) * half])
            m["hin"] = np.ascontiguousarray(h[b, s * half:(s + 1) * half])
            maps.append(m)
        res = run_bass_kernel_spmd(ncB, maps, core_ids=cores).results
        hn = np.empty_like(x)
        for cid in cores:
            b, s = cid // 2, cid % 2
            hn[b, s * half:(s + 1) * half] = np.asarray(res[cid]["hout"])
        h = hn
    return h.astype(np.float32)


def _build_fused(T):
    nc = bass.Bass("TRN2", target_bir_lowering=False)
    with contextlib.ExitStack() as st:
        C = mk_ctx(nc, st)
        x = nc.dram_tensor("x", [T, D], F32, kind="ExternalInput").ap()
        WA = {(l, g): decl_A(nc, "_%d_%d" % (l, g), l > 0) for l in range(DEPTH) for g in range(2)}
        WB = {l: decl_B(nc, "_%d" % l) for l in range(DEPTH)}
        oT = nc.dram_tensor("oT_scr", [16, 128, T], BF16, kind="Internal").ap()
        vf = nc.dram_tensor("vf_scr", [2, 4, 128, T], F32, kind="Internal").ap()
        h1 = nc.dram_tensor("h1_scr", [T, D], F32, kind="Internal").ap()
        out = nc.dram_tensor("out", [T, D], F32, kind="ExternalOutput").ap()
        emit_consts(C)
        hin = x
        for l in range(DEPTH):
            for g in range(2):
                emit_A(C, hin, WA[(l, g)], oT[8 * g:8 * g + 8], vf[g], l > 0, l, T)
            last = l == DEPTH - 1
            emit_B(C, oT, hin, 0, T, WB[l], out if last else h1, last)
            hin = h1
        C.S.emit(st)
    return nc


def kernel_fused(inp):
    x = np.asarray(inp["x"], np.float32)
    Bn, T, _ = x.shape
    cores = list(range(8))
    nc = _build_fused(T)
    shared = {}
    for l in range(DEPTH):
        for g in range(2):
            for k, v in pack_A(inp, l, g).items():
                shared["%s_%d_%d" % (k, l, g)] = v
        for k, v in pack_B(inp, l).items():
            shared["%s_%d" % (k, l)] = v
    maps = []
    for cid in cores:
        m = dict(shared)
        m["x"] = np.ascontiguousarray(x[cid // 2])
        maps.append(m)
    res = run_bass_kernel_spmd(nc, maps, core_ids=cores).results
    half = T // 2
    out = np.empty_like(x)
    for cid in cores:
        b, s = cid // 2, cid % 2
        out[b, s * half:(s + 1) * half] = np.asarray(res[cid]["out"])[s * half:(s + 1) * half]
    return out.astype(np.float32)


def kernel(**inputs):
    return kernel_fused(inputs)
```

```python
import contextlib
import os
import numpy as np
import ml_dtypes
import concourse.bass as bass
import concourse.mybir as mybir
from concourse.bass_utils import run_bass_kernel_spmd

F32 = mybir.dt.float32
BF16 = mybir.dt.bfloat16
ALU = mybir.AluOpType
AF = mybir.ActivationFunctionType

D = 2048
KC = D // 128
SEQ = 2048
BATCH = 4
DEPTH = 2
TB = 256
CH = 64
NCH = TB // CH
ALPHA = (2 * DEPTH) ** 0.25
LN_EPS = 1e-5
GN_EPS = 64e-5
RMS_EPS = 1e-5
LOGW_SCALE = -float(np.exp(-0.5))


class Buf:
    __slots__ = ("name", "writers", "readers")

    def __init__(self, name=""):
        self.name = name
        self.writers = {}
        self.readers = {}


class Sched:
    ENGS = ("pe", "act", "dve", "pool", "sp")

    def __init__(self, nc, n_dma_sems=32):
        self.nc = nc
        self.cnt = {e: 0 for e in self.ENGS}
        self.seen = {e: {} for e in self.ENGS}
        self.prog = {e: [] for e in self.ENGS}
        self.n_dma_sems = n_dma_sems
        self.dma_val = [0] * n_dma_sems
        self.dma_rr = 0
        self.out_dma = []
        self.pending = {e: {} for e in self.ENGS}

    def _need(self, eng, deps, key, val):
        if val <= self.seen[eng].get(key, 0):
            return
        if deps.get(key, 0) < val:
            deps[key] = val

    def barrier(self):
        snap = {e: self.cnt[e] for e in self.ENGS if self.cnt[e] > 0}
        for s in range(self.n_dma_sems):
            if self.dma_val[s] > 0:
                snap[("dma", s)] = self.dma_val[s]
        for e in self.ENGS:
            for k, v in snap.items():
                if k == e:
                    continue
                if self.pending[e].get(k, 0) < v:
                    self.pending[e][k] = v

    def _take_pending(self, eng, deps):
        if self.pending[eng]:
            for k, v in self.pending[eng].items():
                self._need(eng, deps, k, v)
            self.pending[eng] = {}

    def op(self, eng, fn, reads=(), writes=()):
        deps = {}
        self._take_pending(eng, deps)
        for b in reads:
            for k, v in b.writers.items():
                if k == eng and eng == "pe":
                    continue
                self._need(eng, deps, k, v)
        strict = False
        for b in writes:
            for k, v in b.writers.items():
                if k == eng and not strict:
                    continue
                self._need(eng, deps, k, v)
            for k, v in b.readers.items():
                if k == eng and not strict:
                    continue
                self._need(eng, deps, k, v)
        for k, v in deps.items():
            self.seen[eng][k] = v
        self.cnt[eng] += 1
        idx = self.cnt[eng]
        self.prog[eng].append((list(deps.items()), fn, ("eng", eng)))
        for b in reads:
            b.readers[eng] = idx
        for b in writes:
            b.writers = {eng: idx}
            b.readers = {}
        return idx

    def dma(self, q, fn, reads=(), writes=(), is_output=False):
        deps = {}
        self._take_pending(q, deps)
        for b in reads:
            for k, v in b.writers.items():
                self._need(q, deps, k, v)
        for b in writes:
            for k, v in b.writers.items():
                self._need(q, deps, k, v)
            for k, v in b.readers.items():
                self._need(q, deps, k, v)
        s = self.dma_rr
        self.dma_rr = (self.dma_rr + 1) % self.n_dma_sems
        key = ("dma", s)
        if self.dma_val[s] > 0:
            self._need(q, deps, key, self.dma_val[s])
        for k, v in deps.items():
            self.seen[q][k] = v
        self.dma_val[s] += 16
        val = self.dma_val[s]
        self.prog[q].append((list(deps.items()), fn, ("dma", s)))
        for b in reads:
            b.readers[key] = val
        for b in writes:
            b.writers = {key: val}
            b.readers = {}
        if is_output:
            self.out_dma.append((key, val))
        return key, val

    def emit(self, st, final_eng="sp"):
        nc = self.nc
        esem = {e: st.enter_context(nc.semaphore("s_" + e)) for e in self.ENGS}
        dsem = [st.enter_context(nc.semaphore("d%d" % i)) for i in range(self.n_dma_sems)]

        def semof(key):
            if isinstance(key, tuple):
                return dsem[key[1]]
            return esem[key]

        fin = {}
        for k, v in self.out_dma:
            fin[k] = max(fin.get(k, 0), v)
        block = st.enter_context(nc.Block())
        hmap = {"pe": nc.tensor, "act": nc.scalar, "dve": nc.vector, "pool": nc.gpsimd, "sp": nc.sync}

        def mk(e):
            def body(h):
                for waits, fn, inc in self.prog[e]:
                    for k, v in waits:
                        h.wait_ge(semof(k), v)
                    ins = fn(h)
                    if inc[0] == "eng":
                        ins.then_inc(esem[inc[1]], 1)
                    else:
                        ins.then_inc(dsem[inc[1]], 16)
                if e == final_eng:
                    for k, v in fin.items():
                        h.wait_ge(semof(k), v)
            return body
        block.tensor(mk("pe"))
        block.scalar(mk("act"))
        block.vector(mk("dve"))
        block.gpsimd(mk("pool"))
        block.sync(mk("sp"))


class Arena:
    def __init__(self, tensor, n):
        self.t = tensor
        self.n = n
        self.off = 0

    def reset(self):
        self.off = 0

    def alloc(self, shape):
        free = int(np.prod(shape[1:]))
        free_al = (free + 1) // 2 * 2
        assert self.off + free_al <= self.n, ("arena overflow", self.off, free_al, self.n)
        ap = self.t[0:shape[0], self.off:self.off + free]
        self.off += free_al
        if len(shape) == 2:
            return ap
        names = " ".join("d%d" % i for i in range(len(shape) - 1))
        kw = {"d%d" % i: shape[i + 1] for i in range(len(shape) - 1)}
        return ap.rearrange("p (%s) -> p %s" % (names, names), **kw)


class Ctx:
    pass


def alloc_bf(A, shape):
    free = int(np.prod(shape[1:]))
    ap = A.alloc([shape[0], (free + 1) // 2]).bitcast(BF16)[:, 0:free]
    if len(shape) == 2:
        return ap
    names = " ".join("d%d" % i for i in range(len(shape) - 1))
    kw = {"d%d" % i: shape[i + 1] for i in range(len(shape) - 1)}
    return ap.rearrange("p (%s) -> p %s" % (names, names), **kw)


def bc(ap_col, n):
    return ap_col.unsqueeze(len(ap_col.shape)).to_broadcast(list(ap_col.shape) + [n])


def mk_ctx(nc, st):
    C = Ctx()
    C.nc = nc
    C.S = Sched(nc)
    NF = 53200
    C.arena_t = st.enter_context(nc.sbuf_tensor("arena", [128, NF], F32))
    C.arena = Arena(C.arena_t, NF)
    C.ps = [st.enter_context(nc.psum_tensor("ps%d" % i, [128, 512], F32)) for i in range(8)]
    C.bps = [Buf("ps%d" % i) for i in range(8)]
    C.uid = [0]
    return C


def _ops(C):
    S = C.S

    def mm(out, lhsT, rhs, start, stop, reads, writes):
        S.op("pe", lambda h: h.matmul(out, lhsT=lhsT, rhs=rhs, start=start, stop=stop), reads, writes)

    def tr(out, in_, ident, reads, writes):
        S.op("pe", lambda h: h.transpose(out, in_, ident), reads, writes)

    def act(out, in_, func, reads, writes, bias=None, scale=1.0):
        if bias is None:
            S.op("act", lambda h: h.activation(out=out, in_=in_, func=func, scale=scale), reads, writes)
        else:
            S.op("act", lambda h: h.activation(out=out, in_=in_, func=func, bias=bias, scale=scale), reads, writes)

    def tt(eng, out, a, b, op, reads, writes):
        S.op(eng, lambda h: h.tensor_tensor(out=out, in0=a, in1=b, op=op), reads, writes)

    def ts(eng, out, a, s1, op0, reads, writes, s2=None, op1=None):
        if op1 is None:
            S.op(eng, lambda h: h.tensor_scalar(out, a, s1, None, op0), reads, writes)
        else:
            S.op(eng, lambda h: h.tensor_scalar(out, a, s1, s2, op0, op1), reads, writes)

    def stt(eng, out, in0, scalar, in1, op0, op1, reads, writes):
        S.op(eng, lambda h: h.scalar_tensor_tensor(out=out, in0=in0, scalar=scalar, in1=in1, op0=op0, op1=op1), reads, writes)

    def cp(eng, out, in_, reads, writes):
        if eng == "act":
            S.op("act", lambda h: h.copy(out, in_), reads, writes)
        else:
            S.op(eng, lambda h: h.tensor_copy(out, in_), reads, writes)

    def ms(eng, out, val, writes):
        S.op(eng, lambda h: h.memset(out, val), (), writes)

    return mm, tr, act, tt, ts, stt, cp, ms


def emit_consts(C):
    mm, tr, act, tt, ts, stt, cp, ms = _ops(C)
    S = C.S
    A = C.arena
    K = Ctx()
    C.K = K
    K.b = Buf("consts")
    K.ones = A.alloc([128, 128])
    K.ident = A.alloc([128, 128])
    K.obd1 = A.alloc([128, 128])
    K.obd64 = A.alloc([128, 128])
    K.o128 = A.alloc([128, 128])
    K.mask2 = A.alloc([64, 2, 64])
    K.mask3 = A.alloc([64, 64])
    K.rst = A.alloc([128, TB])
    K.eps_kk = A.alloc([128, 2])
    K.eps_gn = A.alloc([128, 2])
    K.eps_rms = A.alloc([128, 2])
    K.eps_ln = A.alloc([128, 2])
    b = [K.b]
    ms("pool", K.ones, 1.0, b)
    S.op("pool", lambda h: h.affine_select(out=K.ident, in_=K.ones, pattern=[[-1, 128]], compare_op=ALU.is_equal,
                                           fill=0.0, base=0, channel_multiplier=1), b, b)
    ms("pool", K.obd1, 0.0, b)
    ms("pool", K.obd1[0:64, 0:64], 1.0, b)
    ms("pool", K.obd1[64:128, 64:128], 1.0, b)
    ts("pool", K.obd64, K.obd1, 1.0 / 64, ALU.mult, b, b)
    ms("pool", K.o128, 1.0 / 128, b)
    S.op("pool", lambda h: h.affine_select(out=K.mask2[:, 0, :], in_=K.ones[0:64, 0:64], pattern=[[1, 64]],
                                           compare_op=ALU.is_gt, fill=0.0, base=0, channel_multiplier=-1), b, b)
    S.op("pool", lambda h: h.affine_select(out=K.mask2[:, 1, :], in_=K.ones[0:64, 0:64], pattern=[[1, 64]],
                                           compare_op=ALU.is_ge, fill=0.0, base=0, channel_multiplier=-1), b, b)
    S.op("pool", lambda h: h.affine_select(out=K.mask3, in_=K.ones[0:64, 0:64], pattern=[[-1, 64]],
                                           compare_op=ALU.is_gt, fill=0.0, base=0, channel_multiplier=1), b, b)
    ms("pool", K.rst, 1.0, b)
    ms("pool", K.eps_kk, 1e-24, b)
    ms("pool", K.eps_gn, GN_EPS, b)
    ms("pool", K.eps_rms, RMS_EPS, b)
    ms("pool", K.eps_ln, LN_EPS, b)
    ms("pool", K.rst.rearrange("p (c k) -> p c k", k=CH)[:, :, 0:1], 0.0, b)
    C.persist_off = A.off


def _interleave(gens, burst=None):
    if burst is None:
        burst = [1] * len(gens)
    pairs = [(g, b) for g, b in zip(gens, burst) if g is not None]
    while pairs:
        alive = []
        for g, b in pairs:
            ok = True
            for _ in range(b):
                try:
                    next(g)
                except StopIteration:
                    ok = False
                    break
            if ok:
                alive.append((g, b))
        pairs = alive


def _chain(*gens):
    for g in gens:
        if g is not None:
            yield from g


def _drain(g):
    for _ in g:
        pass


def emit_A(C, hin, W, oT_out, vf, has_vmix, li, T, dbg=None):
    mm, tr, act, tt, ts, stt, cp, ms = _ops(C)
    S = C.S
    K = C.K
    A = C.arena
    A.off = C.persist_off
    S.barrier()
    PS = C.ps
    NS = 21 if has_vmix else 17
    NFM = NS + 12
    R0, K0, V0, Z0, WA = 0, 4, 8, 12, 16
    VO0 = 17
    Q0, F0, ZH0 = NS, NS + 4, NS + 8
    cMU = 0
    cW0, cA0, cKK, cKA, cRK, cGNW, cGNB, cVM0, cLB0, cLB1, cGW = [NS + 4 * i for i in range(11)]
    NPP = NS + 44
    kb = K.b
    NB = T // TB
    BK = [Buf("psb%d" % i) for i in range(8)]

    def pv(bank, i):
        return PS[bank][:, i * 256:(i + 1) * 256]

    NG = 2
    PG = 4 // NG

    def tok2(name):
        return [Buf(name + str(i)) for i in range(NG)]

    stg = [A.alloc([128, 2048]) for _ in range(2)]
    bstg = [Buf("stg%d" % i) for i in range(2)]
    HTB = alloc_bf(A, [128, KC, TB])
    bHTB = Buf("HTB")
    NWB = 4
    WB = [alloc_bf(A, [128, KC, 128]) for _ in range(NWB)]
    bWB = [Buf("wb%d" % i) for i in range(NWB)]
    WIB = alloc_bf(A, [128, KC, 512])
    bWIB = Buf("WIB")
    PR = A.alloc([128, NFM, TB + 2])
    bPR = [Buf("PR%d" % i) for i in range(NFM)]
    LAST = A.alloc([128, NS])
    bLAST = Buf("LAST")
    ITOK = alloc_bf(A, [64, NCH, 512])
    bITOK = Buf("ITOK")
    PP = A.alloc([128, NPP])
    PD = A.alloc([128, 32])
    bPP = Buf("PP")
    LRW = A.alloc([128, 512])
    LRA = A.alloc([128, 512])
    if has_vmix:
        VDN = A.alloc([128, 8, 32])
        VUP = A.alloc([32, 512])
        VD = A.alloc([32, TB])
        bVD = Buf("VD")
    Ssl = [A.alloc([128, 4, TB]) for _ in range(7)]
    S1, S2, S3, S4, S5, S6, S7 = Ssl
    b1, b2, b3, b4, b5, b6, b7 = [tok2("S%d" % i) for i in range(7)]
    GC = A.alloc([128, 4, NCH]); bGC = tok2("GC")
    H1, H2, H3 = [A.alloc([128, 4, TB]) for _ in range(3)]
    bh1, bh2, bh3 = [tok2("H%d" % i) for i in range(3)]
    BM = A.alloc([128, 4, NCH]); BL = A.alloc([128, 4, NCH])
    EM = A.alloc([128, 4, NCH]); EL = A.alloc([128, 4, NCH]); ELM = A.alloc([128, 4, NCH])
    bBM = tok2("BM")
    SBmD = [[alloc_bf(A, [64, 4, 2, 64]) for _ in range(2)] for _ in range(2)]
    SKmD = [[alloc_bf(A, [64, 4, 2, 64]) for _ in range(2)] for _ in range(2)]
    tSBD = [[[Buf("SBm%d%d" % (q, i))] for i in range(2)] for q in range(2)]
    tSKD = [[[Buf("SKm%d%d" % (q, i))] for i in range(2)] for q in range(2)]
    PTD = [alloc_bf(A, [64, 8, 64]) for _ in range(2)]
    bPTD = [Buf("PTD0"), Buf("PTD1")]
    NA = [alloc_bf(A, [64, 8, 64]) for _ in range(2)]
    NTP = [alloc_bf(A, [64, 8, 2, 64]) for _ in range(2)]
    bNA = [tok2("NA%d" % i) for i in range(2)]
    bNAT = [tok2("NAT%d" % i) for i in range(2)]
    bPT = tok2("PT")
    Xs = alloc_bf(A, [64, 8, 64]); bXs = tok2("Xs")
    Us = alloc_bf(A, [64, 4, 2, 64]); bUs = tok2("Us")
    VT = alloc_bf(A, [64, 4, 128]); bVT = tok2("VT")
    BST = alloc_bf(A, [64, 4, 128]); bBST = tok2("BST")
    KST = alloc_bf(A, [64, 4, 128]); bKST = tok2("KST")
    HB = [A.alloc([128, 4, 128]) for _ in range(2)]
    bHB = [tok2("HB%d" % i) for i in range(2)]
    HBb = [alloc_bf(A, [128, 4, 128]) for _ in range(2)]
    bHBb = [tok2("HBb%d" % i) for i in range(2)]
    ARb = alloc_bf(A, [128, 4, 2, TB])
    ATb = ARb[:, :, 0, :]; bATb = tok2("ATb")
    RTb = ARb[:, :, 1, :]; bRTb = tok2("RTb")
    BTb = alloc_bf(A, [128, 4, TB]); bBTb = tok2("BTb")
    KTb = alloc_bf(A, [128, 4, TB]); bKTb = tok2("KTb")
    Qb = alloc_bf(A, [128, 4, TB]); bQb = tok2("Qb")
    Fb = alloc_bf(A, [128, 4, TB]); bFb = tok2("Fb")
    HD = A.alloc([128, 4, 64]); bHD = tok2("HD")
    OTR = A.alloc([128, 4, TB]); bOTR = tok2("OTR")
    OTH = A.alloc([128, 4, TB]); bOTH = tok2("OTH")
    ATT = alloc_bf(A, [64, 4, 64]); bATT = tok2("ATT")
    KTK = alloc_bf(A, [64, 4, 128]); bKTK = tok2("KTK")
    SH = A.alloc([128, 4, 128]); bSH = tok2("SH")
    SM = alloc_bf(A, [128, 4, 128]); bSM = tok2("SM")
    SD = A.alloc([128, 4, 128]); bSD = tok2("SD")
    OTB = alloc_bf(A, [128, 8, TB])
    bOTB = [Buf("OTB%d" % i) for i in range(2 * NG)]
    bVF = Buf("vf_dram")
    bWS = [Buf("wscr%d" % i) for i in range(NFM)]

    S.dma("sp", lambda h: h.dma_start(out=PP, in_=W["pp"]), (), [bPP])
    S.dma("sp", lambda h: h.dma_start(out=LRW, in_=W["lrw"]), (), [bPP])
    S.dma("sp", lambda h: h.dma_start(out=LRA, in_=W["lra"]), (), [bPP])
    if has_vmix:
        S.dma("sp", lambda h: h.dma_start(out=VDN, in_=W["vdn"].rearrange("p (c r) -> p c r", c=8)), (), [bPP])
        S.dma("sp", lambda h: h.dma_start(out=VUP, in_=W["vup"]), (), [bPP])
    pq = [bPP]
    ts("pool", PD[:, 0:4], PP[:, cW0:cW0 + 4], 0.5, ALU.mult, pq, pq)
    ts("pool", PD[:, 4:8], PP[:, cA0:cA0 + 4], 0.5, ALU.mult, pq, pq)
    ts("pool", PD[:, 8:12], PP[:, cKA:cKA + 4], -1.0, ALU.mult, pq, pq, 1.0, ALU.add)
    ts("pool", PD[:, 12:16], PP[:, cVM0:cVM0 + 4], 0.5, ALU.mult, pq, pq)
    tt("dve", PD[:, 28:32], PP[:, cLB0:cLB0 + 4], PP[:, cLB1:cLB1 + 4], ALU.max, pq, pq)
    tt("pool", PD[:, 16:20], PP[:, cLB0:cLB0 + 4], PD[:, 28:32], ALU.subtract, pq, pq)
    tt("pool", PD[:, 20:24], PP[:, cLB1:cLB1 + 4], PD[:, 28:32], ALU.subtract, pq, pq)
    act(PD[:, 16:20], PD[:, 16:20], AF.Exp, pq, pq)
    act(PD[:, 20:24], PD[:, 20:24], AF.Exp, pq, pq)
    tt("dve", PD[:, 28:32], PD[:, 16:20], PD[:, 20:24], ALU.add, pq, pq)
    S.op("dve", lambda h: h.reciprocal(PD[:, 28:32], PD[:, 28:32]), pq, pq)
    tt("dve", PD[:, 16:20], PD[:, 16:20], PD[:, 28:32], ALU.mult, pq, pq)
    tt("dve", PD[:, 20:24], PD[:, 20:24], PD[:, 28:32], ALU.mult, pq, pq)
    if li == 0:
        tt("dve", PD[:, 20:24], PD[:, 16:20], PD[:, 16:20], ALU.subtract, pq, pq)
        cp("dve", PD[:, 16:20], PD[:, 20:24], pq, pq)
    else:
        tt("dve", PD[:, 20:24], PD[:, 16:20], PD[:, 20:24], ALU.add, pq, pq)
        tt("dve", PD[:, 16:20], PD[:, 20:24], PD[:, 16:20], ALU.subtract, pq, pq)
    ts("dve", PD[:, 20:24], PD[:, 16:20], -1.0, ALU.mult, pq, pq, 1.0, ALU.add)
    ts("dve", PD[:, 24:28], PD[:, 16:20], 1e-30, ALU.max, pq, pq)
    def G_wib():
        for q in range(4):
            sl = q % 2
            S.dma("sp", lambda h, sl=sl, q=q: h.dma_start(out=stg[sl], in_=W["wi"][:, q * 2048:(q + 1) * 2048]), (), [bstg[sl]])
            cp("dve" if q % 2 == 0 else "act", WIB[:, 4 * q:4 * q + 4, :], stg[sl].rearrange("p (k f) -> p k f", k=4), [bstg[sl]], [bWIB])
            yield

    ms("pool", LAST, 0.0, [bLAST])
    for i in range(2):
        ms("pool", HB[i], 0.0, bHB[i])
        ms("pool", HBb[i], 0.0, bHBb[i])
    ms("pool", SH, 0.0, bSH)
    hcur = [0, 0]
    inrr = [0]

    def in_slot():
        i = inrr[0] % 2
        inrr[0] += 1
        return 6 + i, 0

    def pcolh(c0, h):
        return bc(PP[:, c0 + PG * h:c0 + PG * h + PG], TB)

    def dcolh(c0, h):
        return bc(PD[:, c0 + PG * h:c0 + PG * h + PG], TB)

    def v4(x):
        return x.rearrange("p a (c k) -> p a c k", k=CH)

    def PRc(a, n):
        return PR[:, a:a + n, 1:1 + TB]

    def load_w(tb, fc):
        wsl = fc % NWB
        if tb == 0:
            sl = fc % 2
            S.dma("sp", lambda h, sl=sl, fc=fc: h.dma_start(out=stg[sl], in_=W["wfm"][fc]), (), [bstg[sl]])
            cp("dve" if fc % 2 == 0 else "act", WB[wsl], stg[sl].rearrange("p (k f) -> p k f", k=KC), [bstg[sl]], [bWB[wsl]])
            S.dma("pool", lambda h, wsl=wsl, fc=fc: h.dma_start(out=W["wscr"][fc], in_=WB[wsl].rearrange("p k f -> p (k f)")),
                  [bWB[wsl]], [bWS[fc]])
        else:
            S.dma("sp", lambda h, wsl=wsl, fc=fc: h.dma_start(out=WB[wsl].rearrange("p k f -> p (k f)"), in_=W["wscr"][fc]),
                  [bWS[fc]], [bWB[wsl]])
        return wsl

    def G_proj(tb, fcs):
        for fc in fcs:
            wsl = load_w(tb, fc)
            bk, hh = in_slot()
            o = pv(bk, hh)
            for kc in range(KC):
                mm(o, WB[wsl][:, kc, :], HTB[:, kc, :], kc == 0, kc == KC - 1, [bWB[wsl], bHTB], [BK[bk]])
            cp("act", PR[:, fc, 1:1 + TB], o, [BK[bk]], [bPR[fc]])
            yield

    def G_ht(tb):
        t0 = tb * TB
        for j in range(TB // 128):
            sl = j % 2
            S.dma("sp", lambda h, sl=sl, j=j, t0=t0: h.dma_start(out=stg[sl], in_=hin[t0 + j * 128:t0 + (j + 1) * 128, :]), (), [bstg[sl]])
            for q in range(4):
                bk = 6 + (q % 2)
                for i in range(4):
                    kc = 4 * q + i
                    tr(PS[bk][:, i * 128:(i + 1) * 128], stg[sl][:, kc * 128:(kc + 1) * 128], K.ident, [bstg[sl], kb], [BK[bk]])
                cp("act" if q % 2 == 0 else "dve", HTB[:, 4 * q:4 * q + 4, j * 128:(j + 1) * 128],
                   PS[bk].rearrange("p (k t) -> p k t", k=4), [BK[bk]], [bHTB])
                yield

    def G_itok(tb):
        for c in range(NCH):
            bk = 6 + (c % 2)
            for kc in range(KC):
                mm(PS[bk][0:64, :], HTB[:, kc, c * CH:(c + 1) * CH], WIB[:, kc, :], kc == 0, kc == KC - 1, [bHTB, bWIB], [BK[bk]])
            cp("act" if c % 2 == 0 else "dve", ITOK[:, c, :], PS[bk][0:64, :], [BK[bk]], [bITOK])
            yield

    bLASTe, bLASTl = Buf("LASTe"), Buf("LASTl")

    def _shift_group(a, n, tmp, btmp):
        cur = PR[:, a:a + n, 1:1 + TB]
        prv = PR[:, a:a + n, 0:TB]
        tt("dve", tmp[:, 0:n, :], prv, cur, ALU.subtract, bPR[a:a + n], btmp)
        tt("pool", tmp[:, 0:n, :], tmp[:, 0:n, :], bc(PP[:, cMU + a:cMU + a + n], TB), ALU.mult, btmp + [bPP], btmp)
        tt("dve", cur, cur, tmp[:, 0:n, :], ALU.add, bPR[a:a + n] + btmp, bPR[a:a + n])

    def G_common_early(tb):
        for (lo, hi) in ((0, 8), (16, NS)):
            cp("pool", PR[:, lo:hi, 0], LAST[:, lo:hi], [bLASTe], bPR[lo:hi])
            cp("pool", LAST[:, lo:hi], PR[:, lo:hi, TB], bPR[lo:hi], [bLASTe])
        yield
        groups = [(0, 4), (4, 4), (16, 1)] + ([(17, 4)] if has_vmix else [])
        for gi, (a, n) in enumerate(groups):
            _shift_group(a, n, S4 if gi % 2 == 0 else S6, list(b4) if gi % 2 == 0 else list(b6))
            yield
        act(PR[0:64, WA, 1:1 + TB], PR[0:64, WA, 1:1 + TB], AF.Tanh, [bPR[WA]], [bPR[WA]])
        yield

    def G_common_late(tb):
        t0 = tb * TB
        cp("pool", PR[:, 8:16, 0], LAST[:, 8:16], [bLASTl], bPR[8:16])
        cp("pool", LAST[:, 8:16], PR[:, 8:16, TB], bPR[8:16], [bLASTl])
        yield
        for (a, n) in ((8, 4), (12, 4)):
            _shift_group(a, n, OTR, list(bOTR))
            yield
        if has_vmix:
            bk, hh = in_slot()
            o = pv(bk, hh)
            for c8 in range(8):
                fcv = V0 + c8 if c8 < 4 else VO0 + c8 - 4
                mm(o[0:32, :], VDN[:, c8, :], PR[:, fcv, 1:1 + TB], c8 == 0, c8 == 7, [bPP, bPR[fcv]], [BK[bk]])
            cp("act", VD, o[0:32, :], [BK[bk]], [bVD])
            yield
        else:
            S.dma("pool", lambda h, t0=t0: h.dma_start(out=vf[:, :, t0:t0 + TB].rearrange("c p t -> p c t"), in_=PRc(V0, 4)),
                  bPR[V0:V0 + 4], [bVF], is_output=True)

    def G_prep_rw(tb, h):
        t0 = tb * TB
        p0 = PG * h
        pr2 = range(p0, p0 + PG)
        sl2 = slice(p0, p0 + PG)
        Rr, Kk, Vv, Zz = (PR[:, a + p0:a + p0 + PG, 1:1 + TB] for a in (R0, K0, V0, Z0))
        bR, bK, bV = bPR[R0 + p0:R0 + p0 + PG], bPR[K0 + p0:K0 + p0 + PG], bPR[V0 + p0:V0 + p0 + PG]
        s1, s2, s3, s4, s5, s6, s7 = (x[:, sl2, :] for x in Ssl)
        q1, q2, q3, q4, q5, q6, q7 = ([x[h]] for x in (b1, b2, b3, b4, b5, b6, b7))
        for p in pr2:
            bk, hh = in_slot()
            mm(pv(bk, hh), LRW[:, p * 128:(p + 1) * 128], PR[:, WA, 1:1 + TB], True, True, [bPP, bPR[WA]], [BK[bk]])
            act(S1[:, p, :], pv(bk, hh), AF.Tanh, [BK[bk], bPP], q1, bias=PD[:, p:p + 1], scale=0.5)
            bk, hh = in_slot()
            mm(pv(bk, hh), LRA[:, p * 128:(p + 1) * 128], PR[:, WA, 1:1 + TB], True, True, [bPP, bPR[WA]], [BK[bk]])
            act(S2[:, p, :], pv(bk, hh), AF.Tanh, [BK[bk], bPP], q2, bias=PD[:, 4 + p:5 + p], scale=0.5)
            yield
        ts("pool", s1, s1, 0.5 * LOGW_SCALE, ALU.mult, q1, q1, 0.5 * LOGW_SCALE, ALU.add)
        ts("pool", s2, s2, 0.5, ALU.mult, q2, q2, 0.5, ALU.add)
        yield
        for p in pr2:
            S.op("dve", lambda hd, p=p: hd.tensor_tensor_scan(out=S5[:, p, :], data0=K.rst, data1=S1[:, p, :], initial=0.0,
                                                              op0=ALU.mult, op1=ALU.add), [kb] + q1, q5)
        tt("pool", s1, s5, s1, ALU.subtract, q5 + q1, q1)
        act(s1, s1, AF.Exp, q1, q1)
        yield
        tt("pool", s3, Kk, pcolh(cKK, h), ALU.mult, bK + [bPP], q3)
        act(s4, s3, AF.Square, q3, q4)
        yield
        for p in pr2:
            bk, hh = in_slot()
            mm(pv(bk, hh), K.obd1, S4[:, p, :], True, True, [kb] + q4, [BK[bk]])
            act(S6[:, p, :], pv(bk, hh), AF.Ln, [BK[bk], kb], q6, bias=K.eps_kk[:, 0:1])
        act(s6, s6, AF.Exp, q6, q6, scale=-0.5)
        tt("dve", s3, s3, s6, ALU.mult, q3 + q6, q3)
        yield
        stt("dve", ATb[:, sl2, :], s3, -1.0, s1, ALU.mult, ALU.mult, q3 + q1, [bATb[h]])
        act(s6, s5, AF.Exp, q5, q6)
        tt("dve", RTb[:, sl2, :], Rr, s6, ALU.mult, bR + q6, [bRTb[h]])
        cp("act", GC[:, sl2, :], v4(s6)[:, :, :, CH - 1], q6, [bGC[h]])
        yield
        tt("pool", s4, s2, pcolh(cKA, h), ALU.mult, q2 + [bPP], q4)
        tt("pool", s4, s4, dcolh(8, h), ALU.add, q4 + [bPP], q4)
        tt("dve", Kk, Kk, s4, ALU.mult, bK + q4, bK)
        yield
        tt("pool", s4, Rr, Kk, ALU.mult, bR + bK, q4)
        tt("pool", s4, s4, pcolh(cRK, h), ALU.mult, q4 + [bPP], q4)
        yield
        for p in pr2:
            bk, hh = in_slot()
            mm(pv(bk, hh), K.obd1, S4[:, p, :], True, True, [kb] + q4, [BK[bk]])
            cp("dve", S7[:, p, :], pv(bk, hh), [BK[bk]], q7)
        yield
        act(s6, s5, AF.Exp, q5, q6, scale=-1.0)
        tt("pool", s2, s3, s2, ALU.mult, q3 + q2, q2)
        tt("dve", s3, s2, s6, ALU.mult, q2 + q6, q3)
        tt("dve", Kk, Kk, s6, ALU.mult, bK + q6, bK)
        yield
        cp("act", BTb[:, sl2, :], s3, q3, [bBTb[h]])
        cp("act", KTb[:, sl2, :], Kk, bK, [bKTb[h]])
        gcb = GC[:, sl2, :].unsqueeze(3).to_broadcast([128, PG, NCH, CH])
        tt("pool", v4(s2), v4(s3), gcb, ALU.mult, q3 + [bGC[h]], q2)
        tt("dve", v4(s5), v4(Kk), gcb, ALU.mult, bK + [bGC[h]], q5)
        yield

    def both(x):
        return list(x)

    def G_prep_rw2(tb, h):
        t0 = tb * TB
        p0 = PG * h
        pr2 = range(p0, p0 + PG)
        sl2 = slice(p0, p0 + PG)
        Vv = PR[:, V0 + p0:V0 + p0 + PG, 1:1 + TB]
        bV = bPR[V0 + p0:V0 + p0 + PG]
        s3, s4, s7 = S3[:, sl2, :], S4[:, sl2, :], S7[:, sl2, :]
        q3, q4, q7 = [b3[h]], [b4[h]], [b7[h]]
        if has_vmix:
            for p in pr2:
                bk, hh = in_slot()
                mm(pv(bk, hh), VUP[:, p * 128:(p + 1) * 128], VD, True, True, [bPP, bVD], [BK[bk]])
                act(S3[:, p, :], pv(bk, hh), AF.Tanh, [BK[bk], bPP], q3, bias=PD[:, 12 + p:13 + p], scale=0.5)
            ts("pool", s3, s3, 0.5, ALU.mult, q3, q3, 0.5, ALU.add)
            S.dma("pool", lambda hd, t0=t0: hd.dma_start(out=s4, in_=vf[p0:p0 + PG, :, t0:t0 + TB].rearrange("c p t -> p c t")), [bVF], q4)
            yield
            tt("dve", s4, s4, Vv, ALU.subtract, q4 + bV, q4)
            tt("pool", s4, s4, s3, ALU.mult, q4 + q3, q4)
            tt("dve", Vv, Vv, s4, ALU.add, bV + q4, bV)
            yield
        tt("dve", s7, s7, Vv, ALU.mult, q7 + bV, q7)
        yield

    def G_chunks_rw(tb):
        bV = bPR[V0:V0 + 4]
        m2b = K.mask2.unsqueeze(1).to_broadcast([64, 4, 2, 64])
        m3b = K.mask3.unsqueeze(1).to_broadcast([64, 4, 64])
        idb = K.ident[0:64, 0:64].unsqueeze(1).to_broadcast([64, 8, 64])
        tATb, tRTb, tBTb, tKTb = both(bATb), both(bRTb), both(bBTb), both(bKTb)
        tNA = [both(bNA[0]), both(bNA[1])]
        tNAT = [both(bNAT[0]), both(bNAT[1])]
        tXs, tUs, tVT, tBST, tKST, tHD, tOTR, tGC = (both(x) for x in (bXs, bUs, bVT, bBST, bKST, bHD, bOTR, bGC))
        tHB = [both(bHB[0]), both(bHB[1])]
        tHBb = [both(bHBb[0]), both(bHBb[1])]
        nch = NCH

        def pre(c):
            cs = slice(c * CH, (c + 1) * CH)
            par = c % 2
            SB_, SK_ = SBmD[par], SKmD[par]
            tSB_, tSK_ = tSBD[par], tSKD[par]
            NAv = NA[0].rearrange("p (a x) t -> p a x t", x=2)
            NTPv = NTP[0].rearrange("p (a x) y t -> p a x y t", x=2)
            for e in range(2):
                rows = slice(64 * e, 64 * e + 64)
                vB = PS[0].rearrange("p (a x t) -> p a x t", a=4, x=2)
                vK = PS[1].rearrange("p (a x t) -> p a x t", a=4, x=2)
                vL = PS[2].rearrange("p (a t) -> p a t", a=8)
                for p in range(4):
                    mm(vB[0:64, p, :, :], BTb[rows, p, cs], ARb[rows, p, :, cs], True, True, tBTb + tATb + tRTb, [BK[0]])
                    mm(vK[0:64, p, :, :], KTb[rows, p, cs], ARb[rows, p, :, cs], True, True, tKTb + tATb + tRTb, [BK[1]])
                    mm(vL[0:64, p, :], ATb[rows, p, cs], BTb[rows, p, cs], True, True, tBTb + tATb, [BK[2]])
                tt("dve", SB_[e], vB[0:64], m2b, ALU.mult, [BK[0], kb], tSB_[e])
                tt("dve", SK_[e], vK[0:64], m2b, ALU.mult, [BK[1], kb], tSK_[e])
                tt("dve", NAv[:, :, e, :], vL[0:64, 0:4, :], m3b, ALU.mult, [BK[2], kb], tNA[0])
                cp("pool", NTPv[:, :, e, 0, :], SB_[e][:, :, 0, :], tSB_[e], tNAT[0])
                yield
            tt("pool", NTP[0][:, :, 1, :], NTP[0][:, :, 0, :], idb, ALU.add, tNAT[0] + [kb], tNAT[0])
            cur = 0
            for step in range(1, 7):
                nx = 1 - cur
                if step <= 5:
                    vN = PS[0].rearrange("p (h t) -> p h t", h=8)
                    for h8 in range(8):
                        mm(vN[0:64, h8, :], NTP[cur][:, h8, 0, :], NA[cur][:, h8, :], True, True, tNAT[cur] + tNA[cur], [BK[0]])
                    cp("act", NA[nx], vN[0:64], [BK[0]], tNA[nx])
                if step == 1:
                    vT1 = PS[1].rearrange("p (h t) -> p h t", h=8)
                    for h8 in range(8):
                        mm(vT1[0:64, h8, :], NA[cur][:, h8, :], NTP[cur][:, h8, 0, :], True, True, tNAT[cur] + tNA[cur], [BK[1]])
                    cp("dve", NTP[nx][:, :, 0, :], vT1[0:64], [BK[1]], tNAT[nx])
                    cp("pool", NTP[nx][:, :, 1, :], NTP[cur][:, :, 1, :], tNAT[cur], tNAT[nx])
                elif step <= 4:
                    for half in range(2):
                        bkk = 1 + half
                        vBC = PS[bkk].rearrange("p (h y t) -> p h y t", h=4, y=2)
                        hs4 = slice(4 * half, 4 * half + 4)
                        for j in range(4):
                            h8 = 4 * half + j
                            mm(vBC[0:64, j, :, :], NA[cur][:, h8, :], NTP[cur][:, h8, :, :], True, True, tNAT[cur] + tNA[cur], [BK[bkk]])
                        cp("act" if half else "dve", NTP[nx][:, hs4, 0, :], vBC[0:64, :, 0, :], [BK[bkk]], tNAT[nx])
                        tt("dve", NTP[nx][:, hs4, 1, :], NTP[cur][:, hs4, 1, :], vBC[0:64, :, 1, :], ALU.add, tNAT[cur] + [BK[bkk]], tNAT[nx])
                elif step == 5:
                    vC = PS[1].rearrange("p (h t) -> p h t", h=8)
                    for h8 in range(8):
                        mm(vC[0:64, h8, :], NA[cur][:, h8, :], NTP[cur][:, h8, 1, :], True, True, tNAT[cur] + tNA[cur], [BK[1]])
                    tt("dve", NTP[nx][:, :, 1, :], NTP[cur][:, :, 1, :], vC[0:64], ALU.add, tNAT[cur] + [BK[1]], tNAT[nx])
                else:
                    vC = PS[1].rearrange("p (h t) -> p h t", h=8)
                    for h8 in range(8):
                        mm(vC[0:64, h8, :], NA[cur][:, h8, :], NTP[cur][:, h8, 1, :], True, True, tNAT[cur] + tNA[cur], [BK[1]])
                    tt("dve", PTD[par], NTP[cur][:, :, 1, :], vC[0:64], ALU.add, tNAT[cur] + [BK[1]], [bPTD[par]])
                cur = nx
                yield

        def chain(c):
            cs = slice(c * CH, (c + 1) * CH)
            par = c % 2
            SB_, SK_ = SBmD[par], SKmD[par]
            tSB_, tSK_ = tSBD[par], tSKD[par]
            PTf = PTD[par]
            tPT = [bPTD[par]]
            for (src, bsrc, dst, bdst, bk, eng) in ((PRc(V0, 4), bV, VT, tVT, 3, "act"), (S2, both(b2), BST, tBST, 4, "dve"),
                                                     (S5, both(b5), KST, tKST, 5, "act")):
                vT = PS[bk].rearrange("p (a f) -> p a f", a=4)
                for p in range(4):
                    tr(vT[0:64, p, :], src[:, p, cs], K.ident, bsrc + [kb], [BK[bk]])
                cp(eng, dst, vT[0:64], [BK[bk]], bdst)
                yield
            hc = hcur[0]
            vX = PS[3].rearrange("p (h t) -> p h t", h=8)
            for h8 in range(8):
                p, e = h8 // 2, h8 % 2
                mm(vX[0:64, h8, :], ATb[:, p, cs], HBb[hc][:, p, 64 * e:64 * e + 64], True, False, tATb + tHBb[hc], [BK[3]])
                mm(vX[0:64, h8, :], SK_[e][:, p, 0, :], VT[:, p, 64 * e:64 * e + 64], False, True, tSK_[e] + tVT, [BK[3]])
            cp("act", Xs, vX[0:64], [BK[3]], tXs)
            yield
            vU = PS[4].rearrange("p (h t) -> p h t", h=8)
            for h8 in range(8):
                mm(vU[0:64, h8, :], PTf[:, h8, :], Xs[:, h8, :], True, True, tPT + tXs, [BK[4]])
            cp("dve", Us.rearrange("p a x t -> p (a x) t"), vU[0:64], [BK[4]], tUs)
            yield
            vH = PS[5].rearrange("p (a f) -> p a f", a=4)
            for p in range(4):
                mm(vH[:, p, :], BST[:, p, :], Us[:, p, :, :].rearrange("p x t -> p (x t)"), True, False, tBST + tUs, [BK[5]])
                mm(vH[:, p, :], KST[:, p, :], VT[:, p, :], False, True, tKST + tVT, [BK[5]])
            hn = 1 - hc
            for e in range(2):
                rows = slice(64 * e, 64 * e + 64)
                cols = slice(64 * e, 64 * e + 64)
                tt("pool", HD[rows], HB[hc][rows, :, cols], GC[rows, :, c:c + 1].to_broadcast([64, 4, 64]), ALU.mult,
                   tHB[hc] + tGC, tHD)
                tt("dve", HBb[hn][rows, :, cols], HD[rows], vH[rows, :, cols], ALU.add, tHD + [BK[5]], tHBb[hn])
                tt("dve", HB[hn][rows, :, cols], HD[rows], vH[rows, :, cols], ALU.add, tHD + [BK[5]], tHB[hn])
            yield
            vO = PS[3].rearrange("p (a t) -> p a t", a=8)
            for h8 in range(8):
                p, e = h8 // 2, h8 % 2
                o_ap = vO[64 * e:64 * e + 64, p, :]
                mm(o_ap, HBb[hc][:, p, 64 * e:64 * e + 64], RTb[:, p, cs], True, False, tHBb[hc] + tRTb, [BK[3]])
                mm(o_ap, Us[:, p, e, :], SB_[e][:, p, 1, :], False, False, tUs + tSB_[e], [BK[3]])
                mm(o_ap, VT[:, p, 64 * e:64 * e + 64], SK_[e][:, p, 1, :], False, True, tVT + tSK_[e], [BK[3]])
            cp("act", OTR[:, :, cs], vO[:, 0:4, :], [BK[3]], tOTR)
            hcur[0] = hn
            yield

        if nch:
            yield from pre(0)
            for c in range(nch):
                nxt = pre(c + 1) if c + 1 < nch else None
                yield from _interleave_gen([chain(c), nxt])

    def G_post_rw(tb, h):
        p0 = PG * h
        pr2 = range(p0, p0 + PG)
        sl2 = slice(p0, p0 + PG)
        Zz = PR[:, Z0 + p0:Z0 + p0 + PG, 1:1 + TB]
        bZ = bPR[Z0 + p0:Z0 + p0 + PG]
        s4, s6, s7 = S4[:, sl2, :], S6[:, sl2, :], S7[:, sl2, :]
        q4, q6, q7 = [b4[h]], [b6[h]], [b7[h]]
        bk = 4 + (h % 2)
        sl_ = (h // 2) % 2
        for p in pr2:
            mm(pv(bk, (sl_ + p - p0) % 2), K.obd64, OTR[:, p, :], True, True, [kb, bOTR[h]], [BK[bk]])
            tt("dve", S4[:, p, :], OTR[:, p, :], pv(bk, (sl_ + p - p0) % 2), ALU.subtract, [bOTR[h], BK[bk]], q4)
        act(s6, s4, AF.Square, q4, q6)
        yield
        for p in pr2:
            mm(pv(bk, (sl_ + p - p0) % 2), K.obd64, S6[:, p, :], True, True, [kb] + q6, [BK[bk]])
            act(S6[:, p, :], pv(bk, (sl_ + p - p0) % 2), AF.Ln, [BK[bk], kb], q6, bias=K.eps_gn[:, 0:1])
        act(s6, s6, AF.Exp, q6, q6, scale=-0.5)
        yield
        tt("dve", s4, s4, s6, ALU.mult, q4 + q6, q4)
        tt("pool", s4, s4, pcolh(cGNW, h), ALU.mult, q4 + [bPP], q4)
        tt("pool", s4, s4, pcolh(cGNB, h), ALU.add, q4 + [bPP], q4)
        tt("dve", s4, s4, s7, ALU.add, q4 + q7, q4)
        yield
        act(s6, Zz, AF.Tanh, bZ, q6, scale=0.5)
        ts("pool", s6, s6, 0.5, ALU.mult, q6, q6, 0.5, ALU.add)
        tt("pool", s6, s6, Zz, ALU.mult, q6 + bZ, q6)
        tt("dve", OTB[:, p0:p0 + PG, :], s4, s6, ALU.mult, q4 + q6, [bOTB[h]])
        yield

    def G_prep_hg(tb, h):
        p0 = PG * h
        pr2 = range(p0, p0 + PG)
        sl2 = slice(p0, p0 + PG)
        Qq, Ff = (PR[:, a + p0:a + p0 + PG, 1:1 + TB] for a in (Q0, F0))
        bQ, bF = bPR[Q0 + p0:Q0 + p0 + PG], bPR[F0 + p0:F0 + p0 + PG]
        g1, g2, g3 = H1[:, sl2, :], H2[:, sl2, :], H3[:, sl2, :]
        q1, q2, q3 = [bh1[h]], [bh2[h]], [bh3[h]]
        bm = [bBM[h]]
        act(g1, Qq, AF.Tanh, bQ, q1, scale=0.5)
        ts("pool", g1, g1, 0.5, ALU.mult, q1, q1, 0.5, ALU.add)
        tt("dve", Qq, Qq, g1, ALU.mult, bQ + q1, bQ)
        yield
        act(g1, Ff, AF.Tanh, bF, q1, scale=0.5)
        ts("pool", g1, g1, 0.5, ALU.mult, q1, q1, 0.5, ALU.add)
        tt("pool", g2, g1, dcolh(20, h), ALU.mult, q1 + [bPP], q2)
        tt("pool", g2, g2, dcolh(24, h), ALU.add, q2 + [bPP], q2)
        act(g2, g2, AF.Ln, q2, q2)
        yield
        ts("pool", Ff, g1, -1.0, ALU.mult, q1, bF, 1.0, ALU.add)
        tt("pool", Ff, Ff, dcolh(20, h), ALU.mult, bF + [bPP], bF)
        for p in pr2:
            S.op("dve", lambda hd, p=p: hd.tensor_tensor_scan(out=H3[:, p, :], data0=K.rst, data1=H2[:, p, :], initial=0.0,
                                                              op0=ALU.mult, op1=ALU.add), [kb] + q2, q3)
        yield
        H3v = v4(g3)
        cp("pool", BM[:, sl2, :], H3v[:, :, :, CH // 2 - 1], q3, bm)
        cp("pool", BL[:, sl2, :], H3v[:, :, :, CH - 1], q3, bm)
        bmb = BM[:, sl2, :].unsqueeze(3).to_broadcast([128, PG, NCH, CH])
        tt("pool", v4(g2), H3v, bmb, ALU.subtract, q3 + bm, q2)
        act(g1, g2, AF.Exp, q2, q1)
        tt("dve", Qb[:, sl2, :], Qq, g1, ALU.mult, bQ + q1, [bQb[h]])
        yield
        act(g1, g2, AF.Exp, q2, q1, scale=-1.0)
        tt("dve", Ff, Ff, g1, ALU.mult, bF + q1, bF)
        cp("act", Fb[:, sl2, :], Ff, bF, [bFb[h]])
        act(EM[:, sl2, :], BM[:, sl2, :], AF.Exp, bm, bm)
        act(EL[:, sl2, :], BL[:, sl2, :], AF.Exp, bm, bm)
        tt("pool", ELM[:, sl2, :], BL[:, sl2, :], BM[:, sl2, :], ALU.subtract, bm, bm)
        act(ELM[:, sl2, :], ELM[:, sl2, :], AF.Exp, bm, bm)
        yield

    def G_chunks_hg(tb):
        bF = bPR[F0:F0 + 4]
        tFb, tQb, tATT, tKTK, tSH, tSM, tSD, tOTH, tBM = (both(x) for x in (bFb, bQb, bATT, bKTK, bSH, bSM, bSD, bOTH, bBM))
        for c in range(NCH):
            cs = slice(c * CH, (c + 1) * CH)
            vAt = PS[0].rearrange("p (a t) -> p a t", a=8)
            for p in range(4):
                mm(vAt[0:64, p, :], Fb[:, p, cs], Qb[:, p, cs], True, True, tFb + tQb, [BK[0]])
            tt("dve", ATT, vAt[0:64, 0:4, :], K.mask2[:, 1, :].unsqueeze(1).to_broadcast([64, 4, 64]), ALU.mult, [BK[0], kb], tATT)
            vKt = PS[1].rearrange("p (a f) -> p a f", a=4)
            for p in range(4):
                tr(vKt[0:64, p, :], PR[:, F0 + p, 1 + c * CH:1 + (c + 1) * CH], K.ident, bF + [kb], [BK[1]])
            cp("act", KTK, vKt[0:64], [BK[1]], tKTK)
            tt("pool", SM, SH, EM[:, :, c:c + 1].to_broadcast([128, 4, 128]), ALU.mult, tSH + tBM, tSM)
            yield
            vOh = PS[2].rearrange("p (a t) -> p a t", a=8)
            for p in range(4):
                mm(vOh[:, p, :], SM[:, p, :], Qb[:, p, cs], True, False, tSM + tQb, [BK[2]])
                mm(vOh[:, p, :], ITOK[:, c, p * 128:(p + 1) * 128], ATT[:, p, :], False, True, [bITOK] + tATT, [BK[2]])
            cp("act", OTH[:, :, cs], vOh[:, 0:4, :], [BK[2]], tOTH)
            vS = PS[3].rearrange("p (a f) -> p a f", a=4)
            for p in range(4):
                mm(vS[:, p, :], KTK[:, p, :], ITOK[:, c, p * 128:(p + 1) * 128], True, True, tKTK + [bITOK], [BK[3]])
            tt("dve", SD, vS, ELM[:, :, c:c + 1].to_broadcast([128, 4, 128]), ALU.mult, [BK[3]] + tBM, tSD)
            tt("pool", SH, SH, EL[:, :, c:c + 1].to_broadcast([128, 4, 128]), ALU.mult, tSH + tBM, tSH)
            tt("pool", SH, SH, SD, ALU.add, tSH + tSD, tSH)
            yield

    def G_post_hg(tb, h):
        p0 = PG * h
        pr2 = range(p0, p0 + PG)
        sl2 = slice(p0, p0 + PG)
        Zh = PR[:, ZH0 + p0:ZH0 + p0 + PG, 1:1 + TB]
        bZ = bPR[ZH0 + p0:ZH0 + p0 + PG]
        g1, g2 = H1[:, sl2, :], H2[:, sl2, :]
        q1, q2 = [bh1[h]], [bh2[h]]
        oth = OTH[:, sl2, :]
        act(g1, oth, AF.Square, [bOTH[h]], q1)
        bk = 4 + (h % 2)
        sl_ = (h // 2) % 2
        for p in pr2:
            mm(pv(bk, (sl_ + p - p0) % 2), K.o128, H1[:, p, :], True, True, [kb] + q1, [BK[bk]])
            act(H1[:, p, :], pv(bk, (sl_ + p - p0) % 2), AF.Ln, [BK[bk], kb], q1, bias=K.eps_rms[:, 0:1])
        act(g1, g1, AF.Exp, q1, q1, scale=-0.5)
        yield
        tt("dve", oth, oth, g1, ALU.mult, [bOTH[h]] + q1, [bOTH[h]])
        tt("pool", oth, oth, pcolh(cGW, h), ALU.mult, [bOTH[h], bPP], [bOTH[h]])
        act(g2, Zh, AF.Tanh, bZ, q2, scale=0.5)
        ts("pool", g2, g2, 0.5, ALU.mult, q2, q2, 0.5, ALU.add)
        tt("pool", g2, g2, Zh, ALU.mult, q2 + bZ, q2)
        tt("dve", OTB[:, 4 + p0:4 + p0 + PG, :], oth, g2, ALU.mult, [bOTH[h]] + q2, [bOTB[NG + h]])
        yield

    def G_store(tb):
        t0 = tb * TB
        S.dma("pool", lambda hd, t0=t0: hd.dma_start(out=oT_out[:, :, t0:t0 + TB].rearrange("c p t -> p c t"), in_=OTB), bOTB, [],
              is_output=True)
        yield

    early = list(range(R0, R0 + 8)) + [WA] + (list(range(VO0, VO0 + 4)) if has_vmix else [])
    late = list(range(V0, V0 + 8))

    def rw_prep_stage(tb, other=None):
        _interleave([
            other,
            _chain(G_common_early(tb), _interleave_gen([G_prep_rw(tb, g) for g in range(NG)])),
            _chain(G_proj(tb, late), G_common_late(tb)),
        ], burst=[1, 1, 2] if li == 0 else [1, 1, 1])
        _drain(_interleave_gen([G_prep_rw2(tb, g) for g in range(NG)]))

    _drain(_chain(G_ht(0), G_proj(0, early)))
    rw_prep_stage(0)
    for tb in range(NB):
        more = tb + 1 < NB
        _interleave([
            _chain(G_chunks_rw(tb), _interleave_gen([G_post_rw(tb, g) for g in range(NG)])),
            _chain(G_proj(tb, range(NS, NFM)), G_wib() if tb == 0 else None, G_itok(tb),
                   G_ht(tb + 1) if more else None, G_proj(tb + 1, early) if more else None,
                   _interleave_gen([G_prep_hg(tb, g) for g in range(NG)])),
        ])
        hg = _chain(G_chunks_hg(tb), _interleave_gen([G_post_hg(tb, g) for g in range(NG)]))
        if more:
            rw_prep_stage(tb + 1, hg)
        else:
            _drain(hg)
        _drain(G_store(tb))


def _interleave_gen(gens):
    gens = [g for g in gens if g is not None]
    while gens:
        alive = []
        for g in gens:
            try:
                next(g)
                alive.append(g)
            except StopIteration:
                pass
        gens = alive
        yield


def _cols(g, has_vmix):
    fr = g * 512 + np.arange(512)
    fo = (1 - g) * 512 + np.arange(512)
    sh = [fr, 1024 + fr, 2048 + fr, 3072 + fr, 4096 + np.arange(128)]
    if has_vmix:
        sh.append(2048 + fo)
    shift_cols = np.concatenate(sh)
    base = 4224
    hg = np.concatenate([base + fr, base + 1024 + fr, base + 3072 + fr])
    icols = base + 2048 + fr
    return fr, fo, shift_cols, np.concatenate([shift_cols, hg]), icols


def pack_A(inp, l, g):
    has_vmix = l > 0
    fr, fo, shift_cols, fm_cols, icols = _cols(g, has_vmix)
    NS = len(shift_cols) // 128
    NFM = len(fm_cols) // 128
    w = np.asarray(inp["w_in"][l], np.float32)
    wfm = np.ascontiguousarray(w[:, fm_cols].reshape(KC, 128, NFM, 128).transpose(2, 1, 0, 3)).reshape(NFM, 128, KC * 128)
    wi = np.ascontiguousarray(w[:, icols].reshape(KC, 128, 512).transpose(1, 0, 2)).reshape(128, KC * 512)

    def c4(v):
        return np.asarray(v, np.float32).reshape(4, 128).T

    mu = np.asarray(inp["shift_mu"][l], np.float32)[shift_cols].reshape(NS, 128).T
    vm0 = inp["v_mix0"][0][fr] if has_vmix else np.zeros(512, np.float32)
    pp = np.concatenate([mu, c4(inp["w_decay0"][l][fr]), c4(inp["a0"][l][fr]), c4(inp["k_k"][l][fr]), c4(inp["k_a"][l][fr]),
                         c4(inp["r_k"][l][fr]), c4(inp["ln_x_w"][l][fr]), c4(inp["ln_x_b"][l][fr]), c4(vm0),
                         c4(inp["lb_logits"][0][fr]), c4(inp["lb_logits"][1][fr]), c4(inp["g_norm_w"][l][fr])], axis=1)
    lrw = np.zeros((128, 512), np.float32)
    lrw[0:64] = inp["w_decay_up"][l][:, fr]
    lra = np.zeros((128, 512), np.float32)
    lra[64:128] = inp["a_up"][l][:, fr]
    d = {"wfm": wfm, "wi": wi, "pp": np.ascontiguousarray(pp, dtype=np.float32), "lrw": lrw, "lra": lra}
    if has_vmix:
        vd = np.asarray(inp["v_mix_down"][0], np.float32)
        rows = np.concatenate([fr, fo])
        d["vdn"] = np.ascontiguousarray(vd[rows].reshape(8, 128, 32).transpose(1, 0, 2)).reshape(128, 256)
        d["vup"] = np.ascontiguousarray(np.asarray(inp["v_mix_up"][0], np.float32)[:, fr])
    return d


def decl_A(nc, tag, has_vmix, kind="ExternalInput"):
    NS = 21 if has_vmix else 17
    NFM = NS + 12
    W = {
        "wfm": nc.dram_tensor("wfm" + tag, [NFM, 128, KC * 128], F32, kind=kind).ap(),
        "wi": nc.dram_tensor("wi" + tag, [128, KC * 512], F32, kind=kind).ap(),
        "pp": nc.dram_tensor("pp" + tag, [128, NS + 44], F32, kind=kind).ap(),
        "lrw": nc.dram_tensor("lrw" + tag, [128, 512], F32, kind=kind).ap(),
        "lra": nc.dram_tensor("lra" + tag, [128, 512], F32, kind=kind).ap(),
        "wscr": nc.dram_tensor("wscr" + tag, [NFM, 128, KC * 128], BF16, kind="Internal").ap(),
    }
    if has_vmix:
        W["vdn"] = nc.dram_tensor("vdn" + tag, [128, 256], F32, kind=kind).ap()
        W["vup"] = nc.dram_tensor("vup" + tag, [32, 512], F32, kind=kind).ap()
    return W


def emit_B(C, oT_full, hin, tok0, ntok, W, hout, is_output):
    mm, tr, act, tt, ts, stt, cp, ms = _ops(C)
    S = C.S
    K = C.K
    A = C.arena
    A.off = C.persist_off
    S.barrier()
    PS, BPS = C.ps, C.bps
    kb = K.b
    stg = [A.alloc([128, 2048]) for _ in range(2)]
    bstg = [Buf("bstg%d" % i) for i in range(2)]
    WOB = A.alloc([128, KC * 2048 // 2]).bitcast(BF16).rearrange("p (k d) -> p k d", k=KC)
    bWOB = Buf("WOB")
    LNW = A.alloc([128, 2048]); LNB = A.alloc([128, 2048]); bLN = Buf("LN")
    OTt = [A.alloc([128, KC * 128 // 2]).bitcast(BF16).rearrange("p (c t) -> p c t", c=KC) for _ in range(2)]
    bOTt = [Buf("OTt%d" % i) for i in range(2)]
    Ht = [A.alloc([128, 2048]) for _ in range(2)]
    bHt = [Buf("Ht%d" % i) for i in range(2)]
    Zs = [A.alloc([128, 2048]) for _ in range(2)]; bZs = [Buf("Z0"), Buf("Z1")]
    JK = A.alloc([128, 2048]); bJK = Buf("JK")
    OUs = [A.alloc([128, 2048]) for _ in range(2)]; bOUs = [Buf("OU0"), Buf("OU1")]
    STs = [A.alloc([128, 8]) for _ in range(2)]; bSTs = [Buf("ST0"), Buf("ST1")]
    S.dma("sp", lambda h: h.dma_start(out=LNW, in_=W["lnw"]), (), [bLN])
    S.dma("sp", lambda h: h.dma_start(out=LNB, in_=W["lnb"]), (), [bLN])
    for kc in range(KC):
        sl = kc % 2
        S.dma("sp", lambda h, sl=sl, kc=kc: h.dma_start(out=stg[sl], in_=W["wout"][:, kc * 2048:(kc + 1) * 2048]), (), [bstg[sl]])
        cp("dve" if kc % 2 == 0 else "act", WOB[:, kc, :], stg[sl], [bstg[sl]], [bWOB])
    for i in range(ntok // 128):
        tk = tok0 + i * 128
        sl = i % 2
        Z, bZ, OU, bOU, ST, bST = Zs[sl], bZs[sl], OUs[sl], bOUs[sl], STs[sl], bSTs[sl]
        S.dma("sp", lambda h, sl=sl, tk=tk: h.dma_start(out=OTt[sl], in_=oT_full[:, :, tk:tk + 128].rearrange("c p t -> p c t")),
              W.get("oT_deps", ()), [bOTt[sl]])
        S.dma("sp", lambda h, sl=sl, tk=tk: h.dma_start(out=Ht[sl], in_=hin[tk:tk + 128, :]), W.get("h_deps", ()), [bHt[sl]])
        for n in range(4):
            bk = (4 * (i % 2)) + n
            for kc in range(KC):
                mm(PS[bk][:, :], OTt[sl][:, kc, :], WOB[:, kc, n * 512:(n + 1) * 512], kc == 0, kc == KC - 1, [bOTt[sl], bWOB], [BPS[bk]])
            stt("dve", Z[:, n * 512:(n + 1) * 512], Ht[sl][:, n * 512:(n + 1) * 512], ALPHA, PS[bk][:, :], ALU.mult, ALU.add,
                [bHt[sl], BPS[bk]], [bZ])
        S.op("act", lambda h, Z=Z, ST=ST: h.activation(out=JK, in_=Z, func=AF.Identity, accum_out=ST[:, 0:1]), [bZ], [bJK, bST])
        S.op("act", lambda h, Z=Z, ST=ST: h.activation(out=JK, in_=Z, func=AF.Square, accum_out=ST[:, 1:2]), [bZ], [bJK, bST])
        q = [bST]
        ts("dve", ST[:, 2:3], ST[:, 0:1], 1.0 / D, ALU.mult, q, q)
        tt("dve", ST[:, 3:4], ST[:, 2:3], ST[:, 2:3], ALU.mult, q, q)
        stt("dve", ST[:, 4:5], ST[:, 1:2], 1.0 / D, ST[:, 3:4], ALU.mult, ALU.subtract, q, q)
        act(ST[:, 5:6], ST[:, 4:5], AF.Ln, q + [kb], q, bias=K.eps_ln[:, 0:1])
        act(ST[:, 5:6], ST[:, 5:6], AF.Exp, q, q, scale=-0.5)
        S.op("dve", lambda h, Z=Z, ST=ST, OU=OU: h.tensor_scalar(OU, Z, ST[:, 2:3], ST[:, 5:6], ALU.subtract, ALU.mult), [bZ, bST], [bOU])
        tt("pool", OU, OU, LNW, ALU.mult, [bOU, bLN], [bOU])
        tt("dve", OU, OU, LNB, ALU.add, [bOU, bLN], [bOU])
        S.dma("pool", lambda h, i=i, OU=OU: h.dma_start(out=hout[i * 128:(i + 1) * 128, :], in_=OU), [bOU], W.get("out_bufs", []),
              is_output=is_output)


def pack_B(inp, l):
    rows = np.concatenate([np.arange(0, 512), 1024 + np.arange(0, 512), 512 + np.arange(0, 512), 1536 + np.arange(0, 512)])
    w = np.asarray(inp["w_out"][l], np.float32)[rows]
    wout = np.ascontiguousarray(w.reshape(KC, 128, D).transpose(1, 0, 2)).reshape(128, KC * D)
    lnw = np.ascontiguousarray(np.broadcast_to(np.asarray(inp["ln_w"][l], np.float32)[None, :], (128, D)))
    lnb = np.ascontiguousarray(np.broadcast_to(np.asarray(inp["ln_b"][l], np.float32)[None, :], (128, D)))
    return {"wout": wout, "lnw": lnw, "lnb": lnb}


def decl_B(nc, tag, kind="ExternalInput"):
    return {"wout": nc.dram_tensor("wout" + tag, [128, KC * D], F32, kind=kind).ap(),
            "lnw": nc.dram_tensor("lnw" + tag, [128, D], F32, kind=kind).ap(),
            "lnb": nc.dram_tensor("lnb" + tag, [128, D], F32, kind=kind).ap()}


def _build_A(has_vmix, li, T):
    nc = bass.Bass("TRN2", target_bir_lowering=False)
    with contextlib.ExitStack() as st:
        C = mk_ctx(nc, st)
        hin = nc.dram_tensor("hin", [T, D], F32, kind="ExternalInput").ap()
        W = decl_A(nc, "", has_vmix)
        oT = nc.dram_tensor("oT", [8, 128, T], BF16, kind="ExternalOutput").ap()
        vf = nc.dram_tensor("vf", [4, 128, T], F32, kind="ExternalInput" if has_vmix else "ExternalOutput").ap()
        emit_consts(C)
        emit_A(C, hin, W, oT, vf, has_vmix, li, T)
        C.S.emit(st)
    return nc


def _build_B(ntok):
    nc = bass.Bass("TRN2", target_bir_lowering=False)
    with contextlib.ExitStack() as st:
        C = mk_ctx(nc, st)
        hin = nc.dram_tensor("hin", [ntok, D], F32, kind="ExternalInput").ap()
        oT = nc.dram_tensor("oT", [16, 128, ntok], BF16, kind="ExternalInput").ap()
        W = decl_B(nc, "")
        hout = nc.dram_tensor("hout", [ntok, D], F32, kind="ExternalOutput").ap()
        emit_consts(C)
        emit_B(C, oT, hin, 0, ntok, W, hout, True)
        C.S.emit(st)
    return nc


def kernel_unfused(inp):
    x = np.asarray(inp["x"], np.float32)
    Bn, T, _ = x.shape
    cores = list(range(8))
    half = T // 2
    h = x
    vfs = None
    for l in range(DEPTH):
        ncA = _build_A(l > 0, l, T)
        packs = [pack_A(inp, l, g) for g in range(2)]
        maps = []
        for cid in cores:
            b, g = cid // 2, cid % 2
            m = dict(packs[g])
            m["hin"] = np.ascontiguousarray(h[b])
            if l > 0:
                m["vf"] = vfs[cid]
            maps.append(m)
        res = run_bass_kernel_spmd(ncA, maps, core_ids=cores).results
        if l == 0:
            vfs = [np.asarray(r["vf"]) for r in res]
        oTs = [np.asarray(r["oT"]) for r in res]
        ncB = _build_B(half)
        pb = pack_B(inp, l)
        maps = []
        for cid in cores:
            b, s = cid // 2, cid % 2
            m = dict(pb)
            full = np.concatenate([oTs[2 * b], oTs[2 * b + 1]], axis=0)
            m["oT"] = np.ascontiguousarray(full[:, :, s * half:(s + 1) * half])
            m["hin"] = np.ascontiguousarray(h[b, s * half:(s + 1) * half])
            maps.append(m)
        res = run_bass_kernel_spmd(ncB, maps, core_ids=cores).results
        hn = np.empty_like(x)
        for cid in cores:
            b, s = cid // 2, cid % 2
            hn[b, s * half:(s + 1) * half] = np.asarray(res[cid]["hout"])
        h = hn
    return h.astype(np.float32)


def _build_fused(T):
    nc = bass.Bass("TRN2", target_bir_lowering=False)
    with contextlib.ExitStack() as st:
        C = mk_ctx(nc, st)
        x = nc.dram_tensor("x", [T, D], F32, kind="ExternalInput").ap()
        WA = {(l, g): decl_A(nc, "_%d_%d" % (l, g), l > 0) for l in range(DEPTH) for g in range(2)}
        WB = {l: decl_B(nc, "_%d" % l) for l in range(DEPTH)}
        oT = nc.dram_tensor("oT_scr", [16, 128, T], BF16, kind="Internal").ap()
        vf = nc.dram_tensor("vf_scr", [2, 4, 128, T], F32, kind="Internal").ap()
        h1 = nc.dram_tensor("h1_scr", [T, D], F32, kind="Internal").ap()
        out = nc.dram_tensor("out", [T, D], F32, kind="ExternalOutput").ap()
        emit_consts(C)
        hin = x
        for l in range(DEPTH):
            for g in range(2):
                emit_A(C, hin, WA[(l, g)], oT[8 * g:8 * g + 8], vf[g], l > 0, l, T)
            last = l == DEPTH - 1
            emit_B(C, oT, hin, 0, T, WB[l], out if last else h1, last)
            hin = h1
        C.S.emit(st)
    return nc


def kernel_fused(inp):
    x = np.asarray(inp["x"], np.float32)
    Bn, T, _ = x.shape
    cores = list(range(8))
    nc = _build_fused(T)
    shared = {}
    for l in range(DEPTH):
        for g in range(2):
            for k, v in pack_A(inp, l, g).items():
                shared["%s_%d_%d" % (k, l, g)] = v
        for k, v in pack_B(inp, l).items():
            shared["%s_%d" % (k, l)] = v
    maps = []
    for cid in cores:
        m = dict(shared)
        m["x"] = np.ascontiguousarray(x[cid // 2])
        maps.append(m)
    res = run_bass_kernel_spmd(nc, maps, core_ids=cores).results
    half = T // 2
    out = np.empty_like(x)
    for cid in cores:
        b, s = cid // 2, cid % 2
        out[b, s * half:(s + 1) * half] = np.asarray(res[cid]["out"])[s * half:(s + 1) * half]
    return out.astype(np.float32)


def kernel(**inputs):
    return kernel_fused(inputs)
```
